# Optimizing a Trainium2 kernel written in Bass

```python
import math
import jax, jax.numpy as jnp
from jax import lax
import numpy as np

D_MODEL = 2048
BATCH = 1
SEQ = 8192
DEPTH = 2

D_FF = 4 * D_MODEL
MIX_WIDTH = D_MODEL
GROUP_WIDTH = MIX_WIDTH // 2
CHUNK = 64
Q_BLOCK = 128
ROPE_THETA = 10000.0
EPS = 1e-6

HG_HEADS = 8
HG_DK = 128
HG_DV = GROUP_WIDTH // HG_HEADS
MLA_HEADS = 8
MLA_NOPE = 128
MLA_ROPE = 64
MLA_V = GROUP_WIDTH // MLA_HEADS
MLA_Q_RANK = 512
MLA_KV_RANK = 256
RET_HEADS = 4
RET_DK = 128
RET_DV = GROUP_WIDTH // RET_HEADS
GDN_HEADS = 8
GDN_DK = 128
GDN_DV = GROUP_WIDTH // GDN_HEADS
CONV_WIDTH = 5
GDN_QKV = GDN_HEADS * (2 * GDN_DK + GDN_DV)

N_EVEN = (DEPTH + 1) // 2
N_ODD = DEPTH // 2

EVEN_COLS = (HG_HEADS * HG_DK, HG_HEADS * HG_DK, HG_HEADS * HG_DK, HG_HEADS * HG_DV,
             HG_HEADS * HG_DV, MLA_Q_RANK, MLA_KV_RANK + MLA_ROPE)
ODD_COLS = (RET_HEADS * RET_DK, RET_HEADS * RET_DK, RET_HEADS * RET_DV, RET_HEADS * RET_DV,
            GDN_QKV, GDN_HEADS, GDN_HEADS, GDN_HEADS, GDN_HEADS, GDN_HEADS * GDN_DV)

kernel_name = 'hybrid_hgrn2_mla_retention_gdn_encoder'


def split_cols(h, sizes):
    offsets = []
    total = 0
    for s in sizes[:-1]:
        total += s
        offsets.append(total)
    return jnp.split(h, offsets, axis=-1)


def rmsnorm(x, w):
    xf = x.astype(jnp.float32)
    y = xf * lax.rsqrt(jnp.mean(xf * xf, axis=-1, keepdims=True) + EPS)
    return (y * w.astype(jnp.float32)).astype(x.dtype)


def head_rmsnorm(o, w):
    B, S, H, d = o.shape
    y = o * lax.rsqrt(jnp.mean(o * o, axis=-1, keepdims=True) + EPS)
    return y.reshape(B, S, H * d) * w.astype(jnp.float32)


def rope(x, positions):
    half = x.shape[-1] // 2
    inv = ROPE_THETA ** (-jnp.arange(half, dtype=jnp.float32) / half)
    ang = positions.astype(jnp.float32)[..., None] * inv
    cos = jnp.cos(ang)[:, :, None, :]
    sin = jnp.sin(ang)[:, :, None, :]
    x1 = x[..., :half].astype(jnp.float32)
    x2 = x[..., half:].astype(jnp.float32)
    return jnp.concatenate([x1 * cos - x2 * sin, x1 * sin + x2 * cos], axis=-1).astype(x.dtype)


def to_bhtd(a):
    return a.astype(jnp.float32).transpose(0, 2, 1, 3)


def flip_t(a):
    return jnp.flip(a, axis=2)


def chunk_gla(q, k, v, g):
    B, H, T, dk = q.shape
    dv = v.shape[-1]
    n = T // CHUNK
    q, k, v, g = (a.reshape(B, H, n, CHUNK, a.shape[-1]) for a in (q, k, v, g))
    b = jnp.cumsum(g, axis=-2)
    b_last = b[..., -1:, :]
    b_mid = b[..., CHUNK // 2:CHUNK // 2 + 1, :]
    causal = jnp.tril(jnp.ones((CHUNK, CHUNK), dtype=bool))
    scores = jnp.einsum('bhntd,bhnsd->bhnts', q * jnp.exp(b - b_mid), k * jnp.exp(b_mid - b))
    o_intra = jnp.einsum('bhnts,bhnsv->bhntv', jnp.where(causal, scores, 0.0), v)
    q_dec = q * jnp.exp(b)
    k_dec = k * jnp.exp(b_last - b)
    chunk_decay = jnp.exp(b_last[..., 0, :])

    def step(state, xs):
        qd, kd, vc, dec = xs
        o = jnp.einsum('bhcd,bhdv->bhcv', qd, state)
        state = state * dec[..., None] + jnp.einsum('bhcd,bhcv->bhdv', kd, vc)
        return state, o

    xs = tuple(jnp.moveaxis(a, 2, 0) for a in (q_dec, k_dec, v, chunk_decay))
    _, o_inter = lax.scan(step, jnp.zeros((B, H, dk, dv), jnp.float32), xs)
    o = jnp.moveaxis(o_inter, 0, 2) + o_intra
    return o.reshape(B, H, T, dv)


def chunk_gated_delta(q, k, v, log_alpha, beta):
    B, H, T, dk = q.shape
    dv = v.shape[-1]
    n = T // CHUNK
    q, k, v = (a.reshape(B, H, n, CHUNK, a.shape[-1]) for a in (q, k, v))
    log_alpha = log_alpha.reshape(B, H, n, CHUNK)
    beta = beta.reshape(B, H, n, CHUNK)
    g = jnp.cumsum(log_alpha, axis=-1)
    incl = jnp.tril(jnp.ones((CHUNK, CHUNK), dtype=bool))
    strict = jnp.tril(jnp.ones((CHUNK, CHUNK), dtype=bool), -1)
    decay = jnp.exp(jnp.where(incl, g[..., :, None] - g[..., None, :], -jnp.inf))
    k_beta = k * beta[..., None]
    a_mat = jnp.where(strict, jnp.einsum('bhntd,bhnsd->bhnts', k_beta, k) * decay, 0.0)
    eye = jnp.eye(CHUNK, dtype=q.dtype)
    rhs = jnp.concatenate([v * beta[..., None], k_beta * jnp.exp(g)[..., None]], axis=-1)
    sol = lax.linalg.triangular_solve(a_mat + eye, rhs, left_side=True, lower=True,
                                      unit_diagonal=True)
    u, w = sol[..., :dv], sol[..., dv:]
    qk = jnp.einsum('bhntd,bhnsd->bhnts', q, k) * decay
    g_last = g[..., -1:]
    q_dec = q * jnp.exp(g)[..., None]
    k_dec = k * jnp.exp(g_last - g)[..., None]
    chunk_decay = jnp.exp(g_last[..., 0])

    def step(state, xs):
        qd, kd, qk_c, u_c, w_c, dec = xs
        v_new = u_c - jnp.einsum('bhcd,bhdv->bhcv', w_c, state)
        o = jnp.einsum('bhcd,bhdv->bhcv', qd, state) + jnp.einsum('bhts,bhsv->bhtv', qk_c, v_new)
        state = state * dec[..., None, None] + jnp.einsum('bhcd,bhcv->bhdv', kd, v_new)
        return state, o

    xs = tuple(jnp.moveaxis(a, 2, 0) for a in (q_dec, k_dec, qk, u, w, chunk_decay))
    _, o = lax.scan(step, jnp.zeros((B, H, dk, dv), jnp.float32), xs)
    return jnp.moveaxis(o, 0, 2).reshape(B, H, T, dv)


def hgrn2_mixer(hq, hf_fwd, hf_bwd, hi, hgate, lb, norm_w):
    B, S, _ = hq.shape
    heads = lambda a, d: a.reshape(B, S, HG_HEADS, d).transpose(0, 2, 1, 3)
    q = heads(jax.nn.silu(hq.astype(jnp.float32)), HG_DK) * HG_DK ** -0.5
    v = heads(hi.astype(jnp.float32), HG_DV)

    def direction(hf):
        z = hf.astype(jnp.float32)
        f = lb + (1.0 - lb) * jax.nn.sigmoid(z)
        one_minus_f = (1.0 - lb) * jax.nn.sigmoid(-z)
        return heads(one_minus_f, HG_DK), heads(jnp.log(f), HG_DK)

    k_f, g_f = direction(hf_fwd)
    k_b, g_b = direction(hf_bwd)
    o = chunk_gla(q, k_f, v, g_f) + flip_t(chunk_gla(flip_t(q), flip_t(k_b), flip_t(v), flip_t(g_b)))
    o = head_rmsnorm(jnp.swapaxes(o, 1, 2), norm_w) * jax.nn.silu(hgate.astype(jnp.float32))
    return o.astype(hq.dtype)


def mla_mixer(c_q, kv_a, positions, q_norm_w, w_q_b, kv_norm_w, w_kv_b):
    B, S, _ = c_q.shape
    q = (rmsnorm(c_q, q_norm_w) @ w_q_b).reshape(B, S, MLA_HEADS, MLA_NOPE + MLA_ROPE)
    q = jnp.concatenate([q[..., :MLA_NOPE], rope(q[..., MLA_NOPE:], positions)], axis=-1)
    c_kv, k_pe = kv_a[..., :MLA_KV_RANK], kv_a[..., MLA_KV_RANK:]
    kv = (rmsnorm(c_kv, kv_norm_w) @ w_kv_b).reshape(B, S, MLA_HEADS, MLA_NOPE + MLA_V)
    k_nope, v = kv[..., :MLA_NOPE], kv[..., MLA_NOPE:]
    k_pe = rope(k_pe[:, :, None, :], positions)
    k = jnp.concatenate([k_nope, jnp.broadcast_to(k_pe, (B, S, MLA_HEADS, MLA_ROPE))], axis=-1)
    scale = (MLA_NOPE + MLA_ROPE) ** -0.5
    q_blocks = q.reshape(B, S // Q_BLOCK, Q_BLOCK, MLA_HEADS, MLA_NOPE + MLA_ROPE).transpose(1, 0, 2, 3, 4)

    def attend(qb):
        s = jnp.einsum('bqhd,bkhd->bhqk', qb, k, preferred_element_type=jnp.float32) * scale
        p = jax.nn.softmax(s, axis=-1)
        return jnp.einsum('bhqk,bkhd->bqhd', p.astype(v.dtype), v)

    o = lax.map(attend, q_blocks)
    return o.transpose(1, 0, 2, 3, 4).reshape(B, S, MLA_HEADS * MLA_V)


def retention_mixer(hq, hk, hv, hgate, positions, norm_w):
    B, S, _ = hq.shape
    q = to_bhtd(rope(hq.reshape(B, S, RET_HEADS, RET_DK), positions)) * RET_DK ** -0.5
    k = to_bhtd(rope(hk.reshape(B, S, RET_HEADS, RET_DK), positions))
    v = to_bhtd(hv.reshape(B, S, RET_HEADS, RET_DV))
    log_gamma = jnp.log(1.0 - jnp.exp2(-5.0 - jnp.arange(RET_HEADS, dtype=jnp.float32)))
    g_fwd = jnp.broadcast_to(log_gamma[None, :, None, None], q.shape)
    g_bwd = jnp.broadcast_to(log_gamma[::-1][None, :, None, None], q.shape)
    o = chunk_gla(q, k, v, g_fwd) + flip_t(chunk_gla(flip_t(q), flip_t(k), flip_t(v), g_bwd))
    o = head_rmsnorm(jnp.swapaxes(o, 1, 2), norm_w) * jax.nn.silu(hgate.astype(jnp.float32))
    return o.astype(hq.dtype)


def gdn_mixer(hqkv, ha_f, ha_b, hb_f, hb_b, hgate, conv_w, a_log, dt_bias, norm_w):
    B, S, C = hqkv.shape
    qkv = jax.nn.silu(lax.conv_general_dilated(
        hqkv, conv_w[:, None, :], window_strides=(1,),
        padding=[(CONV_WIDTH // 2, CONV_WIDTH // 2)],
        dimension_numbers=('NWC', 'WIO', 'NWC'), feature_group_count=C))
    q, k, v = jnp.split(qkv, [GDN_HEADS * GDN_DK, 2 * GDN_HEADS * GDN_DK], axis=-1)
    heads = lambda a, d: to_bhtd(a.reshape(B, S, GDN_HEADS, d))
    l2 = lambda a: a * lax.rsqrt(jnp.sum(a * a, axis=-1, keepdims=True) + EPS)
    q = l2(heads(q, GDN_DK)) * GDN_DK ** -0.5
    k = l2(heads(k, GDN_DK))
    v = heads(v, GDN_DV)

    def gates(ha, hb, d):
        la = -jnp.exp(a_log[d].astype(jnp.float32)) * jax.nn.softplus(
            ha.astype(jnp.float32) + dt_bias[d].astype(jnp.float32))
        beta = jax.nn.sigmoid(hb.astype(jnp.float32))
        return la.transpose(0, 2, 1), beta.transpose(0, 2, 1)

    la_f, beta_f = gates(ha_f, hb_f, 0)
    la_b, beta_b = gates(ha_b, hb_b, 1)
    o = chunk_gated_delta(q, k, v, la_f, beta_f) + flip_t(
        chunk_gated_delta(flip_t(q), flip_t(k), flip_t(v), flip_t(la_b), flip_t(beta_b)))
    o = head_rmsnorm(jnp.swapaxes(o, 1, 2), norm_w) * jax.nn.silu(hgate.astype(jnp.float32))
    return o.astype(hqkv.dtype)


def setup_inputs(seed: int = 0) -> dict:
    key = jax.random.key(seed)
    ks = jax.random.split(key, 24)
    f32 = jnp.float32
    nrm = lambda k, shape, fan_in: jax.random.normal(k, shape, f32) * fan_in ** -0.5
    gain = lambda k, shape: 1.0 + 0.02 * jax.random.normal(k, shape, f32)
    x = jax.random.normal(ks[0], (BATCH, SEQ, D_MODEL), f32)
    offset = jax.random.randint(ks[1], (BATCH, 1), 0, SEQ, dtype=jnp.int32)
    positions = jnp.arange(SEQ, dtype=jnp.int32)[None, :] + offset
    dt = jnp.exp(jax.random.uniform(ks[2], (N_ODD, 2, GDN_HEADS), f32, math.log(1e-3), math.log(1e-1)))
    return {
        'x': x,
        'positions': positions,
        'norm_mix_w': gain(ks[3], (DEPTH, D_MODEL)),
        'norm_ffn_w': gain(ks[4], (DEPTH, D_MODEL)),
        'final_norm_w': gain(ks[5], (D_MODEL,)),
        'hg_lb_logits': 0.1 * jax.random.normal(ks[6], (DEPTH + 1, HG_HEADS * HG_DK), f32),
        'even_w_in': nrm(ks[7], (N_EVEN, D_MODEL, sum(EVEN_COLS)), D_MODEL),
        'hg_norm_w': gain(ks[8], (N_EVEN, HG_HEADS * HG_DV)),
        'mla_q_norm_w': gain(ks[9], (N_EVEN, MLA_Q_RANK)),
        'mla_w_q_b': nrm(ks[10], (N_EVEN, MLA_Q_RANK, MLA_HEADS * (MLA_NOPE + MLA_ROPE)), MLA_Q_RANK),
        'mla_kv_norm_w': gain(ks[11], (N_EVEN, MLA_KV_RANK)),
        'mla_w_kv_b': nrm(ks[12], (N_EVEN, MLA_KV_RANK, MLA_HEADS * (MLA_NOPE + MLA_V)), MLA_KV_RANK),
        'even_w_out': nrm(ks[13], (N_EVEN, MIX_WIDTH, D_MODEL), MIX_WIDTH),
        'odd_w_in': nrm(ks[14], (N_ODD, D_MODEL, sum(ODD_COLS)), D_MODEL),
        'ret_norm_w': gain(ks[15], (N_ODD, RET_HEADS * RET_DV)),
        'gdn_conv_w': nrm(ks[16], (N_ODD, CONV_WIDTH, GDN_QKV), CONV_WIDTH),
        'gdn_a_log': jnp.log(jax.random.uniform(ks[17], (N_ODD, 2, GDN_HEADS), f32, 1.0, 16.0)),
        'gdn_dt_bias': dt + jnp.log(-jnp.expm1(-dt)),
        'gdn_norm_w': gain(ks[18], (N_ODD, GDN_HEADS * GDN_DV)),
        'odd_w_out': nrm(ks[19], (N_ODD, MIX_WIDTH, D_MODEL), MIX_WIDTH),
        'ffn_w_up': nrm(ks[20], (DEPTH, D_MODEL, D_FF), D_MODEL),
        'ffn_w_down': nrm(ks[21], (DEPTH, D_FF, D_MODEL), D_FF),
    }


def reference(x, positions, norm_mix_w, norm_ffn_w, final_norm_w, hg_lb_logits, even_w_in,
              hg_norm_w, mla_q_norm_w, mla_w_q_b, mla_kv_norm_w, mla_w_kv_b, even_w_out,
              odd_w_in, ret_norm_w, gdn_conv_w, gdn_a_log, gdn_dt_bias, gdn_norm_w, odd_w_out,
              ffn_w_up, ffn_w_down):
    lb_table = jnp.cumsum(jax.nn.softmax(hg_lb_logits.astype(jnp.float32), axis=0), axis=0)
    for layer in range(DEPTH):
        j = layer // 2
        h = rmsnorm(x, norm_mix_w[layer])
        if layer % 2 == 0:
            hq, hf_f, hf_b, hi, hg, c_q, kv_a = split_cols(h @ even_w_in[j], EVEN_COLS)
            o_a = hgrn2_mixer(hq, hf_f, hf_b, hi, hg, lb_table[layer], hg_norm_w[j])
            o_b = mla_mixer(c_q, kv_a, positions, mla_q_norm_w[j], mla_w_q_b[j],
                            mla_kv_norm_w[j], mla_w_kv_b[j])
            mix = jnp.concatenate([o_a, o_b], axis=-1) @ even_w_out[j]
        else:
            rq, rk, rv, rg, gqkv, ga_f, ga_b, gb_f, gb_b, gg = split_cols(h @ odd_w_in[j], ODD_COLS)
            o_c = retention_mixer(rq, rk, rv, rg, positions, ret_norm_w[j])
            o_d = gdn_mixer(gqkv, ga_f, ga_b, gb_f, gb_b, gg, gdn_conv_w[j], gdn_a_log[j],
                            gdn_dt_bias[j], gdn_norm_w[j])
            mix = jnp.concatenate([o_c, o_d], axis=-1) @ odd_w_out[j]
        x = x + mix
        h = rmsnorm(x, norm_ffn_w[layer])
        x = x + jnp.square(jax.nn.relu(h @ ffn_w_up[layer])) @ ffn_w_down[layer]
    return rmsnorm(x, final_norm_w)
```

```python
import math
import numpy as np
from contextlib import ExitStack
import concourse.bass as bass
import concourse.mybir as mybir
from concourse.bass_utils import run_bass_kernel_spmd

F32 = mybir.dt.float32
BF16 = mybir.dt.bfloat16
I32 = mybir.dt.int32
AF = mybir.ActivationFunctionType
ALU = mybir.AluOpType
AX = mybir.AxisListType

_DT_SIZE = {F32: 4, BF16: 2, I32: 4}


def _region(ap):
    t = ap.tensor
    name = t.name
    esz = _DT_SIZE.get(ap.dtype, 4)
    dims = list(ap.ap)
    cls = type(t).__name__
    if cls.startswith("DRam"):
        ext = 0
        for step, cnt in dims:
            ext += (cnt - 1) * abs(step)
        return (name, 0, 1, ap.offset * esz, (ap.offset + ext + 1) * esz)
    shape = list(t.shape)
    row = 1
    for s in shape[1:]:
        row *= s
    tesz = _DT_SIZE.get(t.dtype, 4)
    rowb = row * tesz
    offb = ap.offset * esz
    p0 = offb // rowb
    f0 = offb % rowb
    pstep, pcnt = dims[0]
    if pstep * esz != rowb:
        pcnt = 1
        fd = dims
    else:
        fd = dims[1:]
    ext = 0
    for step, cnt in fd:
        ext += (cnt - 1) * abs(step)
    if cls.startswith("PSum"):
        return (name, 0, 128, 0, rowb)
    return (name, p0, p0 + pcnt, f0, f0 + (ext + 1) * esz)


def _overlap(a, b):
    return a[1] < b[2] and b[1] < a[2] and a[3] < b[4] and b[3] < a[4]


def _covers(a, b):
    return a[1] <= b[1] and a[2] >= b[2] and a[3] <= b[3] and a[4] >= b[4]


class Op:
    __slots__ = ("eng", "emit", "idx", "deps", "dma_key", "inc", "count", "pe_acc")

    def __init__(self, eng, emit, idx, dma_key=None):
        self.eng = eng
        self.emit = emit
        self.idx = idx
        self.deps = {}
        self.dma_key = dma_key
        self.inc = False
        self.count = 0


ENGS = ("pe", "act", "dve", "pool", "sp")


class Sched:
    def __init__(self, nc):
        self.nc = nc
        self.ops = []
        self.per_eng = {e: [] for e in ENGS}
        self.recs = {}
        self.dma_total = {}
        self.final_waits = []

    def add(self, eng, emit, reads=(), writes=(), dma_key=None):
        op = Op(eng, emit, len(self.ops), dma_key)
        self.ops.append(op)
        self.per_eng[eng].append(op)
        pend = []
        for ap in reads:
            r = _region(ap)
            lst = self.recs.setdefault(r[0], [])
            for rec in lst:
                if rec[2] and _overlap(rec[0], r):
                    self._dep(op, rec[1], raw=True)
            pend.append((lst, r, False))
        for ap in writes:
            r = _region(ap)
            lst = self.recs.setdefault(r[0], [])
            keep = []
            for rec in lst:
                if _overlap(rec[0], r):
                    self._dep(op, rec[1], raw=False)
                    if _covers(r, rec[0]):
                        continue
                keep.append(rec)
            lst[:] = keep
            pend.append((lst, r, True))
        for lst, r, w in pend:
            if not w:
                if op.dma_key is None:
                    lst[:] = [rec for rec in lst if not (not rec[2] and rec[0] == r and rec[1].eng == op.eng
                                                        and rec[1].dma_key is None)]
            lst.append([r, op, w])
        if dma_key is not None:
            self.dma_total[dma_key] = self.dma_total.get(dma_key, 0) + 16
            op.count = self.dma_total[dma_key]
        return op

    def _dep(self, op, prod, raw):
        if prod is op:
            return
        if prod.dma_key is not None:
            key = ("dma", prod.dma_key)
            val = self.dma_total[prod.dma_key]
            op.deps[key] = max(op.deps.get(key, 0), val)
            return
        if prod.eng == op.eng and op.dma_key is None:
            if op.eng == "pe":
                return
        key = ("eng", prod.eng)
        prod.inc = True
        op.deps[key] = max(op.deps.get(key, -1), prod.idx)

    def dma(self, eng, out, in_, key, **kw):
        def emit(e, out=out, in_=in_, kw=kw):
            return e.dma_start(out=out, in_=in_, **kw)
        return self.add(eng, emit, reads=[in_], writes=[out], dma_key=key)

    def finish_wait(self, eng, keys):
        self.final_waits.append((eng, keys))

    def build(self):
        nc = self.nc
        cnt = {e: 0 for e in ENGS}
        for op in self.ops:
            if op.dma_key is None and op.inc:
                cnt[op.eng] += 1
                op.count = cnt[op.eng]
        with ExitStack() as st:
            esem = {e: st.enter_context(nc.semaphore("s_" + e)) for e in ENGS}
            dsem = {k: st.enter_context(nc.semaphore("d_%s" % str(k))) for k in self.dma_total}
            block = st.enter_context(nc.Block())
            ops = self.ops

            def run(engname, eng):
                waited = {}
                for op in self.per_eng[engname]:
                    for key, v in op.deps.items():
                        if key[0] == "dma":
                            sem = dsem[key[1]]
                            val = v
                        else:
                            sem = esem[key[1]]
                            val = ops[v].count
                        if waited.get(key, 0) >= val:
                            continue
                        waited[key] = val
                        eng.wait_ge(sem, val)
                    ins = op.emit(eng)
                    if op.dma_key is not None:
                        ins.then_inc(dsem[op.dma_key], 16)
                    elif op.inc:
                        ins.then_inc(esem[engname], 1)
                for (e, keys) in self.final_waits:
                    if e == engname:
                        for k in keys:
                            eng.wait_ge(dsem[k], self.dma_total[k])

            @block.tensor
            def _(e):
                run("pe", e)

            @block.scalar
            def _(e):
                run("act", e)

            @block.vector
            def _(e):
                run("dve", e)

            @block.gpsimd
            def _(e):
                run("pool", e)

            @block.sync
            def _(e):
                run("sp", e)


T = 8192
NT = T // 128
PIECE = min(2048, T)
NP = T // PIECE
PT = PIECE // 128
EPS = 1e-6
D = 2048
TOK = 1024
HALF = 512
DFF = 8192


class Arena:
    def __init__(self, t, words):
        self.t = t
        self.words = words
        self.off = 0

    def alloc(self, shape, dt, parts=None):
        n = 1
        for s in shape[1:]:
            n *= s
        esz = 2 if dt == BF16 else 4
        w = (n * esz + 3) // 4
        assert self.off + w <= self.words, ("arena overflow", self.off, w, self.words)
        ap = self.t[0:shape[0], self.off:self.off + w]
        self.off += w
        if dt != F32:
            ap = ap.bitcast(dt)
        if len(shape) == 3:
            ap = ap.rearrange("p (a b) -> p a b", b=shape[2])
        return ap


def consts_np():
    s = np.arange(128)[:, None]
    t = np.arange(128)[None, :]
    c = {
        "U": (s <= t), "L": (s >= t), "SU": (s < t), "SL": (s > t),
    }
    return {k: v.astype(np.float32) for k, v in c.items()}


def merge_gens(gens):
    active = list(gens)
    while active:
        for g_ in list(active):
            try:
                next(g_)
            except StopIteration:
                active.remove(g_)


def gla_stream(S, pools, sid, qT, kT, k_tok, g_hl, v_tok, o_acc, rev, C, const_g=False):
    tmp = pools["tmp"][sid]
    bk = pools["banks"][sid]
    St, Sb = pools["St"][sid], pools["Sb"][sid]
    cst = pools["cst"][sid]
    Ucs = C["L"] if rev else C["U"]
    Mst = C["SU"] if rev else C["SL"]
    gB, gC, gS, gO, gU = bk[0][:, 0:128], bk[0][:, 128:256], bk[1][:, 0:128], bk[2][:, 0:128], bk[3][:, 0:128]

    def decays(tb, c):
        for hl in range(2):
            gt = g_hl[hl][:, c, :]
            S.add("pe", lambda e, gt=gt, hl=hl: e.matmul(gB, lhsT=gt, rhs=Ucs, start=(hl == 0), stop=(hl == 1)), reads=[gt, Ucs], writes=[gB])
            yield
        for hl in range(2):
            gt = g_hl[hl][:, c, :]
            S.add("pe", lambda e, gt=gt, hl=hl: e.matmul(gC, lhsT=Mst, rhs=gt, start=(hl == 0), stop=(hl == 1)), reads=[gt, Mst], writes=[gC])
            yield
        S.add("act", lambda e: e.activation(out=tb["eb"], in_=gB, func=AF.Exp), reads=[gB], writes=[tb["eb"]])
        yield
        S.add("act", lambda e: e.activation(out=tb["enb"], in_=gB, func=AF.Exp, scale=-1.0), reads=[gB], writes=[tb["enb"]])
        yield
        S.add("act", lambda e: e.activation(out=tb["ec"], in_=gC, func=AF.Exp), reads=[gC], writes=[tb["ec"]])
        yield

    def init():
        S.add("pool", lambda e: e.memset(St, 0.0), writes=[St])
        S.add("pool", lambda e: e.memset(Sb, 0.0), writes=[Sb])
        if const_g:
            for _ in decays(cst, 0):
                pass

    def prep(i, c):
        tb = tmp[i % 2]
        sl = slice(c * 128, (c + 1) * 128)
        if const_g:
            dk = cst
        else:
            dk = tb
            yield from decays(tb, c)
        S.add("pool", lambda e: e.tensor_tensor(out=tb["qd"], in0=qT[:, sl], in1=dk["eb"], op=ALU.mult), reads=[qT[:, sl], dk["eb"]], writes=[tb["qd"]])
        yield
        S.add("pool", lambda e: e.tensor_tensor(out=tb["kb"], in0=kT[:, sl], in1=dk["enb"], op=ALU.mult), reads=[kT[:, sl], dk["enb"]], writes=[tb["kb"]])
        yield
        S.add("pool", lambda e: e.tensor_tensor(out=tb["kd"], in0=k_tok[:, c, :], in1=dk["ec"], op=ALU.mult), reads=[k_tok[:, c, :], dk["ec"]], writes=[tb["kd"]])
        yield
        S.add("pe", lambda e: e.matmul(gS, lhsT=tb["kb"], rhs=tb["qd"], start=True, stop=True), reads=[tb["kb"], tb["qd"]], writes=[gS])
        yield
        S.add("dve", lambda e: e.tensor_tensor(out=tb["pm"], in0=gS, in1=Ucs, op=ALU.mult), reads=[gS, Ucs], writes=[tb["pm"]])
        yield

    def step(i, c, first):
        tb = tmp[i % 2]
        dk = cst if const_g else tb
        S.add("pe", lambda e: e.matmul(gO, lhsT=tb["pm"], rhs=v_tok[:, c, :], start=True, stop=False), reads=[tb["pm"], v_tok[:, c, :]], writes=[gO])
        yield
        S.add("pe", lambda e: e.matmul(gO, lhsT=tb["qd"], rhs=Sb, start=False, stop=True), reads=[tb["qd"], Sb], writes=[gO])
        yield
        S.add("pe", lambda e: e.matmul(gU, lhsT=tb["kd"], rhs=v_tok[:, c, :], start=True, stop=True), reads=[tb["kd"], v_tok[:, c, :]], writes=[gU])
        yield
        if first:
            S.add("act", lambda e: e.activation(out=o_acc[:, c, :], in_=gO, func=AF.Copy), reads=[gO], writes=[o_acc[:, c, :]])
        else:
            S.add("dve", lambda e: e.tensor_tensor(out=o_acc[:, c, :], in0=gO, in1=o_acc[:, c, :], op=ALU.add),
                  reads=[gO, o_acc[:, c, :]], writes=[o_acc[:, c, :]])
        yield
        dcol = dk["eb"][:, 0:1] if rev else dk["eb"][:, 127:128]
        S.add("dve", lambda e: e.scalar_tensor_tensor(out=St, in0=St, scalar=dcol, in1=gU, op0=ALU.mult, op1=ALU.add), reads=[St, dcol, gU], writes=[St])
        yield
        S.add("act", lambda e: e.activation(out=Sb, in_=St, func=AF.Copy), reads=[St], writes=[Sb])
        yield

    order = list(range(NT - 1, -1, -1)) if rev else list(range(NT))
    return {"init": init, "prep": prep, "step": step, "order": order}


def run_gla(streams, first_fn):
    for st_ in streams:
        st_["init"]()
    merge_gens([st_["prep"](0, st_["order"][0]) for st_ in streams])
    for i in range(NT):
        gens = []
        for k, st_ in enumerate(streams):
            if i + 1 < NT:
                gens.append(st_["prep"](i + 1, st_["order"][i + 1]))
            gens.append(st_["step"](i, st_["order"][i], first_fn(k, i)))
        merge_gens(gens)


def gla_pools(ar, banks, nstreams):
    pools = {"tmp": [], "St": [], "Sb": [], "cst": [], "banks": []}
    for s_ in range(nstreams):
        tmp = []
        for i in range(2):
            tmp.append({
                "eb": ar.alloc([128, 128], F32), "enb": ar.alloc([128, 128], F32), "ec": ar.alloc([128, 128], F32),
                "qd": ar.alloc([128, 128], BF16), "kb": ar.alloc([128, 128], BF16), "kd": ar.alloc([128, 128], BF16),
                "pm": ar.alloc([128, 128], BF16),
            })
        pools["tmp"].append(tmp)
        pools["St"].append(ar.alloc([128, 128], F32))
        pools["Sb"].append(ar.alloc([128, 128], BF16))
        pools["cst"].append({"eb": ar.alloc([128, 128], F32), "enb": ar.alloc([128, 128], F32), "ec": ar.alloc([128, 128], F32)})
        pools["banks"].append(banks[4 * s_:4 * s_ + 4])
    return pools


def load_consts(S, nc, st, names):
    C = {}
    for n in names:
        d = nc.dram_tensor("c_" + n, [128, 128], F32, kind="ExternalInput").ap()
        t = st.enter_context(nc.sbuf_tensor("cs_" + n, [128, 128], BF16))
        S.dma("pool", t[:], d, "const")
        C[n] = t[:]
    return C


def hgrn_part(S, nc, st, ar, C, psA, psB_, oa_out):
    hqT = nc.dram_tensor("hqT", [128, T], F32, kind="ExternalInput").ap()
    hfT = [nc.dram_tensor("hfT%d" % d, [128, T], F32, kind="ExternalInput").ap() for d in range(2)]
    hft = [nc.dram_tensor("hft%d" % d, [T, 128], F32, kind="ExternalInput").ap() for d in range(2)]
    hit = nc.dram_tensor("hit", [T, 128], F32, kind="ExternalInput").ap()
    lbc = nc.dram_tensor("lbc", [128, 3], F32, kind="ExternalInput").ap()
    lbr = nc.dram_tensor("lbr", [128, 3, 128], F32, kind="ExternalInput").ap()

    qT = ar.alloc([128, T], BF16)
    v_tok = ar.alloc([128, NT, 128], BF16)
    o_acc = ar.alloc([128, NT, 128], F32)
    kT = ar.alloc([128, T], BF16)
    k_tok = ar.alloc([128, NT, 128], BF16)
    g_hl = [ar.alloc([128, NT, 128], BF16) for _ in range(2)]
    stg = [ar.alloc([128, PIECE], F32) for _ in range(2)]
    pools = gla_pools(ar, list(psA) + list(psB_), 1)
    lc = ar.alloc([128, 3], F32)
    lr = ar.alloc([128, 3, 128], F32)
    ssum = ar.alloc([128, 1], F32)
    omlc = ar.alloc([128, 1], F32)
    omlr = ar.alloc([128, 128], F32)
    rs = ar.alloc([128, 128], F32)

    S.dma("sp", lc, lbc, "lb")
    S.dma("sp", lr, lbr, "lb")
    S.add("act", lambda e: e.activation(out=lc, in_=lc, func=AF.Exp), reads=[lc], writes=[lc])
    S.add("act", lambda e: e.activation(out=lr, in_=lr, func=AF.Exp), reads=[lr], writes=[lr])
    S.add("dve", lambda e: e.tensor_tensor(out=omlc, in0=lc[:, 1:2], in1=lc[:, 2:3], op=ALU.add),
          reads=[lc], writes=[omlc])
    S.add("dve", lambda e: e.tensor_tensor(out=ssum, in0=omlc, in1=lc[:, 0:1], op=ALU.add), reads=[omlc, lc], writes=[ssum])
    S.add("dve", lambda e: e.reciprocal(out=ssum, in_=ssum), reads=[ssum], writes=[ssum])
    S.add("dve", lambda e: e.tensor_tensor(out=omlc, in0=omlc, in1=ssum, op=ALU.mult), reads=[omlc, ssum], writes=[omlc])
    S.add("dve", lambda e: e.tensor_tensor(out=omlr, in0=lr[:, 1, :], in1=lr[:, 2, :], op=ALU.add), reads=[lr], writes=[omlr])
    S.add("dve", lambda e: e.tensor_tensor(out=rs, in0=omlr, in1=lr[:, 0, :], op=ALU.add), reads=[omlr, lr], writes=[rs])
    S.add("dve", lambda e: e.reciprocal(out=rs, in_=rs), reads=[rs], writes=[rs])
    S.add("dve", lambda e: e.tensor_tensor(out=omlr, in0=omlr, in1=rs, op=ALU.mult), reads=[omlr, rs], writes=[omlr])

    si = [0]

    def stage():
        s_ = stg[si[0] % 2]
        si[0] += 1
        return s_

    for p in range(NP):
        sl = slice(p * PIECE, (p + 1) * PIECE)
        s_ = stage()
        S.dma("sp", s_, hqT[:, sl], "stg%d" % (si[0] % 2))
        S.add("act", lambda e, s_=s_, sl=sl: e.activation(out=qT[:, sl], in_=s_, func=AF.Silu), reads=[s_], writes=[qT[:, sl]])
    for p in range(NP):
        s_ = stage()
        s3 = s_.rearrange("p (a b) -> p a b", b=128)
        S.dma("sp", s3, hit[p * PIECE:(p + 1) * PIECE, :].rearrange("(n p) d -> p n d", p=128), "stg%d" % (si[0] % 2))
        vv = v_tok[:, p * PT:(p + 1) * PT, :]
        S.add("act", lambda e, s3=s3, vv=vv: e.activation(out=vv, in_=s3, func=AF.Copy, scale=128 ** -0.5),
              reads=[s3], writes=[vv])
    for d in range(2):
        for p in range(NP):
            sl = slice(p * PIECE, (p + 1) * PIECE)
            s_ = stage()
            S.dma("sp", s_, hfT[d][:, sl], "stg%d" % (si[0] % 2))
            S.add("act", lambda e, s_=s_: e.activation(out=s_, in_=s_, func=AF.Sigmoid, scale=-1.0), reads=[s_], writes=[s_])
            S.add("dve", lambda e, s_=s_, sl=sl: e.tensor_scalar(out=kT[:, sl], in0=s_, scalar1=omlc, scalar2=None, op0=ALU.mult),
                  reads=[s_, omlc], writes=[kT[:, sl]])
        for p in range(NP):
            s_ = stage()
            s3 = s_.rearrange("p (a b) -> p a b", b=128)
            S.dma("sp", s3, hft[d][p * PIECE:(p + 1) * PIECE, :].rearrange("(n p) d -> p n d", p=128), "stg%d" % (si[0] % 2))
            S.add("act", lambda e, s3=s3: e.activation(out=s3, in_=s3, func=AF.Sigmoid, scale=-1.0), reads=[s3], writes=[s3])
            ob = omlr.unsqueeze(1).broadcast_to([128, PT, 128])
            S.add("dve", lambda e, s3=s3, ob=ob: e.tensor_tensor(out=s3, in0=s3, in1=ob, op=ALU.mult), reads=[s3, omlr], writes=[s3])
            kk = k_tok[:, p * PT:(p + 1) * PT, :]
            gh = g_hl[0][:, p * PT:(p + 1) * PT, :]
            gl = g_hl[1][:, p * PT:(p + 1) * PT, :]
            S.add("pool", lambda e, s3=s3, kk=kk: e.tensor_copy(out=kk, in_=s3), reads=[s3], writes=[kk])
            S.add("act", lambda e, s3=s3: e.activation(out=s3, in_=s3, func=AF.Ln, scale=-1.0, bias=1.0), reads=[s3], writes=[s3])
            S.add("pool", lambda e, s3=s3, gh=gh: e.tensor_copy(out=gh, in_=s3), reads=[s3], writes=[gh])
            S.add("dve", lambda e, s3=s3, gh=gh, gl=gl: e.tensor_tensor(out=gl, in0=s3, in1=gh, op=ALU.subtract), reads=[s3, gh], writes=[gl])
        run_gla([gla_stream(S, pools, 0, qT, kT, k_tok, g_hl, v_tok, o_acc, d == 1, C)], lambda k, i, d=d: d == 0)
    for p in range(NP):
        S.dma("sp", oa_out[p * PIECE:(p + 1) * PIECE, :].rearrange("(n p) d -> p n d", p=128), o_acc[:, p * PT:(p + 1) * PT, :], "oa")


BLK = 512
SCALE = 192 ** -0.5
PI = math.pi


def mla_part(S, nc, st, ar, banks, obT):
    NB = T // BLK
    din = lambda n, shp, dt=F32: nc.dram_tensor(n, shp, dt, kind="ExternalInput").ap()
    cqT = din("cqT", [512, T]); ckvT = din("ckvT", [256, T]); kpeT = din("kpeT", [64, T]); kpeTs = din("kpeTs", [64, T])
    posb = din("posb", [64, T], I32)
    qnw = din("qnw", [128, 4]); kvnw = din("kvnw", [128, 2])
    wqa = din("wqa", [512, 128]); wqr = din("wqr", [512, 64]); wqrs = din("wqrs", [512, 64])
    wkn = din("wkn", [256, 128]); wkv = din("wkv", [256, 128])
    inv2 = din("inv2", [64, 1]); sgn = din("sgn", [64, 1])

    QA = ar.alloc([128, T], BF16); QB = ar.alloc([128, T], BF16)
    KA = ar.alloc([128, T], BF16); KB = ar.alloc([128, T], BF16)
    v_tok = ar.alloc([128, NT, 128], BF16)
    ctxs = []
    for i in range(2):
        ctxs.append({
            "i": i, "stg": ar.alloc([128, 4, BLK], F32), "sq": ar.alloc([128, 4, BLK], BF16), "hh": ar.alloc([128, 4, BLK], BF16),
            "rrow": ar.alloc([128, BLK], F32), "tm": [ar.alloc([128, BLK], F32) for _ in range(3)],
            "C2": ar.alloc([64, BLK], F32), "S2": ar.alloc([64, BLK], F32), "posi": ar.alloc([64, BLK], I32), "ang": ar.alloc([64, BLK], F32),
            "kpe": [ar.alloc([64, BLK], F32) for _ in range(2)], "rcol": ar.alloc([128, 1], F32), "kmx": ar.alloc([128, 1], F32),
            "banks": banks[4 * i:4 * i + 4],
        })
    osb = [ar.alloc([128, BLK], F32) for _ in range(4)]
    accs = [ar.alloc([128, BLK], F32) for _ in range(2)]
    acc_h = ar.alloc([128, BLK], BF16); acc_l = ar.alloc([128, BLK], BF16)
    pT = [ar.alloc([128, BLK], BF16) for _ in range(3)]
    wqa_b = ar.alloc([128, 4, 128], BF16); wqr_b = ar.alloc([128, 4, 64], BF16); wqrs_b = ar.alloc([128, 4, 64], BF16)
    wkn_b = ar.alloc([128, 2, 128], BF16); wkv_b = ar.alloc([128, 2, 128], BF16)
    qnw_s = ar.alloc([128, 4], F32); kvnw_s = ar.alloc([128, 2], F32)
    inv_s = ar.alloc([64, 1], F32); sgn_s = ar.alloc([64, 1], F32)
    ones = ar.alloc([128, 128], BF16)
    kmax = ar.alloc([128, 1], F32)
    epsq = ar.alloc([128, 1], F32)
    b_s0, b_s1, b_o, b_sum = banks[4:8]

    S.add("pool", lambda e: e.memset(ones, 1.0), writes=[ones])
    S.add("pool", lambda e: e.memset(kmax, 0.0), writes=[kmax])
    S.add("pool", lambda e: e.memset(epsq, EPS), writes=[epsq])
    S.add("pool", lambda e: e.memset(KB[64:65, :], 1.0), writes=[KB[64:65, :]])
    for dst, src, k in [(wqa_b, wqa, 128), (wqr_b, wqr, 64), (wqrs_b, wqrs, 64), (wkn_b, wkn, 128), (wkv_b, wkv, 128)]:
        S.dma("pool", dst, src.rearrange("(c p) n -> p c n", p=128), "mw")
    S.dma("sp", qnw_s, qnw, "mc"); S.dma("sp", kvnw_s, kvnw, "mc")
    S.dma("sp", inv_s, inv2, "mc"); S.dma("sp", sgn_s, sgn, "mc")

    def norm_in(cx, src, nch, dfeat, nw_s, b):
        sl = slice(b * BLK, (b + 1) * BLK)
        s_, sq, hh, rrow, b_ss = cx["stg"], cx["sq"], cx["hh"], cx["rrow"], cx["banks"][0]
        S.dma("sp", s_[:, 0:nch, :], src[:, sl].rearrange("(c p) t -> p c t", p=128), "ms%d" % cx["i"])
        yield
        for c in range(nch):
            S.add("act", lambda e, c=c: e.activation(out=sq[:, c, :], in_=s_[:, c, :], func=AF.Square), reads=[s_[:, c, :]], writes=[sq[:, c, :]])
            yield
            S.add("pe", lambda e, c=c: e.matmul(b_ss, lhsT=ones, rhs=sq[:, c, :], start=(c == 0), stop=(c == nch - 1)), reads=[ones, sq[:, c, :]], writes=[b_ss])
            yield
            S.add("act", lambda e, c=c: e.activation(out=hh[:, c, :], in_=s_[:, c, :], func=AF.Copy, scale=nw_s[:, c:c + 1]),
                  reads=[s_[:, c, :], nw_s[:, c:c + 1]], writes=[hh[:, c, :]])
            yield
        S.add("act", lambda e: e.activation(out=rrow, in_=b_ss, func=AF.Sqrt, scale=1.0 / dfeat, bias=epsq), reads=[b_ss, epsq], writes=[rrow])
        yield
        S.add("dve", lambda e: e.reciprocal(out=rrow, in_=rrow), reads=[rrow], writes=[rrow])
        yield

    def rope_tables(cx, b):
        sl = slice(b * BLK, (b + 1) * BLK)
        posi, ang, C2, S2 = cx["posi"], cx["ang"], cx["C2"], cx["S2"]
        S.dma("sp", posi, posb[:, sl], "mpos%d" % cx["i"])
        yield
        ops = [
            ("dve", lambda e: e.tensor_copy(out=ang, in_=posi), [posi], [ang]),
            ("dve", lambda e: e.tensor_scalar(out=ang, in0=ang, scalar1=inv_s, scalar2=None, op0=ALU.mult), [ang, inv_s], [ang]),
            ("dve", lambda e: e.tensor_copy(out=posi, in_=ang), [ang], [posi]),
            ("dve", lambda e: e.tensor_copy(out=S2, in_=posi), [posi], [S2]),
            ("dve", lambda e: e.tensor_tensor(out=ang, in0=ang, in1=S2, op=ALU.subtract), [ang, S2], [ang]),
            ("dve", lambda e: e.tensor_scalar(out=C2, in0=ang, scalar1=0.25, scalar2=None, op0=ALU.add), [ang], [C2]),
            ("dve", lambda e: e.scalar_tensor_tensor(out=S2, in0=ang, scalar=0.5, in1=ang, op0=ALU.is_ge, op1=ALU.subtract), [ang], [S2]),
            ("dve", lambda e: e.scalar_tensor_tensor(out=ang, in0=C2, scalar=0.5, in1=C2, op0=ALU.is_ge, op1=ALU.subtract), [C2], [ang]),
            ("act", lambda e: e.activation(out=S2, in_=S2, func=AF.Sin, scale=sgn_s), [S2, sgn_s], [S2]),
            ("act", lambda e: e.activation(out=C2, in_=ang, func=AF.Sin, scale=2 * PI), [ang], [C2]),
        ]
        for eng, fn, r, w in ops:
            S.add(eng, fn, reads=r, writes=w)
            yield

    def rope_apply(cx, x, xs, out):
        C2, S2 = cx["C2"], cx["S2"]
        S.add("dve", lambda e: e.tensor_tensor(out=x, in0=x, in1=C2, op=ALU.mult), reads=[x, C2], writes=[x])
        yield
        S.add("dve", lambda e: e.tensor_tensor(out=xs, in0=xs, in1=S2, op=ALU.mult), reads=[xs, S2], writes=[xs])
        yield
        S.add("dve", lambda e: e.tensor_tensor(out=out, in0=xs, in1=x, op=ALU.subtract), reads=[x, xs], writes=[out])
        yield

    def sumsq(cx, A_, B_, sl):
        sq, b_ss = cx["sq"], cx["banks"][0]
        S.add("act", lambda e: e.activation(out=sq[:, 0, :], in_=A_[:, sl], func=AF.Square), reads=[A_[:, sl]], writes=[sq[:, 0, :]])
        yield
        S.add("act", lambda e: e.activation(out=sq[0:64, 1, :], in_=B_[0:64, sl], func=AF.Square), reads=[B_[0:64, sl]], writes=[sq[0:64, 1, :]])
        yield
        S.add("pe", lambda e: e.matmul(b_ss, lhsT=ones, rhs=sq[:, 0, :], start=True, stop=False), reads=[ones, sq[:, 0, :]], writes=[b_ss])
        yield
        S.add("pe", lambda e: e.matmul(b_ss, lhsT=ones[0:64, :], rhs=sq[0:64, 1, :], start=False, stop=True), reads=[ones[0:64, :], sq[0:64, 1, :]], writes=[b_ss])
        yield

    def k_block(cx, b):
        sl = slice(b * BLK, (b + 1) * BLK)
        b_ss, b_main, b_r, b_rs = cx["banks"]
        sq, hh, rrow, rcol = cx["sq"], cx["hh"], cx["rrow"], cx["rcol"]
        yield from norm_in(cx, ckvT, 2, 256.0, kvnw_s, b)
        for c in range(2):
            S.add("pe", lambda e, c=c: e.matmul(b_main, lhsT=wkn_b[:, c, :], rhs=hh[:, c, :], start=(c == 0), stop=(c == 1)), reads=[wkn_b[:, c, :], hh[:, c, :]], writes=[b_main])
            yield
        S.add("dve", lambda e: e.tensor_tensor(out=KA[:, sl], in0=b_main, in1=rrow, op=ALU.mult), reads=[b_main, rrow], writes=[KA[:, sl]])
        yield
        for j in range(4):
            tj = slice(j * 128, (j + 1) * 128)
            for c in range(2):
                S.add("pe", lambda e, c=c, tj=tj: e.matmul(b_r[:, 0:128], lhsT=hh[:, c, tj], rhs=wkv_b[:, c, :], start=(c == 0), stop=(c == 1)), reads=[hh[:, c, tj], wkv_b[:, c, :]], writes=[b_r])
                yield
            for c in range(2):
                S.add("pe", lambda e, c=c, tj=tj: e.matmul(b_rs[:, 0:1], lhsT=sq[:, c, tj], rhs=ones[:, 0:1], start=(c == 0), stop=(c == 1)), reads=[sq[:, c, tj], ones[:, 0:1]], writes=[b_rs])
                yield
            S.add("act", lambda e: e.activation(out=rcol, in_=b_rs[:, 0:1], func=AF.Sqrt, scale=1.0 / 256, bias=epsq), reads=[b_rs, epsq], writes=[rcol])
            yield
            S.add("dve", lambda e: e.reciprocal(out=rcol, in_=rcol), reads=[rcol], writes=[rcol])
            yield
            vt = v_tok[:, b * 4 + j, :]
            S.add("act", lambda e, vt=vt: e.activation(out=vt, in_=b_r[:, 0:128], func=AF.Copy, scale=rcol), reads=[b_r, rcol], writes=[vt])
            yield
        yield from rope_tables(cx, b)
        x, xs = cx["kpe"]
        S.dma("sp", x, kpeT[:, sl], "mk%d" % cx["i"])
        S.dma("sp", xs, kpeTs[:, sl], "mk%d" % cx["i"])
        yield
        yield from rope_apply(cx, x, xs, KB[0:64, sl])
        yield from sumsq(cx, KA, KB, sl)
        S.add("dve", lambda e: e.reduce_max(out=cx["kmx"], in_=b_ss, axis=AX.X), reads=[b_ss], writes=[cx["kmx"]])
        yield
        S.add("dve", lambda e: e.tensor_tensor(out=kmax, in0=kmax, in1=cx["kmx"], op=ALU.max), reads=[kmax, cx["kmx"]], writes=[kmax])
        yield

    def q_block(cx, b):
        sl = slice(b * BLK, (b + 1) * BLK)
        b_ss, b_main, b_r, b_rs = cx["banks"]
        hh, rrow, tm = cx["hh"], cx["rrow"], cx["tm"]
        yield from norm_in(cx, cqT, 4, 512.0, qnw_s, b)
        for c in range(4):
            S.add("pe", lambda e, c=c: e.matmul(b_main, lhsT=wqa_b[:, c, :], rhs=hh[:, c, :], start=(c == 0), stop=(c == 3)), reads=[wqa_b[:, c, :], hh[:, c, :]], writes=[b_main])
            yield
        S.add("dve", lambda e: e.scalar_tensor_tensor(out=QA[:, sl], in0=b_main, scalar=SCALE, in1=rrow, op0=ALU.mult, op1=ALU.mult), reads=[b_main, rrow], writes=[QA[:, sl]])
        yield
        for (wb, bank, dst) in [(wqr_b, b_r, tm[0]), (wqrs_b, b_rs, tm[1])]:
            for c in range(4):
                S.add("pe", lambda e, c=c, wb=wb, bank=bank: e.matmul(bank[0:64, :], lhsT=wb[:, c, :], rhs=hh[:, c, :], start=(c == 0), stop=(c == 3)), reads=[wb[:, c, :], hh[:, c, :]], writes=[bank])
                yield
            S.add("dve", lambda e, bank=bank, dst=dst: e.scalar_tensor_tensor(out=dst[0:64, :], in0=bank[0:64, :], scalar=SCALE, in1=rrow[0:64, :], op0=ALU.mult, op1=ALU.mult),
                  reads=[bank, rrow], writes=[dst[0:64, :]])
            yield
        yield from rope_tables(cx, b)
        yield from rope_apply(cx, tm[0][0:64, :], tm[1][0:64, :], QB[0:64, sl])
        yield from sumsq(cx, QA, QB, sl)
        S.add("act", lambda e: e.activation(out=tm[2], in_=b_ss, func=AF.Sqrt, scale=kmax), reads=[b_ss, kmax], writes=[tm[2]])
        yield
        S.add("dve", lambda e: e.tensor_scalar(out=QB[64:65, sl], in0=tm[2][64:65, :], scalar1=-1.0, scalar2=None, op0=ALU.mult), reads=[tm[2][64:65, :]], writes=[QB[64:65, sl]])
        yield

    for b in range(0, NB, 2):
        merge_gens([k_block(ctxs[i], b + i) for i in range(2) if b + i < NB])

    def attention(qb):
        qs = slice(qb * BLK, (qb + 1) * BLK)
        acc = accs[qb % 2]

        def qk(i):
            ks = slice(i * 128, (i + 1) * 128)
            bs = b_s0 if i % 2 == 0 else b_s1
            S.add("pe", lambda e: e.matmul(bs, lhsT=KA[:, ks], rhs=QA[:, qs], start=True, stop=False), reads=[KA[:, ks], QA[:, qs]], writes=[bs])
            S.add("pe", lambda e: e.matmul(bs, lhsT=KB[0:65, ks], rhs=QB[0:65, qs], start=False, stop=True), reads=[KB[0:65, ks], QB[0:65, qs]], writes=[bs])

        qk(0)
        yield
        for kt in range(NT):
            bs = b_s0 if kt % 2 == 0 else b_s1
            p_ = pT[kt % 3]
            S.add("act", lambda e, bs=bs, p_=p_: e.activation(out=p_, in_=bs, func=AF.Exp), reads=[bs], writes=[p_])
            if kt + 1 < NT:
                qk(kt + 1)
            S.add("pe", lambda e, kt=kt, p_=p_: e.matmul(b_o, lhsT=v_tok[:, kt, :], rhs=p_, start=(kt == 0), stop=(kt == NT - 1)), reads=[v_tok[:, kt, :], p_], writes=[b_o])
            if kt == 0:
                S.add("dve", lambda e, p_=p_: e.tensor_copy(out=acc, in_=p_), reads=[p_], writes=[acc])
            else:
                S.add("dve", lambda e, p_=p_: e.tensor_tensor(out=acc, in0=p_, in1=acc, op=ALU.add), reads=[p_, acc], writes=[acc])
            yield
        S.add("act", lambda e: e.activation(out=acc_h, in_=acc, func=AF.Copy), reads=[acc], writes=[acc_h])
        S.add("dve", lambda e: e.tensor_tensor(out=acc_l, in0=acc, in1=acc_h, op=ALU.subtract), reads=[acc, acc_h], writes=[acc_l])
        S.add("pe", lambda e: e.matmul(b_sum, lhsT=ones, rhs=acc_h, start=True, stop=False), reads=[ones, acc_h], writes=[b_sum])
        S.add("pe", lambda e: e.matmul(b_sum, lhsT=ones, rhs=acc_l, start=False, stop=True), reads=[ones, acc_l], writes=[b_sum])
        rs_ = osb[qb % 2]
        ob_ = osb[2 + qb % 2]
        S.add("dve", lambda e: e.reciprocal(out=rs_, in_=b_sum), reads=[b_sum], writes=[rs_])
        S.add("dve", lambda e: e.tensor_tensor(out=ob_, in0=b_o, in1=rs_, op=ALU.mult), reads=[b_o, rs_], writes=[ob_])
        S.dma("sp", obT[:, qs], ob_, "mo%d" % (qb % 2))
        yield

    for _ in q_block(ctxs[0], 0):
        pass
    for qb in range(NB):
        gens = [attention(qb)]
        if qb + 1 < NB:
            gens.append(q_block(ctxs[0], qb + 1))
        merge_gens(gens)
    return ["mo0", "mo1"] if NB > 1 else ["mo0"]


def build_B():
    nc = bass.Bass("TRN2", target_bir_lowering=False)
    oa = nc.dram_tensor("oa", [T, 128], F32, kind="ExternalOutput").ap()
    obT = nc.dram_tensor("obT", [128, T], F32, kind="ExternalOutput").ap()
    S = Sched(nc)
    with ExitStack() as st:
        arena_t = st.enter_context(nc.sbuf_tensor("arena", [128, 44 * 1024], F32))
        C = load_consts(S, nc, st, ["U", "L", "SU", "SL"])
        banks = [st.enter_context(nc.psum_tensor("pb%d" % i, [128, 512], F32))[:] for i in range(8)]
        hgrn_part(S, nc, st, Arena(arena_t, 44 * 1024), C, banks[0:3], banks[3:5], oa)
        keys = mla_part(S, nc, st, Arena(arena_t, 44 * 1024), banks, obT)
        S.finish_wait("sp", ["oa"] + keys)
        S.build()
    return nc

def rope_tables_fm(S, nrows, posb, posi, ang, C2, S2, inv_s, sgn_s, b, key):
    sl = slice(b * BLK, (b + 1) * BLK)
    S.dma("sp", posi, posb[0:nrows, sl], key)
    S.add("dve", lambda e: e.tensor_copy(out=ang, in_=posi), reads=[posi], writes=[ang])
    S.add("dve", lambda e: e.tensor_scalar(out=ang, in0=ang, scalar1=inv_s, scalar2=None, op0=ALU.mult), reads=[ang, inv_s], writes=[ang])
    S.add("dve", lambda e: e.tensor_copy(out=posi, in_=ang), reads=[ang], writes=[posi])
    S.add("dve", lambda e: e.tensor_copy(out=S2, in_=posi), reads=[posi], writes=[S2])
    S.add("dve", lambda e: e.tensor_tensor(out=ang, in0=ang, in1=S2, op=ALU.subtract), reads=[ang, S2], writes=[ang])
    S.add("dve", lambda e: e.tensor_scalar(out=C2, in0=ang, scalar1=0.25, scalar2=None, op0=ALU.add), reads=[ang], writes=[C2])
    S.add("dve", lambda e: e.scalar_tensor_tensor(out=S2, in0=ang, scalar=0.5, in1=ang, op0=ALU.is_ge, op1=ALU.subtract), reads=[ang], writes=[S2])
    S.add("dve", lambda e: e.scalar_tensor_tensor(out=ang, in0=C2, scalar=0.5, in1=C2, op0=ALU.is_ge, op1=ALU.subtract), reads=[C2], writes=[ang])
    S.add("act", lambda e: e.activation(out=S2, in_=S2, func=AF.Sin, scale=2 * PI), reads=[S2], writes=[S2])
    S.add("act", lambda e: e.activation(out=C2, in_=ang, func=AF.Sin, scale=2 * PI), reads=[ang], writes=[C2])
    S.add("dve", lambda e: e.tensor_scalar(out=S2, in0=S2, scalar1=sgn_s, scalar2=None, op0=ALU.mult), reads=[S2, sgn_s], writes=[S2])
    S.add("dve", lambda e: e.tensor_scalar(out=C2, in0=C2, scalar1=-1.0, scalar2=None, op0=ALU.mult), reads=[C2], writes=[C2])


def transpose_to_tok(S, srcT, dst_tok, ident, bank, c, eng="act"):
    pt = bank[:, 0:64].bitcast(BF16)
    sl = slice(c * 128, (c + 1) * 128)
    S.add("pe", lambda e: e.transpose(pt, srcT[:, sl], ident), reads=[srcT[:, sl], ident], writes=[pt])
    if eng == "act":
        S.add("act", lambda e: e.activation(out=dst_tok[:, c, :], in_=pt, func=AF.Copy), reads=[pt], writes=[dst_tok[:, c, :]])
    else:
        S.add("dve", lambda e: e.tensor_copy(out=dst_tok[:, c, :], in_=pt), reads=[pt], writes=[dst_tok[:, c, :]])


def ret_part(S, nc, st, ar, C, banks, oc_out):
    NB = T // BLK
    din = lambda n, shp, dt=F32: nc.dram_tensor(n, shp, dt, kind="ExternalInput").ap()
    rqT = din("rqT", [128, T]); rqTs = din("rqTs", [128, T]); rkT = din("rkT", [128, T]); rkTs = din("rkTs", [128, T])
    rvt = din("rvt", [T, 128]); posb = din("rposb", [128, T], I32)
    inv2 = din("rinv2", [128, 1]); sgn = din("rsgn", [128, 1]); lg = din("rlg", [128, 2])
    qT = ar.alloc([128, T], BF16); kT = ar.alloc([128, T], BF16)
    k_tok = ar.alloc([128, NT, 128], BF16); v_tok = ar.alloc([128, NT, 128], BF16)
    g_hls = [[ar.alloc([128, 1, 128], BF16) for _ in range(2)] for _ in range(2)]
    o_acc = ar.alloc([128, NT, 128], F32)
    pools = gla_pools(ar, banks, 2)
    C2 = ar.alloc([128, BLK], F32); S2 = ar.alloc([128, BLK], F32)
    posi = ar.alloc([128, BLK], I32); ang = ar.alloc([128, BLK], F32)
    xs = [ar.alloc([128, BLK], F32) for _ in range(4)]
    stg = ar.alloc([128, PT, 128], F32)
    inv_s = ar.alloc([128, 1], F32); sgn_s = ar.alloc([128, 1], F32); lg_s = ar.alloc([128, 2], F32)
    S.dma("sp", inv_s, inv2, "rc"); S.dma("sp", sgn_s, sgn, "rc"); S.dma("sp", lg_s, lg, "rc")
    for b in range(NB):
        sl = slice(b * BLK, (b + 1) * BLK)
        rope_tables_fm(S, 128, posb, posi, ang, C2, S2, inv_s, sgn_s, b, "rpos")
        for i, src in enumerate([rqT, rqTs, rkT, rkTs]):
            S.dma("sp", xs[i], src[:, sl], "rx%d" % i)
        for (x, xsw, dst) in [(xs[0], xs[1], qT), (xs[2], xs[3], kT)]:
            S.add("dve", lambda e, x=x: e.tensor_tensor(out=x, in0=x, in1=C2, op=ALU.mult), reads=[x, C2], writes=[x])
            S.add("dve", lambda e, xsw=xsw: e.tensor_tensor(out=xsw, in0=xsw, in1=S2, op=ALU.mult), reads=[xsw, S2], writes=[xsw])
            S.add("dve", lambda e, x=x, xsw=xsw, dst=dst, sl=sl: e.tensor_tensor(out=dst[:, sl], in0=x, in1=xsw, op=ALU.add),
                  reads=[x, xsw], writes=[dst[:, sl]])
        for j in range(BLK // 128):
            transpose_to_tok(S, kT, k_tok, C["I"], banks[5 + (j % 2)], b * (BLK // 128) + j, eng="act" if j % 2 == 0 else "dve")
    for p in range(NP):
        S.dma("sp", stg, rvt[p * PIECE:(p + 1) * PIECE, :].rearrange("(n p) d -> p n d", p=128), "rv")
        vv = v_tok[:, p * PT:(p + 1) * PT, :]
        S.add("act", lambda e, vv=vv: e.activation(out=vv, in_=stg, func=AF.Copy, scale=128 ** -0.5), reads=[stg], writes=[vv])
    streams = []
    for d in range(2):
        gcol = lg_s[:, d:d + 1]
        gh = g_hls[d][0].rearrange("p a b -> p (a b)"); gl = g_hls[d][1].rearrange("p a b -> p (a b)")
        S.add("dve", lambda e, gcol=gcol, gh=gh: e.tensor_copy(out=gh, in_=gcol.broadcast_to([128, 128])), reads=[gcol], writes=[gh])
        S.add("dve", lambda e, gcol=gcol, gh=gh, gl=gl: e.scalar_tensor_tensor(out=gl, in0=gh, scalar=-1.0, in1=gcol.broadcast_to([128, 128]), op0=ALU.mult, op1=ALU.add),
              reads=[gh, gcol], writes=[gl])
        streams.append(gla_stream(S, pools, d, qT, kT, k_tok, g_hls[d], v_tok, o_acc, d == 1, C, const_g=True))
    run_gla(streams, lambda k, i: i < NT // 2)
    for p in range(NP):
        S.dma("sp", oc_out[p * PIECE:(p + 1) * PIECE, :].rearrange("(n p) d -> p n d", p=128), o_acc[:, p * PT:(p + 1) * PT, :], "oc")
    return ["oc"]


def gdn_part(S, nc, st, ar, C, banks, od_out):
    NB = T // BLK
    din = lambda n, shp, dt=F32: nc.dram_tensor(n, shp, dt, kind="ExternalInput").ap()
    gxT = [din("gxT%d" % i, [128, T]) for i in range(3)]
    gcw = din("gcw", [128, 15])
    gha = [din("gha%d" % d, [128, NT]) for d in range(2)]
    ghb = [din("ghb%d" % d, [128, NT]) for d in range(2)]
    gal = din("gal", [128, 2]); gdt = din("gdt", [128, 2])
    b_g, b_kk, b_qk, b_t, b_x, b_p, b_v, b_o = banks

    qT = ar.alloc([128, T], BF16); kT = ar.alloc([128, T], BF16)
    k_tok = ar.alloc([128, NT, 128], BF16); v_tok = ar.alloc([128, NT, 128], BF16)
    o_acc = ar.alloc([128, NT, 128], F32)
    la = [ar.alloc([128, NT], F32) for _ in range(2)]; beta = [ar.alloc([128, NT], F32) for _ in range(2)]
    la_hl = [[ar.alloc([128, NT], BF16) for _ in range(2)] for _ in range(2)]
    cw_s = ar.alloc([128, 15], F32); al_s = ar.alloc([128, 2], F32); dt_s = ar.alloc([128, 2], F32)
    xb = ar.alloc([128, BLK + 4], F32); acc = ar.alloc([128, BLK], F32); sqb = ar.alloc([128, BLK], BF16)
    vTb = ar.alloc([128, BLK], BF16); rrow = ar.alloc([128, BLK], F32)
    ones = ar.alloc([128, 128], BF16); epsb = ar.alloc([128, 1], F32)
    Sts = [ar.alloc([128, 128], F32) for _ in range(2)]; Sbs = [ar.alloc([128, 128], BF16) for _ in range(2)]
    Sls = [ar.alloc([128, 128], BF16) for _ in range(2)]
    tbs = [[], []]
    for i in range(4):
        d_ = {}
        for n in ["tA", "tB", "Eg", "X32", "T32"]:
            d_[n] = ar.alloc([128, 128], F32)
        for n in ["lab0", "lab1", "P0", "P1", "PT0", "PT1", "Xb", "QKm", "Rv", "Rw", "nwT", "nwTl", "qd", "kd", "vn", "Tb", "Y", "Yp", "No64", "No64T", "No128", "No128T"]:
            d_[n] = ar.alloc([128, 128], BF16)
        for n in ["gc", "gl", "bw", "kds"]:
            d_[n] = ar.alloc([128, 1], F32)
        tbs[i // 2].append(d_)

    S.add("pool", lambda e: e.memset(ones, 1.0), writes=[ones])
    S.add("pool", lambda e: e.memset(epsb, EPS), writes=[epsb])
    S.dma("sp", cw_s, gcw, "gc"); S.dma("sp", al_s, gal, "gc"); S.dma("sp", dt_s, gdt, "gc")
    S.add("act", lambda e: e.activation(out=al_s, in_=al_s, func=AF.Exp), reads=[al_s], writes=[al_s])
    for d in range(2):
        S.dma("sp", la[d], gha[d], "gg%d" % d); S.dma("sp", beta[d], ghb[d], "gg%d" % d)
        S.add("act", lambda e, d=d: e.activation(out=la[d], in_=la[d], func=AF.Exp, bias=dt_s[:, d:d + 1]), reads=[la[d], dt_s], writes=[la[d]])
        S.add("act", lambda e, d=d: e.activation(out=la[d], in_=la[d], func=AF.Ln, bias=1.0), reads=[la[d]], writes=[la[d]])
        S.add("dve", lambda e, d=d: e.tensor_scalar(out=la[d], in0=la[d], scalar1=al_s[:, d:d + 1], scalar2=-1.0, op0=ALU.mult, op1=ALU.mult),
              reads=[la[d], al_s], writes=[la[d]])
        S.add("act", lambda e, d=d: e.activation(out=beta[d], in_=beta[d], func=AF.Sigmoid), reads=[beta[d]], writes=[beta[d]])
        S.add("pool", lambda e, d=d: e.tensor_copy(out=la_hl[d][0], in_=la[d]), reads=[la[d]], writes=[la_hl[d][0]])
        S.add("dve", lambda e, d=d: e.tensor_tensor(out=la_hl[d][1], in0=la[d], in1=la_hl[d][0], op=ALU.subtract),
              reads=[la[d], la_hl[d][0]], writes=[la_hl[d][1]])
    for i in range(3):
        for b in range(NB):
            lo = b * BLK - 2
            hi = (b + 1) * BLK + 2
            slo, shi = max(lo, 0), min(hi, T)
            if lo < 0:
                S.add("pool", lambda e: e.memset(xb[:, 0:2], 0.0), writes=[xb[:, 0:2]])
            if hi > T:
                S.add("pool", lambda e: e.memset(xb[:, BLK + 2:BLK + 4], 0.0), writes=[xb[:, BLK + 2:BLK + 4]])
            S.dma("sp", xb[:, slo - lo:shi - lo], gxT[i][:, slo:shi], "gx")
            S.add("dve", lambda e, i=i: e.tensor_scalar(out=acc, in0=xb[:, 0:BLK], scalar1=cw_s[:, i * 5:i * 5 + 1], scalar2=None, op0=ALU.mult),
                  reads=[xb[:, 0:BLK], cw_s], writes=[acc])
            for j in range(1, 5):
                S.add("dve", lambda e, i=i, j=j: e.scalar_tensor_tensor(out=acc, in0=xb[:, j:j + BLK], scalar=cw_s[:, i * 5 + j:i * 5 + j + 1], in1=acc,
                                                                       op0=ALU.mult, op1=ALU.add),
                      reads=[xb[:, j:j + BLK], cw_s, acc], writes=[acc])
            sl = slice(b * BLK, (b + 1) * BLK)
            if i == 2:
                S.add("act", lambda e: e.activation(out=vTb, in_=acc, func=AF.Silu), reads=[acc], writes=[vTb])
                for j in range(BLK // 128):
                    transpose_to_tok(S, vTb, v_tok[:, b * (BLK // 128):(b + 1) * (BLK // 128), :], C["I"], b_t if j % 2 == 0 else b_x, j,
                                     eng="act" if j % 2 == 0 else "dve")
            else:
                dst = qT if i == 0 else kT
                S.add("act", lambda e: e.activation(out=acc, in_=acc, func=AF.Silu), reads=[acc], writes=[acc])
                S.add("act", lambda e: e.activation(out=sqb, in_=acc, func=AF.Square), reads=[acc], writes=[sqb])
                S.add("pe", lambda e: e.matmul(b_g, lhsT=ones, rhs=sqb, start=True, stop=True), reads=[ones, sqb], writes=[b_g])
                S.add("act", lambda e: e.activation(out=rrow, in_=b_g, func=AF.Sqrt, bias=epsb), reads=[b_g, epsb], writes=[rrow])
                S.add("dve", lambda e: e.reciprocal(out=rrow, in_=rrow), reads=[rrow], writes=[rrow])
                sc = 128 ** -0.5 if i == 0 else 1.0
                S.add("dve", lambda e, dst=dst, sl=sl, sc=sc: e.scalar_tensor_tensor(out=dst[:, sl], in0=acc, scalar=sc, in1=rrow, op0=ALU.mult, op1=ALU.mult),
                      reads=[acc, rrow], writes=[dst[:, sl]])
                if i == 1:
                    for j in range(BLK // 128):
                        transpose_to_tok(S, kT, k_tok, C["I"], b_t if j % 2 == 0 else b_x, b * (BLK // 128) + j, eng="act" if j % 2 == 0 else "dve")

    def prep(i, c, d, rev):
        t = tbs[d][i % 2]
        b_g = b_kk = b_qk = banks[4 * d + 0]
        b_t = b_x = banks[4 * d + 1]
        b_p = banks[4 * d + 2]
        XC = slice(0, 128); TC = slice(128, 256); WC = slice(384, 512)
        Ucs = C["L"] if rev else C["U"]
        NegStrict = C["NSU"] if rev else C["NSL"]
        sl = slice(c * 128, (c + 1) * 128)
        ecol = 0 if rev else 127
        for hl in range(2):
            S.add("act", lambda e, hl=hl: e.activation(out=t["lab%d" % hl], in_=la_hl[d][hl][:, c:c + 1].broadcast_to([128, 128]), func=AF.Copy),
                  reads=[la_hl[d][hl][:, c:c + 1]], writes=[t["lab%d" % hl]])
            yield
        gcp = b_g[:, 128:129]; grow = b_g[:, 0:128]
        for hl in range(2):
            S.add("pe", lambda e, hl=hl: e.matmul(gcp, lhsT=Ucs, rhs=la_hl[d][hl][:, c:c + 1], start=(hl == 0), stop=(hl == 1)),
                  reads=[Ucs, la_hl[d][hl][:, c:c + 1]], writes=[gcp])
            yield
        S.add("act", lambda e: e.activation(out=t["gc"], in_=gcp, func=AF.Copy), reads=[gcp], writes=[t["gc"]])
        yield
        for hl in range(2):
            S.add("pe", lambda e, hl=hl: e.matmul(grow, lhsT=t["lab%d" % hl], rhs=Ucs, start=(hl == 0), stop=(hl == 1)),
                  reads=[t["lab%d" % hl], Ucs], writes=[grow])
            yield
        S.add("dve", lambda e: e.tensor_scalar(out=t["tA"], in0=grow, scalar1=t["gc"], scalar2=0.0, op0=ALU.subtract, op1=ALU.max),
              reads=[grow, t["gc"]], writes=[t["tA"]])
        yield
        S.add("act", lambda e: e.activation(out=t["tA"], in_=t["tA"], func=AF.Exp, scale=-1.0), reads=[t["tA"]], writes=[t["tA"]])
        yield
        S.add("dve", lambda e: e.tensor_scalar(out=t["tB"], in0=grow, scalar1=t["gc"], scalar2=0.0, op0=ALU.subtract, op1=ALU.min),
              reads=[grow, t["gc"]], writes=[t["tB"]])
        yield
        S.add("act", lambda e: e.activation(out=t["tB"], in_=t["tB"], func=AF.Exp), reads=[t["tB"]], writes=[t["tB"]])
        yield
        S.add("act", lambda e: e.activation(out=t["Eg"], in_=grow, func=AF.Exp), reads=[grow], writes=[t["Eg"]])
        yield
        S.add("act", lambda e: e.activation(out=t["gl"], in_=grow[:, ecol:ecol + 1], func=AF.Copy), reads=[grow], writes=[t["gl"]])
        yield
        S.add("pe", lambda e: e.matmul(b_kk[:, 256:384], lhsT=kT[:, sl], rhs=kT[:, sl], start=True, stop=True), reads=[kT[:, sl]], writes=[b_kk])
        yield
        S.add("pe", lambda e: e.matmul(b_qk[:, 384:512], lhsT=kT[:, sl], rhs=qT[:, sl], start=True, stop=True), reads=[kT[:, sl], qT[:, sl]], writes=[b_qk])
        yield
        S.add("dve", lambda e: e.tensor_tensor(out=t["tA"], in0=b_kk[:, 256:384], in1=t["tA"], op=ALU.mult), reads=[b_kk, t["tA"]], writes=[t["tA"]])
        yield
        S.add("dve", lambda e: e.scalar_tensor_tensor(out=t["P0"], in0=t["tA"], scalar=beta[d][:, c:c + 1], in1=NegStrict, op0=ALU.mult, op1=ALU.mult),
              reads=[t["tA"], beta[d][:, c:c + 1], NegStrict], writes=[t["P0"]])
        yield
        S.add("dve", lambda e: e.tensor_tensor(out=t["tB"], in0=b_qk[:, 384:512], in1=t["tB"], op=ALU.mult), reads=[b_qk, t["tB"]], writes=[t["tB"]])
        yield
        S.add("pool", lambda e: e.tensor_tensor(out=t["QKm"], in0=t["tB"], in1=Ucs, op=ALU.mult), reads=[t["tB"], Ucs], writes=[t["QKm"]])
        yield
        ptr = b_t[:, 256:320].bitcast(BF16)
        S.add("pe", lambda e: e.transpose(ptr, t["P0"], C["I"]), reads=[t["P0"], C["I"]], writes=[ptr])
        yield
        S.add("act", lambda e: e.activation(out=t["PT0"], in_=ptr, func=AF.Copy), reads=[ptr], writes=[t["PT0"]])
        yield
        for nm, src, msk in [("No64", "P0", "OFF64"), ("No64T", "PT0", "OFF64"), ("No128", "P0", "OFF128"), ("No128T", "PT0", "OFF128")]:
            S.add("pool", lambda e, nm=nm, src=src, msk=msk: e.tensor_tensor(out=t[nm], in0=t[src], in1=C[msk], op=ALU.mult),
                  reads=[t[src], C[msk]], writes=[t[nm]])
            yield
        S.add("pool", lambda e: e.tensor_tensor(out=t["P0"], in0=t["P0"], in1=C["BD32"], op=ALU.mult), reads=[t["P0"], C["BD32"]], writes=[t["P0"]])
        yield
        S.add("pool", lambda e: e.tensor_tensor(out=t["PT0"], in0=t["PT0"], in1=C["BD32"], op=ALU.mult), reads=[t["PT0"], C["BD32"]], writes=[t["PT0"]])
        yield
        S.add("pool", lambda e: e.tensor_copy(out=t["Xb"], in_=C["I"]), reads=[C["I"]], writes=[t["Xb"]])
        yield
        S.add("pool", lambda e: e.tensor_copy(out=t["Tb"], in_=C["I"]), reads=[C["I"]], writes=[t["Tb"]])
        yield
        for lv in range(5):
            P, PT = t["P%d" % (lv % 2)], t["PT%d" % (lv % 2)]
            Pn, PTn = t["P%d" % ((lv + 1) % 2)], t["PT%d" % ((lv + 1) % 2)]
            S.add("pe", lambda e, P=P: e.matmul(b_x[:, 0:128], lhsT=P, rhs=t["Xb"], start=True, stop=True), reads=[P, t["Xb"]], writes=[b_x])
            yield
            S.add("pe", lambda e, P=P: e.matmul(b_x[:, 128:256], lhsT=t["Xb"], rhs=P, start=True, stop=True), reads=[P, t["Xb"]], writes=[b_x])
            yield
            if lv < 4:
                S.add("pe", lambda e, P=P, PT=PT: e.matmul(b_p[:, 0:128], lhsT=PT, rhs=P, start=True, stop=True), reads=[P, PT], writes=[b_p])
                yield
                S.add("pe", lambda e, P=P, PT=PT: e.matmul(b_p[:, 128:256], lhsT=P, rhs=PT, start=True, stop=True), reads=[P, PT], writes=[b_p])
                yield
            S.add("dve", lambda e: e.tensor_tensor(out=t["Xb"], in0=b_x[:, 0:128], in1=t["Xb"], op=ALU.add), reads=[b_x, t["Xb"]], writes=[t["Xb"]])
            yield
            S.add("dve", lambda e: e.tensor_tensor(out=t["Tb"], in0=b_x[:, 128:256], in1=t["Tb"], op=ALU.add), reads=[b_x, t["Tb"]], writes=[t["Tb"]])
            yield
            if lv < 4:
                S.add("act", lambda e, Pn=Pn: e.activation(out=Pn, in_=b_p[:, 0:128], func=AF.Copy), reads=[b_p], writes=[Pn])
                yield
                S.add("act", lambda e, PTn=PTn: e.activation(out=PTn, in_=b_p[:, 128:256], func=AF.Copy), reads=[b_p], writes=[PTn])
                yield
        for (No, NoT) in [("No64", "No64T"), ("No128", "No128T")]:
            S.add("pe", lambda e, NoT=NoT: e.matmul(b_p[:, 0:128], lhsT=t[NoT], rhs=t["Tb"], start=True, stop=True), reads=[t[NoT], t["Tb"]], writes=[b_p])
            yield
            S.add("pe", lambda e, No=No: e.matmul(b_p[:, 128:256], lhsT=t[No], rhs=t["Xb"], start=True, stop=True), reads=[t[No], t["Xb"]], writes=[b_p])
            yield
            S.add("act", lambda e: e.activation(out=t["Yp"], in_=b_p[:, 0:128], func=AF.Copy), reads=[b_p], writes=[t["Yp"]])
            yield
            S.add("act", lambda e: e.activation(out=t["Y"], in_=b_p[:, 128:256], func=AF.Copy), reads=[b_p], writes=[t["Y"]])
            yield
            S.add("pe", lambda e: e.matmul(b_x[:, 0:128], lhsT=t["Tb"], rhs=t["Y"], start=True, stop=True), reads=[t["Tb"], t["Y"]], writes=[b_x])
            yield
            S.add("pe", lambda e: e.matmul(b_x[:, 128:256], lhsT=t["Xb"], rhs=t["Yp"], start=True, stop=True), reads=[t["Xb"], t["Yp"]], writes=[b_x])
            yield
            S.add("dve", lambda e: e.tensor_tensor(out=t["Xb"], in0=b_x[:, 0:128], in1=t["Xb"], op=ALU.add), reads=[b_x, t["Xb"]], writes=[t["Xb"]])
            yield
            S.add("dve", lambda e: e.tensor_tensor(out=t["Tb"], in0=b_x[:, 128:256], in1=t["Tb"], op=ALU.add), reads=[b_x, t["Tb"]], writes=[t["Tb"]])
            yield
        S.add("dve", lambda e: e.tensor_scalar(out=t["Rv"], in0=v_tok[:, c, :], scalar1=beta[d][:, c:c + 1], scalar2=None, op0=ALU.mult),
              reads=[v_tok[:, c, :], beta[d][:, c:c + 1]], writes=[t["Rv"]])
        yield
        S.add("act", lambda e: e.activation(out=t["bw"], in_=t["gc"], func=AF.Exp), reads=[t["gc"]], writes=[t["bw"]])
        yield
        S.add("dve", lambda e: e.tensor_scalar(out=t["Rw"], in0=k_tok[:, c, :], scalar1=t["bw"], scalar2=beta[d][:, c:c + 1], op0=ALU.mult, op1=ALU.mult),
              reads=[k_tok[:, c, :], t["bw"], beta[d][:, c:c + 1]], writes=[t["Rw"]])
        yield
        S.add("pe", lambda e: e.matmul(b_t[:, 384:512], lhsT=t["Rw"], rhs=t["Xb"], start=True, stop=True), reads=[t["Rw"], t["Xb"]], writes=[b_t])
        yield
        S.add("act", lambda e: e.activation(out=t["nwT"], in_=b_t[:, 384:512], func=AF.Copy, scale=-1.0), reads=[b_t], writes=[t["nwT"]])
        yield
        S.add("pool", lambda e: e.tensor_tensor(out=t["qd"], in0=qT[:, sl], in1=t["Eg"], op=ALU.mult), reads=[qT[:, sl], t["Eg"]], writes=[t["qd"]])
        yield
        S.add("act", lambda e: e.activation(out=t["kds"], in_=t["gc"], func=AF.Exp, scale=-1.0, bias=t["gl"]), reads=[t["gc"], t["gl"]], writes=[t["kds"]])
        yield
        S.add("dve", lambda e: e.tensor_scalar(out=t["kd"], in0=k_tok[:, c, :], scalar1=t["kds"], scalar2=None, op0=ALU.mult),
              reads=[k_tok[:, c, :], t["kds"]], writes=[t["kd"]])
        yield

    def step(i, c, first, rev, d):
        t = tbs[d][i % 2]
        b_v = b_o = banks[4 * d + 3]
        St, Sb, Sl = Sts[d], Sbs[d], Sls[d]
        ecol = 0 if rev else 127
        S.add("pe", lambda e: e.matmul(b_v[:, 0:128], lhsT=t["Xb"], rhs=t["Rv"], start=True, stop=False), reads=[t["Xb"], t["Rv"]], writes=[b_v])
        yield
        S.add("pe", lambda e: e.matmul(b_v[:, 0:128], lhsT=t["nwT"], rhs=Sb, start=False, stop=True), reads=[t["nwT"], Sb], writes=[b_v])
        yield
        S.add("act", lambda e: e.activation(out=t["vn"], in_=b_v[:, 0:128], func=AF.Copy), reads=[b_v], writes=[t["vn"]])
        yield
        S.add("pe", lambda e: e.matmul(b_o[:, 256:384], lhsT=t["qd"], rhs=Sb, start=True, stop=False), reads=[t["qd"], Sb], writes=[b_o])
        yield
        S.add("pe", lambda e: e.matmul(b_o[:, 256:384], lhsT=t["QKm"], rhs=t["vn"], start=False, stop=True), reads=[t["QKm"], t["vn"]], writes=[b_o])
        yield
        if first:
            S.add("act", lambda e: e.activation(out=o_acc[:, c, :], in_=b_o[:, 256:384], func=AF.Copy), reads=[b_o], writes=[o_acc[:, c, :]])
            yield
        else:
            S.add("dve", lambda e: e.tensor_tensor(out=o_acc[:, c, :], in0=b_o[:, 256:384], in1=o_acc[:, c, :], op=ALU.add),
                  reads=[b_o, o_acc[:, c, :]], writes=[o_acc[:, c, :]])
            yield
        S.add("pe", lambda e: e.matmul(b_v[:, 128:256], lhsT=t["kd"], rhs=t["vn"], start=True, stop=True), reads=[t["kd"], t["vn"]], writes=[b_v])
        yield
        dcol = t["Eg"][:, ecol:ecol + 1]
        S.add("dve", lambda e: e.scalar_tensor_tensor(out=St, in0=St, scalar=dcol, in1=b_v[:, 128:256], op0=ALU.mult, op1=ALU.add),
              reads=[St, dcol, b_v], writes=[St])
        yield
        S.add("act", lambda e: e.activation(out=Sb, in_=St, func=AF.Copy), reads=[St], writes=[Sb])
        yield

    def merge(gens):
        active = list(gens)
        while active:
            for g_ in list(active):
                try:
                    next(g_)
                except StopIteration:
                    active.remove(g_)

    for d in range(2):
        S.add("pool", lambda e, d=d: e.memset(Sts[d], 0.0), writes=[Sts[d]])
        S.add("pool", lambda e, d=d: e.memset(Sbs[d], 0.0), writes=[Sbs[d]])
        S.add("pool", lambda e, d=d: e.memset(Sls[d], 0.0), writes=[Sls[d]])
    orders = [list(range(NT)), list(range(NT - 1, -1, -1))]
    merge([prep(0, orders[d][0], d, d == 1) for d in range(2)])
    for i in range(NT):
        gens = []
        for d in range(2):
            if i + 1 < NT:
                gens.append(prep(i + 1, orders[d][i + 1], d, d == 1))
            gens.append(step(i, orders[d][i], i < NT // 2, d == 1, d))
        merge(gens)
    for p in range(NP):
        S.dma("sp", od_out[p * PIECE:(p + 1) * PIECE, :].rearrange("(n p) d -> p n d", p=128), o_acc[:, p * PT:(p + 1) * PT, :], "od")
    return ["od"]


def load_consts1(S, nc, st):
    C = load_consts(S, nc, st, ["U", "L", "SU", "SL", "I", "NSU", "NSL", "BD32", "OFF64", "OFF128"])
    d = nc.dram_tensor("c_I32", [128, 128], F32, kind="ExternalInput").ap()
    t = st.enter_context(nc.sbuf_tensor("cs_I32", [128, 128], F32))
    S.dma("sp", t[:], d, "const32")
    C["I32"] = t[:]
    return C


def consts1_np():
    s = np.arange(128)[:, None]; t = np.arange(128)[None, :]
    c = {"U": s <= t, "L": s >= t, "SU": s < t, "SL": s > t, "I": s == t, "I32": s == t}
    c = {"c_" + k: v.astype(np.float32) for k, v in c.items()}
    c["c_BD32"] = ((s // 32) == (t // 32)).astype(np.float32)
    c["c_OFF64"] = (((s // 64) == (t // 64)) & ((s // 32) != (t // 32))).astype(np.float32)
    c["c_OFF128"] = ((s // 64) != (t // 64)).astype(np.float32)
    c["c_NSU"] = -(s < t).astype(np.float32)
    c["c_NSL"] = -(s > t).astype(np.float32)
    return c


def build_D(do_ret=True, do_gdn=True):
    nc = bass.Bass("TRN2", target_bir_lowering=False)
    S = Sched(nc)
    with ExitStack() as st:
        arena_t = st.enter_context(nc.sbuf_tensor("arena", [128, 44 * 1024], F32))
        C = load_consts1(S, nc, st)
        banks = [st.enter_context(nc.psum_tensor("pb%d" % i, [128, 512], F32))[:] for i in range(8)]
        keys = []
        if do_ret:
            oc = nc.dram_tensor("oc", [T, 128], F32, kind="ExternalOutput").ap()
            keys += ret_part(S, nc, st, Arena(arena_t, 44 * 1024), C, banks, oc)
        if do_gdn:
            od = nc.dram_tensor("od", [T, 128], F32, kind="ExternalOutput").ap()
            keys += gdn_part(S, nc, st, Arena(arena_t, 44 * 1024), C, banks, od)
        S.finish_wait("sp", keys)
        S.build()
    return nc


def mix1_inputs(inp, proj, i):
    h, half = i // 2, i % 2
    A = lambda a: np.ascontiguousarray(a, dtype=np.float32)
    rq = proj[:, h * 128:(h + 1) * 128]; rk = proj[:, 512 + h * 128:512 + (h + 1) * 128]
    rv = proj[:, 1024 + h * 256 + half * 128:1024 + h * 256 + (half + 1) * 128]
    sw = lambda a: np.concatenate([a[:, 64:], a[:, :64]], axis=1)
    pos = np.asarray(inp["positions"])[0][:T]
    inv = (10000.0 ** (-np.arange(64, dtype=np.float32) / 64)).astype(np.float32)
    lgam = np.log(1.0 - np.exp2(-5.0 - np.arange(4, dtype=np.float32))).astype(np.float32)
    m = {"rqT": A(rq.T), "rqTs": A(sw(rq).T), "rkT": A(rk.T), "rkTs": A(sw(rk).T), "rvt": A(rv),
         "rposb": np.ascontiguousarray(np.broadcast_to(pos[None, :], (128, T))).astype(np.int32),
         "rinv2": (np.concatenate([inv, inv])[:, None] / (2 * np.pi)).astype(np.float32),
         "rsgn": np.concatenate([np.ones(64), -np.ones(64)])[:, None].astype(np.float32),
         "rlg": A(np.broadcast_to(np.array([lgam[h], lgam[3 - h]], dtype=np.float32)[None, :], (128, 2)))}
    g = i
    qkv = proj[:, 3072:6144]
    cw = np.asarray(inp["gdn_conv_w"])[0]
    for j in range(3):
        m["gxT%d" % j] = A(qkv[:, j * 1024 + g * 128:j * 1024 + (g + 1) * 128].T)
    m["gcw"] = A(np.concatenate([cw[:, j * 1024 + g * 128:j * 1024 + (g + 1) * 128].T for j in range(3)], axis=1))
    for d in range(2):
        m["gha%d" % d] = A(proj[:, 6144 + d * 8 + g].reshape(NT, 128).T)
        m["ghb%d" % d] = A(proj[:, 6160 + d * 8 + g].reshape(NT, 128).T)
    m["gal"] = A(np.broadcast_to(np.asarray(inp["gdn_a_log"])[0][:, g][None, :], (128, 2)))
    m["gdt"] = A(np.broadcast_to(np.asarray(inp["gdn_dt_bias"])[0][:, g][None, :], (128, 2)))
    m.update(consts1_np())
    return m


def build_C(layer, NC2=7200):
    G1 = 1 if layer == 0 else 2
    norm2 = (layer == 1)
    NH = TOK // HALF
    nc = bass.Bass("TRN2", target_bir_lowering=False)
    din = lambda n, shp: nc.dram_tensor(n, shp, F32, kind="ExternalInput").ap()
    xT = din("xT", [D, TOK]); m1T = din("m1T", [1024, TOK]); g1T = din("g1T", [1024, TOK]); nw1 = din("nw1", [128, 8])
    m2T = din("m2T", [1024, TOK])
    if norm2:
        g2T = din("g2T", [1024, TOK]); nw2 = din("nw2", [128, 8])
    w_out = din("w_out", [D, D]); nfw = din("nfw", [128, 16]); w_up = din("w_up", [D, DFF]); w_down = din("w_down", [DFF, D])
    if layer == 0:
        nmw = din("nmw", [128, 16]); w_in2 = din("w_in2", [D, NC2])
        x2T = nc.dram_tensor("x2T", [D, TOK], F32, kind="ExternalOutput").ap()
        yT = nc.dram_tensor("yT", [NC2, TOK], F32, kind="ExternalOutput").ap()
    else:
        fnw = din("fnw", [128, 16])
        outT = nc.dram_tensor("outT", [D, TOK], F32, kind="ExternalOutput").ap()
    NQ = DFF // 1024
    S = Sched(nc)
    with ExitStack() as st:
        WORDS = 46 * 1024
        arena_t = st.enter_context(nc.sbuf_tensor("arena", [128, WORDS], F32))
        ar = Arena(arena_t, WORDS)
        banks = [st.enter_context(nc.psum_tensor("pb%d" % i, [128, 512], F32))[:] for i in range(8)]
        b_ss = banks[0]
        pmm = banks[1:8]
        x = ar.alloc([128, 16, TOK], F32)
        actb = ar.alloc([128, 16, TOK], BF16)
        aT = ar.alloc([128, 8, TOK], BF16)
        wts = [ar.alloc([128, 16, 512], BF16) for _ in range(2)]
        ots_off = ar.off
        ots = [ar.alloc([128, 4, HALF], F32) for _ in range(2)]
        wts.append(arena_t[:, ots_off:ots_off + 4096].bitcast(BF16).rearrange("p (a b) -> p a b", b=512))
        nslots = [3]
        lds = [ar.alloc([128, 2, HALF], F32) for _ in range(2)]
        tmp = [ar.alloc([128, HALF], F32) for _ in range(3)]
        rrow = ar.alloc([128, TOK], F32)
        sqb = ar.alloc([128, 2, HALF], BF16)
        ones = ar.alloc([128, 128], BF16)
        epsb = ar.alloc([128, 1], F32)
        nw1_s = ar.alloc([128, 8], F32); nw2_s = ar.alloc([128, 8], F32)
        nfw_s = ar.alloc([128, 16], F32); nxw_s = ar.alloc([128, 16], F32)
        S.add("pool", lambda e: e.memset(ones, 1.0), writes=[ones])
        S.add("pool", lambda e: e.memset(epsb, EPS), writes=[epsb])
        S.dma("sp", nw1_s, nw1, "cw")
        if norm2:
            S.dma("sp", nw2_s, nw2, "cw")
        S.dma("sp", nfw_s, nfw, "cw")
        S.dma("sp", nxw_s, nmw if layer == 0 else fnw, "cw")
        cnt = {"w": 0, "p": 0, "o": 0, "l": 0, "t": 0}
        halves = [slice(hf * HALF, (hf + 1) * HALF) for hf in range(NH)]

        def wload(src_ap, shape3):
            i = cnt["w"] % nslots[0]
            cnt["w"] += 1
            a_, b_ = shape3
            flat = wts[i].rearrange("p a b -> p (a b)")[:, 0:a_ * b_].rearrange("p (a b) -> p a b", b=b_)
            S.dma("pool", flat, src_ap, "w%d" % i)
            return flat

        def pbank():
            b_ = pmm[cnt["p"] % len(pmm)]
            cnt["p"] += 1
            return b_

        def rms_rows(src_chunks, nfeat, tsl):
            n = len(src_chunks)
            for i, c in enumerate(src_chunks):
                q = sqb[:, i % 2, :]
                S.add("act", lambda e, c=c, q=q: e.activation(out=q, in_=c, func=AF.Square), reads=[c], writes=[q])
                S.add("pe", lambda e, q=q, i=i: e.matmul(b_ss, lhsT=ones, rhs=q, start=(i == 0), stop=(i == n - 1)),
                      reads=[ones, q], writes=[b_ss])
            S.add("act", lambda e: e.activation(out=rrow[:, tsl], in_=b_ss, func=AF.Sqrt, scale=1.0 / nfeat, bias=epsb),
                  reads=[b_ss, epsb], writes=[rrow[:, tsl]])
            S.add("dve", lambda e: e.reciprocal(out=rrow[:, tsl], in_=rrow[:, tsl]), reads=[rrow[:, tsl]], writes=[rrow[:, tsl]])

        def norm_gate(mT, gT, nw_s, G, c0, tsl):
            for grp in range(8 // G):
                ld = lds[cnt["l"] % 2]
                cnt["l"] += 1
                key = "l%d" % (cnt["l"] % 2)
                ldg = lds[cnt["l"] % 2]
                cnt["l"] += 1
                keyg = "l%d" % (cnt["l"] % 2)
                r0 = grp * G * 128
                S.dma("sp", ld[:, 0:G, :], mT[r0:r0 + G * 128, tsl].rearrange("(c p) t -> p c t", p=128), key)
                S.dma("sp", ldg[:, 0:G, :], gT[r0:r0 + G * 128, tsl].rearrange("(c p) t -> p c t", p=128), keyg)
                rms_rows([ld[:, i, :] for i in range(G)], 128.0 * G, tsl)
                for i in range(G):
                    c = grp * G + i
                    S.add("act", lambda e, ldg=ldg, i=i: e.activation(out=ldg[:, i, :], in_=ldg[:, i, :], func=AF.Silu),
                          reads=[ldg[:, i, :]], writes=[ldg[:, i, :]])
                    S.add("dve", lambda e, ld=ld, i=i: e.tensor_tensor(out=ld[:, i, :], in0=ld[:, i, :], in1=rrow[:, tsl], op=ALU.mult),
                          reads=[ld[:, i, :], rrow[:, tsl]], writes=[ld[:, i, :]])
                    S.add("dve", lambda e, ld=ld, ldg=ldg, i=i, c=c: e.scalar_tensor_tensor(
                        out=actb[:, c0 + c, tsl], in0=ld[:, i, :], scalar=nw_s[:, c:c + 1], in1=ldg[:, i, :], op0=ALU.mult, op1=ALU.mult),
                        reads=[ld[:, i, :], nw_s[:, c:c + 1], ldg[:, i, :]], writes=[actb[:, c0 + c, tsl]])

        def rms_to_act(nw_s):
            for tsl in halves:
                rms_rows([x[:, c, tsl] for c in range(16)], float(D), tsl)
                for c in range(16):
                    S.add("act", lambda e, c=c, tsl=tsl: e.activation(out=actb[:, c, tsl], in_=x[:, c, tsl], func=AF.Copy, scale=nw_s[:, c:c + 1]),
                          reads=[x[:, c, tsl], nw_s[:, c:c + 1]], writes=[actb[:, c, tsl]])

        def mm16(pt, wt, j, m, tsl):
            for c in range(16):
                S.add("pe", lambda e, c=c: e.matmul(pt[0:m, :], lhsT=wt[:, c, j * 128:j * 128 + m], rhs=actb[:, c, tsl], start=(c == 0), stop=(c == 15)),
                      reads=[wt[:, c, j * 128:j * 128 + m], actb[:, c, tsl]], writes=[pt])

        S.dma("sp", x, xT.rearrange("(c p) t -> p c t", p=128), "x")
        for tsl in halves:
            norm_gate(m1T, g1T, nw1_s, G1, 0, tsl)
            if norm2:
                norm_gate(m2T, g2T, nw2_s, 1, 8, tsl)
            else:
                for c2 in range(0, 8, 2):
                    ld = lds[cnt["l"] % 2]
                    cnt["l"] += 1
                    S.dma("sp", ld, m2T[c2 * 128:(c2 + 2) * 128, tsl].rearrange("(c p) t -> p c t", p=128), "l%d" % (cnt["l"] % 2))
                    S.add("pool", lambda e, ld=ld, c2=c2, tsl=tsl: e.tensor_copy(out=actb[:, 8 + c2:10 + c2, tsl], in_=ld),
                          reads=[ld], writes=[actb[:, 8 + c2:10 + c2, tsl]])
        for g in range(4):
            wt = wload(w_out[:, g * 512:(g + 1) * 512].rearrange("(c p) n -> p c n", p=128), (16, 512))
            for tsl in halves:
                for j in range(4):
                    pt = pbank()
                    jj = g * 4 + j
                    mm16(pt, wt, j, 128, tsl)
                    S.add("dve", lambda e, pt=pt, jj=jj, tsl=tsl: e.tensor_tensor(out=x[:, jj, tsl], in0=pt, in1=x[:, jj, tsl], op=ALU.add),
                          reads=[pt, x[:, jj, tsl]], writes=[x[:, jj, tsl]])
        rms_to_act(nfw_s)
        for q in range(NQ):
            for g2 in range(2):
                c0 = q * 1024 + g2 * 512
                wt = wload(w_up[:, c0:c0 + 512].rearrange("(c p) n -> p c n", p=128), (16, 512))
                for tsl in halves:
                    for j in range(4):
                        pt = pbank()
                        f = g2 * 4 + j
                        mm16(pt, wt, j, 128, tsl)
                        t_ = tmp[cnt["t"] % 3]
                        cnt["t"] += 1
                        S.add("act", lambda e, pt=pt, t_=t_: e.activation(out=t_, in_=pt, func=AF.Relu), reads=[pt], writes=[t_])
                        S.add("dve", lambda e, t_=t_, tsl=tsl: e.tensor_tensor(out=t_, in0=t_, in1=rrow[:, tsl], op=ALU.mult), reads=[t_, rrow[:, tsl]], writes=[t_])
                        S.add("pool", lambda e, t_=t_, f=f, tsl=tsl: e.tensor_tensor(out=aT[:, f, tsl], in0=t_, in1=t_, op=ALU.mult), reads=[t_], writes=[aT[:, f, tsl]])
            for ch in range(2):
                wt = wload(w_down[q * 1024:(q + 1) * 1024, ch * 1024:(ch + 1) * 1024].rearrange("(c p) n -> p c n", p=128), (8, 1024))
                for jl in range(8):
                    jj = ch * 8 + jl
                    for tsl in halves:
                        pt = pbank()
                        for f in range(8):
                            S.add("pe", lambda e, pt=pt, wt=wt, f=f, jl=jl, tsl=tsl: e.matmul(pt, lhsT=wt[:, f, jl * 128:(jl + 1) * 128], rhs=aT[:, f, tsl],
                                                                                         start=(f == 0), stop=(f == 7)),
                                  reads=[wt[:, f, jl * 128:(jl + 1) * 128], aT[:, f, tsl]], writes=[pt])
                        S.add("dve", lambda e, pt=pt, jj=jj, tsl=tsl: e.tensor_tensor(out=x[:, jj, tsl], in0=pt, in1=x[:, jj, tsl], op=ALU.add),
                              reads=[pt, x[:, jj, tsl]], writes=[x[:, jj, tsl]])
        nslots[0] = 2
        cnt["w"] = 0
        rms_to_act(nxw_s)
        if layer == 0:
            S.dma("sp", x2T.rearrange("(c p) t -> p c t", p=128), x, "xo")
            ng = (NC2 + 511) // 512
            for g in range(ng):
                c0 = g * 512
                gw = min(512, NC2 - c0)
                wt = wload(w_in2[:, c0:c0 + gw].rearrange("(c p) n -> p c n", p=128), (16, gw))
                nj = (gw + 127) // 128
                for tsl in halves:
                    oi = cnt["o"] % 2
                    cnt["o"] += 1
                    ot = ots[oi]
                    for j in range(nj):
                        m = min(128, gw - j * 128)
                        pt = pbank()
                        mm16(pt, wt, j, m, tsl)
                        S.add("dve", lambda e, pt=pt, ot=ot, j=j, m=m, tsl=tsl: e.tensor_tensor(out=ot[0:m, j, :], in0=pt[0:m, :], in1=rrow[0:m, tsl], op=ALU.mult),
                              reads=[pt, rrow[:, tsl]], writes=[ot[0:m, j, :]])
                    nfull = gw // 128
                    if nfull:
                        S.dma("sp", yT[c0:c0 + nfull * 128, tsl].rearrange("(j p) t -> p j t", p=128), ot[:, 0:nfull, :], "o%d" % oi)
                    rem = gw - nfull * 128
                    if rem:
                        S.dma("sp", yT[c0 + nfull * 128:c0 + gw, tsl], ot[0:rem, nfull, :], "o%d" % oi)
        else:
            for tsl in halves:
                for g in range(4):
                    oi = cnt["o"] % 2
                    cnt["o"] += 1
                    ot = ots[oi]
                    for j in range(4):
                        c = g * 4 + j
                        S.add("dve", lambda e, ot=ot, j=j, c=c, tsl=tsl: e.scalar_tensor_tensor(out=ot[:, j, :], in0=x[:, c, tsl], scalar=nxw_s[:, c:c + 1], in1=rrow[:, tsl],
                                                                                               op0=ALU.mult, op1=ALU.mult),
                              reads=[x[:, c, tsl], nxw_s[:, c:c + 1], rrow[:, tsl]], writes=[ot[:, j, :]])
                    S.dma("sp", outT[g * 512:(g + 1) * 512, tsl].rearrange("(j p) t -> p j t", p=128), ot, "o%d" % oi)
        S.finish_wait("sp", ["o0", "o1"] + (["xo"] if layer == 0 else []))
        S.build()
    return nc


def dense_inputs(inp, layer, c, x, m1, g1, m2, g2=None, tok=1024):
    rs = slice(c * tok, (c + 1) * tok)
    T_ = lambda a: np.ascontiguousarray(a[rs].T)
    nwr = lambda w, n: np.ascontiguousarray(np.asarray(w).reshape(n, 128).T)
    m = {"xT": T_(x), "m1T": T_(m1), "g1T": T_(g1), "m2T": T_(m2)}
    if layer == 0:
        m["nw1"] = nwr(inp["hg_norm_w"][0], 8)
        m["w_out"] = np.ascontiguousarray(inp["even_w_out"][0])
        m["nmw"] = nwr(inp["norm_mix_w"][1], 16)
        m["w_in2"] = np.ascontiguousarray(inp["odd_w_in"][0])
    else:
        m["nw1"] = nwr(inp["ret_norm_w"][0], 8)
        m["g2T"] = T_(g2)
        m["nw2"] = nwr(inp["gdn_norm_w"][0], 8)
        m["w_out"] = np.ascontiguousarray(inp["odd_w_out"][0])
        m["fnw"] = nwr(inp["final_norm_w"], 16)
    m["nfw"] = nwr(inp["norm_ffn_w"][layer], 16)
    m["w_up"] = np.ascontiguousarray(inp["ffn_w_up"][layer])
    m["w_down"] = np.ascontiguousarray(inp["ffn_w_down"][layer])
    return m


def build_A(NC):
    nc = bass.Bass("TRN2", target_bir_lowering=False)
    xT = nc.dram_tensor("xT", [D, TOK], F32, kind="ExternalInput").ap()
    nw = nc.dram_tensor("nw", [128, 16], F32, kind="ExternalInput").ap()
    W = nc.dram_tensor("W", [D, NC], F32, kind="ExternalInput").ap()
    yT = nc.dram_tensor("yT", [NC, TOK], F32, kind="ExternalOutput").ap()
    S = Sched(nc)
    with ExitStack() as st:
        sb = lambda name, shape, dt: st.enter_context(nc.sbuf_tensor(name, shape, dt))
        ps = lambda name: st.enter_context(nc.psum_tensor(name, [128, 512], F32))
        x_sb = sb("x_sb", [128, 16, HALF], F32)
        sq = sb("sq", [128, 2, HALF], BF16)
        hT = sb("hT", [128, 16, HALF], BF16)
        nw_sb = sb("nw_sb", [128, 16], F32)
        ones = sb("ones", [128, 128], BF16)
        rstd = sb("rstd", [128, HALF], F32)
        wts = [sb("wt%d" % i, [128, 16, 512], BF16) for i in range(2)]
        outs = [sb("ot%d" % i, [128, 4, HALF], F32) for i in range(2)]
        pss = [ps("ps%d" % i) for i in range(4)]
        ps_ss = ps("ps_ss")

        eps_sb = sb("eps_sb", [128, 1], F32)
        S.add("pool", lambda e: e.memset(ones[:], 1.0), writes=[ones[:]])
        S.add("pool", lambda e: e.memset(eps_sb[:], EPS), writes=[eps_sb[:]])
        S.dma("sp", nw_sb[:], nw, "nw")
        ngroups = (NC + 511) // 512
        gi = 0
        pi = 0
        for hf in range(TOK // HALF):
            tsl = slice(hf * HALF, (hf + 1) * HALF)
            S.dma("sp", x_sb[:], xT[:, tsl].rearrange("(c p) t -> p c t", p=128), "x")
            for c in range(16):
                sqc = sq[:, c % 2, :]
                S.add("act", lambda e, o=sqc, i=x_sb[:, c, :]: e.activation(out=o, in_=i, func=AF.Square),
                      reads=[x_sb[:, c, :]], writes=[sqc])
                S.add("pe", lambda e, i=sqc, c=c: e.matmul(ps_ss[:], lhsT=ones[:], rhs=i, start=(c == 0), stop=(c == 15)),
                      reads=[ones[:], sqc], writes=[ps_ss[:]])
            S.add("act", lambda e: e.activation(out=rstd[:], in_=ps_ss[:], func=AF.Sqrt, scale=1.0 / D, bias=eps_sb[:]),
                  reads=[ps_ss[:], eps_sb[:]], writes=[rstd[:]])
            S.add("dve", lambda e: e.reciprocal(out=rstd[:], in_=rstd[:]),
                  reads=[rstd[:]], writes=[rstd[:]])
            for c in range(16):
                S.add("act", lambda e, c=c: e.activation(out=hT[:, c, :], in_=x_sb[:, c, :], func=AF.Copy,
                                                         scale=nw_sb[:, c:c + 1]),
                      reads=[x_sb[:, c, :], nw_sb[:, c:c + 1]], writes=[hT[:, c, :]])
            for g in range(ngroups):
                c0 = g * 512
                gw = min(512, NC - c0)
                wt = wts[gi % 2]
                ot = outs[gi % 2]
                gi += 1
                S.dma("pool", wt[:, :, 0:gw], W[:, c0:c0 + gw].rearrange("(c p) n -> p c n", p=128), "w%d" % (gi % 2))
                nj = (gw + 127) // 128
                for j in range(nj):
                    m = min(128, gw - j * 128)
                    pt = pss[pi % 4]
                    pi += 1
                    for c in range(16):
                        S.add("pe", lambda e, pt=pt, wt=wt, c=c, j=j, m=m: e.matmul(
                            pt[0:m, :], lhsT=wt[:, c, j * 128:j * 128 + m], rhs=hT[:, c, :],
                            start=(c == 0), stop=(c == 15)),
                            reads=[wt[:, c, j * 128:j * 128 + m], hT[:, c, :]], writes=[pt[0:m, :]])
                    S.add("dve", lambda e, pt=pt, ot=ot, j=j, m=m: e.tensor_tensor(
                        out=ot[0:m, j, :], in0=pt[0:m, :], in1=rstd[0:m, :], op=ALU.mult),
                        reads=[pt[0:m, :], rstd[0:m, :]], writes=[ot[0:m, j, :]])
                nfull = gw // 128
                if nfull:
                    S.dma("sp", yT[c0:c0 + nfull * 128, tsl].rearrange("(j p) t -> p j t", p=128),
                          ot[:, 0:nfull, :], "o%d" % (gi % 2))
                rem = gw - nfull * 128
                if rem:
                    S.dma("sp", yT[c0 + nfull * 128:c0 + gw, tsl], ot[0:rem, nfull, :], "o%d" % (gi % 2))
        S.finish_wait("sp", ["o0", "o1"])
        S.build()
    return nc


def mla_inputs(z, proj, h):
    c_q = proj[:, 4096 + 1024:4096 + 1024 + 512]
    kv_a = proj[:, 4096 + 1024 + 512:]
    ckv, kpe = kv_a[:, :256], kv_a[:, 256:]
    wq = np.asarray(z["mla_w_q_b"])[0][:, h * 192:(h + 1) * 192]
    wkv = np.asarray(z["mla_w_kv_b"])[0][:, h * 256:(h + 1) * 256]
    pos = np.asarray(z["positions"])[0][:T]
    inv = (10000.0 ** (-np.arange(32, dtype=np.float32) / 32)).astype(np.float32)
    m = {"cqT": np.ascontiguousarray(c_q.T), "ckvT": np.ascontiguousarray(ckv.T), "kpeT": np.ascontiguousarray(kpe.T),
         "kpeTs": np.ascontiguousarray(np.concatenate([kpe[:, 32:], kpe[:, :32]], axis=1).T),
         "posb": np.ascontiguousarray(np.broadcast_to(pos[None, :], (64, T))).astype(np.int32),
         "qnw": np.ascontiguousarray(np.asarray(z["mla_q_norm_w"])[0].reshape(4, 128).T), "kvnw": np.ascontiguousarray(np.asarray(z["mla_kv_norm_w"])[0].reshape(2, 128).T),
         "wqa": np.ascontiguousarray(wq[:, :128]), "wqr": np.ascontiguousarray(wq[:, 128:]),
         "wqrs": np.ascontiguousarray(np.concatenate([wq[:, 160:], wq[:, 128:160]], axis=1)),
         "wkn": np.ascontiguousarray(wkv[:, :128]), "wkv": np.ascontiguousarray(wkv[:, 128:]),
         "inv2": (np.concatenate([inv, inv])[:, None] / (2 * np.pi)).astype(np.float32),
         "sgn": (2 * np.pi * np.concatenate([np.ones(32), -np.ones(32)]))[:, None].astype(np.float32)}
    return m


def _consts0_np():
    s = np.arange(128)[:, None]
    t = np.arange(128)[None, :]
    return {"c_U": (s <= t).astype(np.float32), "c_L": (s >= t).astype(np.float32),
            "c_SU": (s < t).astype(np.float32), "c_SL": (s > t).astype(np.float32)}


def _run(nc, maps):
    return run_bass_kernel_spmd(nc, maps, core_ids=list(range(8))).results


def kernel(**inp):
    inp = {k: np.asarray(v) for k, v in inp.items()}
    x = inp["x"][0]
    Win = np.ascontiguousarray(inp["even_w_in"][0])
    nw = np.ascontiguousarray(inp["norm_mix_w"][0].reshape(16, 128).T)
    resA = _run(build_A(5952), [{"xT": np.ascontiguousarray(x[c * TOK:(c + 1) * TOK].T), "nw": nw, "W": Win} for c in range(8)])
    proj0 = np.concatenate([r["yT"] for r in resA], axis=1).T
    cn = _consts0_np()
    mapsB = []
    for h in range(8):
        sl = slice(h * 128, (h + 1) * 128)
        hf = [proj0[:, 1024:2048][:, sl], proj0[:, 2048:3072][:, sl]]
        lbl = inp["hg_lb_logits"][:, sl]
        m = {"hqT": np.ascontiguousarray(proj0[:, 0:1024][:, sl].T), "hit": np.ascontiguousarray(proj0[:, 3072:4096][:, sl]),
             "lbc": np.ascontiguousarray(lbl.T), "lbr": np.ascontiguousarray(np.broadcast_to(lbl[None], (128, 3, 128)))}
        for d in range(2):
            m["hfT%d" % d] = np.ascontiguousarray(hf[d].T)
            m["hft%d" % d] = np.ascontiguousarray(hf[d])
        m.update(cn)
        m.update(mla_inputs(inp, proj0, h))
        mapsB.append(m)
    resB = _run(build_B(), mapsB)
    o_a = np.concatenate([r["oa"] for r in resB], axis=1)
    o_b = np.concatenate([r["obT"].T for r in resB], axis=1)
    resC = _run(build_C(0), [dense_inputs(inp, 0, c, x, o_a, proj0[:, 4096:5120], o_b) for c in range(8)])
    x2 = np.concatenate([r["x2T"] for r in resC], axis=1).T
    proj1 = np.concatenate([r["yT"] for r in resC], axis=1).T
    resD = _run(build_D(), [mix1_inputs(inp, proj1, i) for i in range(8)])
    o_c = np.concatenate([r["oc"] for r in resD], axis=1)
    o_d = np.concatenate([r["od"] for r in resD], axis=1)
    resE = _run(build_C(1), [dense_inputs(inp, 1, c, x2, o_c, proj1[:, 2048:3072], o_d, proj1[:, 6176:7200]) for c in range(8)])
    out = np.concatenate([r["outT"] for r in resE], axis=1).T
    return np.ascontiguousarray(out[None]).astype(np.float32)
```

```python
import math
import numpy as np
from contextlib import ExitStack
import concourse.bass as bass
import concourse.mybir as mybir
from concourse.bass_utils import run_bass_kernel_spmd

F32 = mybir.dt.float32
BF16 = mybir.dt.bfloat16
I32 = mybir.dt.int32
AF = mybir.ActivationFunctionType
ALU = mybir.AluOpType
AX = mybir.AxisListType

_DT_SIZE = {F32: 4, BF16: 2, I32: 4}


def _region(ap):
    t = ap.tensor
    name = t.name
    esz = _DT_SIZE.get(ap.dtype, 4)
    dims = list(ap.ap)
    cls = type(t).__name__
    if cls.startswith("DRam"):
        ext = 0
        for step, cnt in dims:
            ext += (cnt - 1) * abs(step)
        return (name, 0, 1, ap.offset * esz, (ap.offset + ext + 1) * esz)
    shape = list(t.shape)
    row = 1
    for s in shape[1:]:
        row *= s
    tesz = _DT_SIZE.get(t.dtype, 4)
    rowb = row * tesz
    offb = ap.offset * esz
    p0 = offb // rowb
    f0 = offb % rowb
    pstep, pcnt = dims[0]
    if pstep * esz != rowb:
        pcnt = 1
        fd = dims
    else:
        fd = dims[1:]
    ext = 0
    for step, cnt in fd:
        ext += (cnt - 1) * abs(step)
    if cls.startswith("PSum"):
        return (name, 0, 128, 0, rowb)
    return (name, p0, p0 + pcnt, f0, f0 + (ext + 1) * esz)


def _overlap(a, b):
    return a[1] < b[2] and b[1] < a[2] and a[3] < b[4] and b[3] < a[4]


def _covers(a, b):
    return a[1] <= b[1] and a[2] >= b[2] and a[3] <= b[3] and a[4] >= b[4]


class Op:
    __slots__ = ("eng", "emit", "idx", "deps", "dma_key", "inc", "count", "pe_acc")

    def __init__(self, eng, emit, idx, dma_key=None):
        self.eng = eng
        self.emit = emit
        self.idx = idx
        self.deps = {}
        self.dma_key = dma_key
        self.inc = False
        self.count = 0


ENGS = ("pe", "act", "dve", "pool", "sp")


class Sched:
    def __init__(self, nc):
        self.nc = nc
        self.ops = []
        self.per_eng = {e: [] for e in ENGS}
        self.recs = {}
        self.dma_total = {}
        self.final_waits = []

    def add(self, eng, emit, reads=(), writes=(), dma_key=None):
        op = Op(eng, emit, len(self.ops), dma_key)
        self.ops.append(op)
        self.per_eng[eng].append(op)
        pend = []
        for ap in reads:
            r = _region(ap)
            lst = self.recs.setdefault(r[0], [])
            for rec in lst:
                if rec[2] and _overlap(rec[0], r):
                    self._dep(op, rec[1], raw=True)
            pend.append((lst, r, False))
        for ap in writes:
            r = _region(ap)
            lst = self.recs.setdefault(r[0], [])
            keep = []
            for rec in lst:
                if _overlap(rec[0], r):
                    self._dep(op, rec[1], raw=False)
                    if _covers(r, rec[0]):
                        continue
                keep.append(rec)
            lst[:] = keep
            pend.append((lst, r, True))
        for lst, r, w in pend:
            if not w:
                if op.dma_key is None:
                    lst[:] = [rec for rec in lst if not (not rec[2] and rec[0] == r and rec[1].eng == op.eng
                                                        and rec[1].dma_key is None)]
            lst.append([r, op, w])
        if dma_key is not None:
            self.dma_total[dma_key] = self.dma_total.get(dma_key, 0) + 16
            op.count = self.dma_total[dma_key]
        return op

    def _dep(self, op, prod, raw):
        if prod is op:
            return
        if prod.dma_key is not None:
            key = ("dma", prod.dma_key)
            val = self.dma_total[prod.dma_key]
            op.deps[key] = max(op.deps.get(key, 0), val)
            return
        if prod.eng == op.eng and op.dma_key is None:
            if op.eng == "pe":
                return
        key = ("eng", prod.eng)
        prod.inc = True
        op.deps[key] = max(op.deps.get(key, -1), prod.idx)

    def dma(self, eng, out, in_, key, **kw):
        def emit(e, out=out, in_=in_, kw=kw):
            return e.dma_start(out=out, in_=in_, **kw)
        return self.add(eng, emit, reads=[in_], writes=[out], dma_key=key)

    def finish_wait(self, eng, keys):
        self.final_waits.append((eng, keys))

    def build(self):
        nc = self.nc
        cnt = {e: 0 for e in ENGS}
        for op in self.ops:
            if op.dma_key is None and op.inc:
                cnt[op.eng] += 1
                op.count = cnt[op.eng]
        with ExitStack() as st:
            esem = {e: st.enter_context(nc.semaphore("s_" + e)) for e in ENGS}
            dsem = {k: st.enter_context(nc.semaphore("d_%s" % str(k))) for k in self.dma_total}
            block = st.enter_context(nc.Block())
            ops = self.ops

            def run(engname, eng):
                waited = {}
                for op in self.per_eng[engname]:
                    for key, v in op.deps.items():
                        if key[0] == "dma":
                            sem = dsem[key[1]]
                            val = v
                        else:
                            sem = esem[key[1]]
                            val = ops[v].count
                        if waited.get(key, 0) >= val:
                            continue
                        waited[key] = val
                        eng.wait_ge(sem, val)
                    ins = op.emit(eng)
                    if op.dma_key is not None:
                        ins.then_inc(dsem[op.dma_key], 16)
                    elif op.inc:
                        ins.then_inc(esem[engname], 1)
                for (e, keys) in self.final_waits:
                    if e == engname:
                        for k in keys:
                            eng.wait_ge(dsem[k], self.dma_total[k])

            @block.tensor
            def _(e):
                run("pe", e)

            @block.scalar
            def _(e):
                run("act", e)

            @block.vector
            def _(e):
                run("dve", e)

            @block.gpsimd
            def _(e):
                run("pool", e)

            @block.sync
            def _(e):
                run("sp", e)


T = 8192
NT = T // 128
PIECE = min(2048, T)
NP = T // PIECE
PT = PIECE // 128
EPS = 1e-6
D = 2048
TOK = 1024
HALF = 512
DFF = 8192


class Arena:
    def __init__(self, t, words):
        self.t = t
        self.words = words
        self.off = 0

    def alloc(self, shape, dt, parts=None):
        n = 1
        for s in shape[1:]:
            n *= s
        esz = 2 if dt == BF16 else 4
        w = (n * esz + 3) // 4
        assert self.off + w <= self.words, ("arena overflow", self.off, w, self.words)
        ap = self.t[0:shape[0], self.off:self.off + w]
        self.off += w
        if dt != F32:
            ap = ap.bitcast(dt)
        if len(shape) == 3:
            ap = ap.rearrange("p (a b) -> p a b", b=shape[2])
        return ap


def consts_np():
    s = np.arange(128)[:, None]
    t = np.arange(128)[None, :]
    c = {
        "U": (s <= t), "L": (s >= t), "SU": (s < t), "SL": (s > t),
    }
    return {k: v.astype(np.float32) for k, v in c.items()}


def merge_gens(gens):
    active = list(gens)
    while active:
        for g_ in list(active):
            try:
                next(g_)
            except StopIteration:
                active.remove(g_)


def gla_stream(S, pools, sid, qT, kT, k_tok, g_hl, v_tok, o_acc, rev, C, const_g=False):
    tmp = pools["tmp"][sid]
    bk = pools["banks"][sid]
    St, Sb = pools["St"][sid], pools["Sb"][sid]
    cst = pools["cst"][sid]
    Ucs = C["L"] if rev else C["U"]
    Mst = C["SU"] if rev else C["SL"]
    gB, gC, gS, gO, gU = bk[0][:, 0:128], bk[0][:, 128:256], bk[1][:, 0:128], bk[2][:, 0:128], bk[3][:, 0:128]

    def decays(tb, c):
        for hl in range(2):
            gt = g_hl[hl][:, c, :]
            S.add("pe", lambda e, gt=gt, hl=hl: e.matmul(gB, lhsT=gt, rhs=Ucs, start=(hl == 0), stop=(hl == 1)), reads=[gt, Ucs], writes=[gB])
            yield
        for hl in range(2):
            gt = g_hl[hl][:, c, :]
            S.add("pe", lambda e, gt=gt, hl=hl: e.matmul(gC, lhsT=Mst, rhs=gt, start=(hl == 0), stop=(hl == 1)), reads=[gt, Mst], writes=[gC])
            yield
        S.add("act", lambda e: e.activation(out=tb["eb"], in_=gB, func=AF.Exp), reads=[gB], writes=[tb["eb"]])
        yield
        S.add("act", lambda e: e.activation(out=tb["enb"], in_=gB, func=AF.Exp, scale=-1.0), reads=[gB], writes=[tb["enb"]])
        yield
        S.add("act", lambda e: e.activation(out=tb["ec"], in_=gC, func=AF.Exp), reads=[gC], writes=[tb["ec"]])
        yield

    def init():
        S.add("pool", lambda e: e.memset(St, 0.0), writes=[St])
        S.add("pool", lambda e: e.memset(Sb, 0.0), writes=[Sb])
        if const_g:
            for _ in decays(cst, 0):
                pass

    def prep(i, c):
        tb = tmp[i % 2]
        sl = slice(c * 128, (c + 1) * 128)
        if const_g:
            dk = cst
        else:
            dk = tb
            yield from decays(tb, c)
        S.add("pool", lambda e: e.tensor_tensor(out=tb["qd"], in0=qT[:, sl], in1=dk["eb"], op=ALU.mult), reads=[qT[:, sl], dk["eb"]], writes=[tb["qd"]])
        yield
        S.add("dve", lambda e: e.tensor_tensor(out=tb["kb"], in0=kT[:, sl], in1=dk["enb"], op=ALU.mult), reads=[kT[:, sl], dk["enb"]], writes=[tb["kb"]])
        yield
        S.add("pool", lambda e: e.tensor_tensor(out=tb["kd"], in0=k_tok[:, c, :], in1=dk["ec"], op=ALU.mult), reads=[k_tok[:, c, :], dk["ec"]], writes=[tb["kd"]])
        yield
        S.add("pe", lambda e: e.matmul(gS, lhsT=tb["kb"], rhs=tb["qd"], start=True, stop=True), reads=[tb["kb"], tb["qd"]], writes=[gS])
        yield
        S.add("dve", lambda e: e.tensor_tensor(out=tb["pm"], in0=gS, in1=Ucs, op=ALU.mult), reads=[gS, Ucs], writes=[tb["pm"]])
        yield

    def step(i, c, first):
        tb = tmp[i % 2]
        dk = cst if const_g else tb
        S.add("pe", lambda e: e.matmul(gO, lhsT=tb["pm"], rhs=v_tok[:, c, :], start=True, stop=False), reads=[tb["pm"], v_tok[:, c, :]], writes=[gO])
        yield
        S.add("pe", lambda e: e.matmul(gO, lhsT=tb["qd"], rhs=Sb, start=False, stop=True), reads=[tb["qd"], Sb], writes=[gO])
        yield
        S.add("pe", lambda e: e.matmul(gU, lhsT=tb["kd"], rhs=v_tok[:, c, :], start=True, stop=True), reads=[tb["kd"], v_tok[:, c, :]], writes=[gU])
        yield
        if first:
            S.add("act", lambda e: e.activation(out=o_acc[:, c, :], in_=gO, func=AF.Copy), reads=[gO], writes=[o_acc[:, c, :]])
        else:
            S.add("dve", lambda e: e.tensor_tensor(out=o_acc[:, c, :], in0=gO, in1=o_acc[:, c, :], op=ALU.add),
                  reads=[gO, o_acc[:, c, :]], writes=[o_acc[:, c, :]])
        yield
        dcol = dk["eb"][:, 0:1] if rev else dk["eb"][:, 127:128]
        S.add("dve", lambda e: e.scalar_tensor_tensor(out=St, in0=St, scalar=dcol, in1=gU, op0=ALU.mult, op1=ALU.add), reads=[St, dcol, gU], writes=[St])
        yield
        S.add("act", lambda e: e.activation(out=Sb, in_=St, func=AF.Copy), reads=[St], writes=[Sb])
        yield

    order = list(range(NT - 1, -1, -1)) if rev else list(range(NT))
    return {"init": init, "prep": prep, "step": step, "order": order}


def run_gla(streams, first_fn):
    for st_ in streams:
        st_["init"]()
    merge_gens([st_["prep"](0, st_["order"][0]) for st_ in streams])
    for i in range(NT):
        gens = []
        for k, st_ in enumerate(streams):
            if i + 1 < NT:
                gens.append(st_["prep"](i + 1, st_["order"][i + 1]))
            gens.append(st_["step"](i, st_["order"][i], first_fn(k, i)))
        merge_gens(gens)


def gla_pools(ar, banks, nstreams):
    pools = {"tmp": [], "St": [], "Sb": [], "cst": [], "banks": []}
    for s_ in range(nstreams):
        tmp = []
        for i in range(2):
            tmp.append({
                "eb": ar.alloc([128, 128], F32), "enb": ar.alloc([128, 128], F32), "ec": ar.alloc([128, 128], F32),
                "qd": ar.alloc([128, 128], BF16), "kb": ar.alloc([128, 128], BF16), "kd": ar.alloc([128, 128], BF16),
                "pm": ar.alloc([128, 128], BF16),
            })
        pools["tmp"].append(tmp)
        pools["St"].append(ar.alloc([128, 128], F32))
        pools["Sb"].append(ar.alloc([128, 128], BF16))
        pools["cst"].append({"eb": ar.alloc([128, 128], F32), "enb": ar.alloc([128, 128], F32), "ec": ar.alloc([128, 128], F32)})
        pools["banks"].append(banks[4 * s_:4 * s_ + 4])
    return pools


def load_consts(S, nc, st, names):
    C = {}
    for n in names:
        d = nc.dram_tensor("c_" + n, [128, 128], F32, kind="ExternalInput").ap()
        t = st.enter_context(nc.sbuf_tensor("cs_" + n, [128, 128], BF16))
        S.dma("pool", t[:], d, "const")
        C[n] = t[:]
    return C


def hgrn_part(S, nc, st, ar, C, psA, psB_, oa_out):
    hqT = nc.dram_tensor("hqT", [128, T], F32, kind="ExternalInput").ap()
    hfT = [nc.dram_tensor("hfT%d" % d, [128, T], F32, kind="ExternalInput").ap() for d in range(2)]
    hft = [nc.dram_tensor("hft%d" % d, [T, 128], F32, kind="ExternalInput").ap() for d in range(2)]
    hit = nc.dram_tensor("hit", [T, 128], F32, kind="ExternalInput").ap()
    lbc = nc.dram_tensor("lbc", [128, 3], F32, kind="ExternalInput").ap()
    lbr = nc.dram_tensor("lbr", [128, 3, 128], F32, kind="ExternalInput").ap()

    qT = ar.alloc([128, T], BF16)
    v_tok = ar.alloc([128, NT, 128], BF16)
    o_acc = ar.alloc([128, NT, 128], F32)
    kT = ar.alloc([128, T], BF16)
    k_tok = ar.alloc([128, NT, 128], BF16)
    g_hl = [ar.alloc([128, NT, 128], BF16) for _ in range(2)]
    stg = [ar.alloc([128, PIECE], F32) for _ in range(2)]
    pools = gla_pools(ar, list(psA) + list(psB_), 1)
    lc = ar.alloc([128, 3], F32)
    lr = ar.alloc([128, 3, 128], F32)
    ssum = ar.alloc([128, 1], F32)
    omlc = ar.alloc([128, 1], F32)
    omlr = ar.alloc([128, 128], F32)
    rs = ar.alloc([128, 128], F32)

    S.dma("sp", lc, lbc, "lb")
    S.dma("sp", lr, lbr, "lb")
    S.add("act", lambda e: e.activation(out=lc, in_=lc, func=AF.Exp), reads=[lc], writes=[lc])
    S.add("act", lambda e: e.activation(out=lr, in_=lr, func=AF.Exp), reads=[lr], writes=[lr])
    S.add("dve", lambda e: e.tensor_tensor(out=omlc, in0=lc[:, 1:2], in1=lc[:, 2:3], op=ALU.add),
          reads=[lc], writes=[omlc])
    S.add("dve", lambda e: e.tensor_tensor(out=ssum, in0=omlc, in1=lc[:, 0:1], op=ALU.add), reads=[omlc, lc], writes=[ssum])
    S.add("dve", lambda e: e.reciprocal(out=ssum, in_=ssum), reads=[ssum], writes=[ssum])
    S.add("dve", lambda e: e.tensor_tensor(out=omlc, in0=omlc, in1=ssum, op=ALU.mult), reads=[omlc, ssum], writes=[omlc])
    S.add("dve", lambda e: e.tensor_tensor(out=omlr, in0=lr[:, 1, :], in1=lr[:, 2, :], op=ALU.add), reads=[lr], writes=[omlr])
    S.add("dve", lambda e: e.tensor_tensor(out=rs, in0=omlr, in1=lr[:, 0, :], op=ALU.add), reads=[omlr, lr], writes=[rs])
    S.add("dve", lambda e: e.reciprocal(out=rs, in_=rs), reads=[rs], writes=[rs])
    S.add("dve", lambda e: e.tensor_tensor(out=omlr, in0=omlr, in1=rs, op=ALU.mult), reads=[omlr, rs], writes=[omlr])

    si = [0]

    def stage():
        s_ = stg[si[0] % 2]
        si[0] += 1
        return s_

    for p in range(NP):
        sl = slice(p * PIECE, (p + 1) * PIECE)
        s_ = stage()
        S.dma("sp", s_, hqT[:, sl], "stg%d" % (si[0] % 2))
        S.add("act", lambda e, s_=s_, sl=sl: e.activation(out=qT[:, sl], in_=s_, func=AF.Silu), reads=[s_], writes=[qT[:, sl]])
    for p in range(NP):
        s_ = stage()
        s3 = s_.rearrange("p (a b) -> p a b", b=128)
        S.dma("sp", s3, hit[p * PIECE:(p + 1) * PIECE, :].rearrange("(n p) d -> p n d", p=128), "stg%d" % (si[0] % 2))
        vv = v_tok[:, p * PT:(p + 1) * PT, :]
        S.add("act", lambda e, s3=s3, vv=vv: e.activation(out=vv, in_=s3, func=AF.Copy, scale=128 ** -0.5),
              reads=[s3], writes=[vv])
    for d in range(2):
        for p in range(NP):
            sl = slice(p * PIECE, (p + 1) * PIECE)
            s_ = stage()
            S.dma("sp", s_, hfT[d][:, sl], "stg%d" % (si[0] % 2))
            S.add("act", lambda e, s_=s_: e.activation(out=s_, in_=s_, func=AF.Sigmoid, scale=-1.0), reads=[s_], writes=[s_])
            S.add("dve", lambda e, s_=s_, sl=sl: e.tensor_scalar(out=kT[:, sl], in0=s_, scalar1=omlc, scalar2=None, op0=ALU.mult),
                  reads=[s_, omlc], writes=[kT[:, sl]])
        for p in range(NP):
            s_ = stage()
            s3 = s_.rearrange("p (a b) -> p a b", b=128)
            S.dma("sp", s3, hft[d][p * PIECE:(p + 1) * PIECE, :].rearrange("(n p) d -> p n d", p=128), "stg%d" % (si[0] % 2))
            S.add("act", lambda e, s3=s3: e.activation(out=s3, in_=s3, func=AF.Sigmoid, scale=-1.0), reads=[s3], writes=[s3])
            ob = omlr.unsqueeze(1).broadcast_to([128, PT, 128])
            S.add("dve", lambda e, s3=s3, ob=ob: e.tensor_tensor(out=s3, in0=s3, in1=ob, op=ALU.mult), reads=[s3, omlr], writes=[s3])
            kk = k_tok[:, p * PT:(p + 1) * PT, :]
            gh = g_hl[0][:, p * PT:(p + 1) * PT, :]
            gl = g_hl[1][:, p * PT:(p + 1) * PT, :]
            S.add("pool", lambda e, s3=s3, kk=kk: e.tensor_copy(out=kk, in_=s3), reads=[s3], writes=[kk])
            S.add("act", lambda e, s3=s3: e.activation(out=s3, in_=s3, func=AF.Ln, scale=-1.0, bias=1.0), reads=[s3], writes=[s3])
            S.add("pool", lambda e, s3=s3, gh=gh: e.tensor_copy(out=gh, in_=s3), reads=[s3], writes=[gh])
            S.add("dve", lambda e, s3=s3, gh=gh, gl=gl: e.tensor_tensor(out=gl, in0=s3, in1=gh, op=ALU.subtract), reads=[s3, gh], writes=[gl])
        run_gla([gla_stream(S, pools, 0, qT, kT, k_tok, g_hl, v_tok, o_acc, d == 1, C)], lambda k, i, d=d: d == 0)
    for p in range(NP):
        S.dma("sp", oa_out[p * PIECE:(p + 1) * PIECE, :].rearrange("(n p) d -> p n d", p=128), o_acc[:, p * PT:(p + 1) * PT, :], "oa")


BLK = 512
SCALE = 192 ** -0.5
PI = math.pi


def mla_part(S, nc, st, ar, banks, obT):
    NB = T // BLK
    din = lambda n, shp, dt=F32: nc.dram_tensor(n, shp, dt, kind="ExternalInput").ap()
    cqT = din("cqT", [512, T]); ckvT = din("ckvT", [256, T]); kpeT = din("kpeT", [64, T]); kpeTs = din("kpeTs", [64, T])
    posb = din("posb", [64, T], I32)
    qnw = din("qnw", [128, 4]); kvnw = din("kvnw", [128, 2])
    wqa = din("wqa", [512, 128]); wqr = din("wqr", [512, 64]); wqrs = din("wqrs", [512, 64])
    wkn = din("wkn", [256, 128]); wkv = din("wkv", [256, 128])
    inv2 = din("inv2", [64, 1]); sgn = din("sgn", [64, 1])

    QA = ar.alloc([128, T], BF16); QB = ar.alloc([128, T], BF16)
    KA = ar.alloc([128, T], BF16); KB = ar.alloc([128, T], BF16)
    v_tok = ar.alloc([128, NT, 128], BF16)
    ctxs = []
    for i in range(2):
        ctxs.append({
            "i": i, "stg": ar.alloc([128, 4, BLK], F32), "sq": ar.alloc([128, 4, BLK], BF16), "hh": ar.alloc([128, 4, BLK], BF16),
            "rrow": ar.alloc([128, BLK], F32), "tm": [ar.alloc([128, BLK], F32) for _ in range(3)],
            "C2": ar.alloc([64, BLK], F32), "S2": ar.alloc([64, BLK], F32), "posi": ar.alloc([64, BLK], I32), "ang": ar.alloc([64, BLK], F32),
            "kpe": [ar.alloc([64, BLK], F32) for _ in range(2)], "rcol": ar.alloc([128, 1], F32), "kmx": ar.alloc([128, 1], F32),
            "banks": banks[4 * i:4 * i + 4],
        })
    osb = [ar.alloc([128, BLK], F32) for _ in range(4)]
    accs = [ar.alloc([128, BLK], F32) for _ in range(2)]
    acc_h = ar.alloc([128, BLK], BF16); acc_l = ar.alloc([128, BLK], BF16)
    pT = [ar.alloc([128, BLK], BF16) for _ in range(3)]
    wqa_b = ar.alloc([128, 4, 128], BF16); wqr_b = ar.alloc([128, 4, 64], BF16); wqrs_b = ar.alloc([128, 4, 64], BF16)
    wkn_b = ar.alloc([128, 2, 128], BF16); wkv_b = ar.alloc([128, 2, 128], BF16)
    qnw_s = ar.alloc([128, 4], F32); kvnw_s = ar.alloc([128, 2], F32)
    inv_s = ar.alloc([64, 1], F32); sgn_s = ar.alloc([64, 1], F32)
    ones = ar.alloc([128, 128], BF16)
    kmax = ar.alloc([128, 1], F32)
    epsq = ar.alloc([128, 1], F32)
    b_s0, b_s1, b_o, b_sum = banks[4:8]

    S.add("pool", lambda e: e.memset(ones, 1.0), writes=[ones])
    S.add("pool", lambda e: e.memset(kmax, 0.0), writes=[kmax])
    S.add("pool", lambda e: e.memset(epsq, EPS), writes=[epsq])
    S.add("pool", lambda e: e.memset(KB[64:65, :], 1.0), writes=[KB[64:65, :]])
    for dst, src, k in [(wqa_b, wqa, 128), (wqr_b, wqr, 64), (wqrs_b, wqrs, 64), (wkn_b, wkn, 128), (wkv_b, wkv, 128)]:
        S.dma("pool", dst, src.rearrange("(c p) n -> p c n", p=128), "mw")
    S.dma("sp", qnw_s, qnw, "mc"); S.dma("sp", kvnw_s, kvnw, "mc")
    S.dma("sp", inv_s, inv2, "mc"); S.dma("sp", sgn_s, sgn, "mc")

    def norm_in(cx, src, nch, dfeat, nw_s, b):
        sl = slice(b * BLK, (b + 1) * BLK)
        s_, sq, hh, rrow, b_ss = cx["stg"], cx["sq"], cx["hh"], cx["rrow"], cx["banks"][0]
        S.dma("sp", s_[:, 0:nch, :], src[:, sl].rearrange("(c p) t -> p c t", p=128), "ms%d" % cx["i"])
        yield
        for c in range(nch):
            S.add("act", lambda e, c=c: e.activation(out=sq[:, c, :], in_=s_[:, c, :], func=AF.Square), reads=[s_[:, c, :]], writes=[sq[:, c, :]])
            yield
            S.add("pe", lambda e, c=c: e.matmul(b_ss, lhsT=ones, rhs=sq[:, c, :], start=(c == 0), stop=(c == nch - 1)), reads=[ones, sq[:, c, :]], writes=[b_ss])
            yield
            S.add("act", lambda e, c=c: e.activation(out=hh[:, c, :], in_=s_[:, c, :], func=AF.Copy, scale=nw_s[:, c:c + 1]),
                  reads=[s_[:, c, :], nw_s[:, c:c + 1]], writes=[hh[:, c, :]])
            yield
        S.add("act", lambda e: e.activation(out=rrow, in_=b_ss, func=AF.Sqrt, scale=1.0 / dfeat, bias=epsq), reads=[b_ss, epsq], writes=[rrow])
        yield
        S.add("dve", lambda e: e.reciprocal(out=rrow, in_=rrow), reads=[rrow], writes=[rrow])
        yield

    def rope_tables(cx, b):
        sl = slice(b * BLK, (b + 1) * BLK)
        posi, ang, C2, S2 = cx["posi"], cx["ang"], cx["C2"], cx["S2"]
        S.dma("sp", posi, posb[:, sl], "mpos%d" % cx["i"])
        yield
        ops = [
            ("dve", lambda e: e.tensor_copy(out=ang, in_=posi), [posi], [ang]),
            ("dve", lambda e: e.tensor_scalar(out=ang, in0=ang, scalar1=inv_s, scalar2=None, op0=ALU.mult), [ang, inv_s], [ang]),
            ("dve", lambda e: e.tensor_copy(out=posi, in_=ang), [ang], [posi]),
            ("dve", lambda e: e.tensor_copy(out=S2, in_=posi), [posi], [S2]),
            ("dve", lambda e: e.tensor_tensor(out=ang, in0=ang, in1=S2, op=ALU.subtract), [ang, S2], [ang]),
            ("dve", lambda e: e.tensor_scalar(out=C2, in0=ang, scalar1=0.25, scalar2=None, op0=ALU.add), [ang], [C2]),
            ("dve", lambda e: e.scalar_tensor_tensor(out=S2, in0=ang, scalar=0.5, in1=ang, op0=ALU.is_ge, op1=ALU.subtract), [ang], [S2]),
            ("dve", lambda e: e.scalar_tensor_tensor(out=ang, in0=C2, scalar=0.5, in1=C2, op0=ALU.is_ge, op1=ALU.subtract), [C2], [ang]),
            ("act", lambda e: e.activation(out=S2, in_=S2, func=AF.Sin, scale=sgn_s), [S2, sgn_s], [S2]),
            ("act", lambda e: e.activation(out=C2, in_=ang, func=AF.Sin, scale=2 * PI), [ang], [C2]),
        ]
        for eng, fn, r, w in ops:
            S.add(eng, fn, reads=r, writes=w)
            yield

    def rope_apply(cx, x, xs, out):
        C2, S2 = cx["C2"], cx["S2"]
        S.add("dve", lambda e: e.tensor_tensor(out=x, in0=x, in1=C2, op=ALU.mult), reads=[x, C2], writes=[x])
        yield
        S.add("dve", lambda e: e.tensor_tensor(out=xs, in0=xs, in1=S2, op=ALU.mult), reads=[xs, S2], writes=[xs])
        yield
        S.add("dve", lambda e: e.tensor_tensor(out=out, in0=xs, in1=x, op=ALU.subtract), reads=[x, xs], writes=[out])
        yield

    def sumsq(cx, A_, B_, sl):
        sq, b_ss = cx["sq"], cx["banks"][0]
        S.add("act", lambda e: e.activation(out=sq[:, 0, :], in_=A_[:, sl], func=AF.Square), reads=[A_[:, sl]], writes=[sq[:, 0, :]])
        yield
        S.add("act", lambda e: e.activation(out=sq[0:64, 1, :], in_=B_[0:64, sl], func=AF.Square), reads=[B_[0:64, sl]], writes=[sq[0:64, 1, :]])
        yield
        S.add("pe", lambda e: e.matmul(b_ss, lhsT=ones, rhs=sq[:, 0, :], start=True, stop=False), reads=[ones, sq[:, 0, :]], writes=[b_ss])
        yield
        S.add("pe", lambda e: e.matmul(b_ss, lhsT=ones[0:64, :], rhs=sq[0:64, 1, :], start=False, stop=True), reads=[ones[0:64, :], sq[0:64, 1, :]], writes=[b_ss])
        yield

    def k_block(cx, b):
        sl = slice(b * BLK, (b + 1) * BLK)
        b_ss, b_main, b_r, b_rs = cx["banks"]
        sq, hh, rrow, rcol = cx["sq"], cx["hh"], cx["rrow"], cx["rcol"]
        yield from norm_in(cx, ckvT, 2, 256.0, kvnw_s, b)
        for c in range(2):
            S.add("pe", lambda e, c=c: e.matmul(b_main, lhsT=wkn_b[:, c, :], rhs=hh[:, c, :], start=(c == 0), stop=(c == 1)), reads=[wkn_b[:, c, :], hh[:, c, :]], writes=[b_main])
            yield
        S.add("dve", lambda e: e.tensor_tensor(out=KA[:, sl], in0=b_main, in1=rrow, op=ALU.mult), reads=[b_main, rrow], writes=[KA[:, sl]])
        yield
        for j in range(4):
            tj = slice(j * 128, (j + 1) * 128)
            for c in range(2):
                S.add("pe", lambda e, c=c, tj=tj: e.matmul(b_r[:, 0:128], lhsT=hh[:, c, tj], rhs=wkv_b[:, c, :], start=(c == 0), stop=(c == 1)), reads=[hh[:, c, tj], wkv_b[:, c, :]], writes=[b_r])
                yield
            for c in range(2):
                S.add("pe", lambda e, c=c, tj=tj: e.matmul(b_rs[:, 0:1], lhsT=sq[:, c, tj], rhs=ones[:, 0:1], start=(c == 0), stop=(c == 1)), reads=[sq[:, c, tj], ones[:, 0:1]], writes=[b_rs])
                yield
            S.add("act", lambda e: e.activation(out=rcol, in_=b_rs[:, 0:1], func=AF.Sqrt, scale=1.0 / 256, bias=epsq), reads=[b_rs, epsq], writes=[rcol])
            yield
            S.add("dve", lambda e: e.reciprocal(out=rcol, in_=rcol), reads=[rcol], writes=[rcol])
            yield
            vt = v_tok[:, b * 4 + j, :]
            S.add("act", lambda e, vt=vt: e.activation(out=vt, in_=b_r[:, 0:128], func=AF.Copy, scale=rcol), reads=[b_r, rcol], writes=[vt])
            yield
        yield from rope_tables(cx, b)
        x, xs = cx["kpe"]
        S.dma("sp", x, kpeT[:, sl], "mk%d" % cx["i"])
        S.dma("sp", xs, kpeTs[:, sl], "mk%d" % cx["i"])
        yield
        yield from rope_apply(cx, x, xs, KB[0:64, sl])
        yield from sumsq(cx, KA, KB, sl)
        S.add("dve", lambda e: e.reduce_max(out=cx["kmx"], in_=b_ss, axis=AX.X), reads=[b_ss], writes=[cx["kmx"]])
        yield
        S.add("dve", lambda e: e.tensor_tensor(out=kmax, in0=kmax, in1=cx["kmx"], op=ALU.max), reads=[kmax, cx["kmx"]], writes=[kmax])
        yield

    def q_block(cx, b):
        sl = slice(b * BLK, (b + 1) * BLK)
        b_ss, b_main, b_r, b_rs = cx["banks"]
        hh, rrow, tm = cx["hh"], cx["rrow"], cx["tm"]
        yield from norm_in(cx, cqT, 4, 512.0, qnw_s, b)
        for c in range(4):
            S.add("pe", lambda e, c=c: e.matmul(b_main, lhsT=wqa_b[:, c, :], rhs=hh[:, c, :], start=(c == 0), stop=(c == 3)), reads=[wqa_b[:, c, :], hh[:, c, :]], writes=[b_main])
            yield
        S.add("dve", lambda e: e.scalar_tensor_tensor(out=QA[:, sl], in0=b_main, scalar=SCALE, in1=rrow, op0=ALU.mult, op1=ALU.mult), reads=[b_main, rrow], writes=[QA[:, sl]])
        yield
        for (wb, bank, dst) in [(wqr_b, b_r, tm[0]), (wqrs_b, b_rs, tm[1])]:
            for c in range(4):
                S.add("pe", lambda e, c=c, wb=wb, bank=bank: e.matmul(bank[0:64, :], lhsT=wb[:, c, :], rhs=hh[:, c, :], start=(c == 0), stop=(c == 3)), reads=[wb[:, c, :], hh[:, c, :]], writes=[bank])
                yield
            S.add("dve", lambda e, bank=bank, dst=dst: e.scalar_tensor_tensor(out=dst[0:64, :], in0=bank[0:64, :], scalar=SCALE, in1=rrow[0:64, :], op0=ALU.mult, op1=ALU.mult),
                  reads=[bank, rrow], writes=[dst[0:64, :]])
            yield
        yield from rope_tables(cx, b)
        yield from rope_apply(cx, tm[0][0:64, :], tm[1][0:64, :], QB[0:64, sl])
        yield from sumsq(cx, QA, QB, sl)
        S.add("act", lambda e: e.activation(out=tm[2], in_=b_ss, func=AF.Sqrt, scale=kmax), reads=[b_ss, kmax], writes=[tm[2]])
        yield
        S.add("dve", lambda e: e.tensor_scalar(out=QB[64:65, sl], in0=tm[2][64:65, :], scalar1=-1.0, scalar2=None, op0=ALU.mult), reads=[tm[2][64:65, :]], writes=[QB[64:65, sl]])
        yield

    for b in range(0, NB, 2):
        merge_gens([k_block(ctxs[i], b + i) for i in range(2) if b + i < NB])

    def attention(qb):
        qs = slice(qb * BLK, (qb + 1) * BLK)
        acc = accs[qb % 2]

        def qk(i):
            ks = slice(i * 128, (i + 1) * 128)
            bs = b_s0 if i % 2 == 0 else b_s1
            S.add("pe", lambda e: e.matmul(bs, lhsT=KA[:, ks], rhs=QA[:, qs], start=True, stop=False), reads=[KA[:, ks], QA[:, qs]], writes=[bs])
            S.add("pe", lambda e: e.matmul(bs, lhsT=KB[0:65, ks], rhs=QB[0:65, qs], start=False, stop=True), reads=[KB[0:65, ks], QB[0:65, qs]], writes=[bs])

        qk(0)
        yield
        for kt in range(NT):
            bs = b_s0 if kt % 2 == 0 else b_s1
            p_ = pT[kt % 3]
            S.add("act", lambda e, bs=bs, p_=p_: e.activation(out=p_, in_=bs, func=AF.Exp), reads=[bs], writes=[p_])
            if kt + 1 < NT:
                qk(kt + 1)
            S.add("pe", lambda e, kt=kt, p_=p_: e.matmul(b_o, lhsT=v_tok[:, kt, :], rhs=p_, start=(kt == 0), stop=(kt == NT - 1)), reads=[v_tok[:, kt, :], p_], writes=[b_o])
            if kt == 0:
                S.add("dve", lambda e, p_=p_: e.tensor_copy(out=acc, in_=p_), reads=[p_], writes=[acc])
            else:
                S.add("dve", lambda e, p_=p_: e.tensor_tensor(out=acc, in0=p_, in1=acc, op=ALU.add), reads=[p_, acc], writes=[acc])
            yield
        S.add("act", lambda e: e.activation(out=acc_h, in_=acc, func=AF.Copy), reads=[acc], writes=[acc_h])
        S.add("dve", lambda e: e.tensor_tensor(out=acc_l, in0=acc, in1=acc_h, op=ALU.subtract), reads=[acc, acc_h], writes=[acc_l])
        S.add("pe", lambda e: e.matmul(b_sum, lhsT=ones, rhs=acc_h, start=True, stop=False), reads=[ones, acc_h], writes=[b_sum])
        S.add("pe", lambda e: e.matmul(b_sum, lhsT=ones, rhs=acc_l, start=False, stop=True), reads=[ones, acc_l], writes=[b_sum])
        rs_ = osb[qb % 2]
        ob_ = osb[2 + qb % 2]
        S.add("dve", lambda e: e.reciprocal(out=rs_, in_=b_sum), reads=[b_sum], writes=[rs_])
        S.add("dve", lambda e: e.tensor_tensor(out=ob_, in0=b_o, in1=rs_, op=ALU.mult), reads=[b_o, rs_], writes=[ob_])
        S.dma("sp", obT[:, qs], ob_, "mo%d" % (qb % 2))
        yield

    for _ in q_block(ctxs[0], 0):
        pass
    for qb in range(NB):
        gens = [attention(qb)]
        if qb + 1 < NB:
            gens.append(q_block(ctxs[0], qb + 1))
        merge_gens(gens)
    return ["mo0", "mo1"] if NB > 1 else ["mo0"]


def build_B():
    nc = bass.Bass("TRN2", target_bir_lowering=False)
    oa = nc.dram_tensor("oa", [T, 128], F32, kind="ExternalOutput").ap()
    obT = nc.dram_tensor("obT", [128, T], F32, kind="ExternalOutput").ap()
    S = Sched(nc)
    with ExitStack() as st:
        arena_t = st.enter_context(nc.sbuf_tensor("arena", [128, 44 * 1024], F32))
        C = load_consts(S, nc, st, ["U", "L", "SU", "SL"])
        banks = [st.enter_context(nc.psum_tensor("pb%d" % i, [128, 512], F32))[:] for i in range(8)]
        hgrn_part(S, nc, st, Arena(arena_t, 44 * 1024), C, banks[0:3], banks[3:5], oa)
        keys = mla_part(S, nc, st, Arena(arena_t, 44 * 1024), banks, obT)
        S.finish_wait("sp", ["oa"] + keys)
        S.build()
    return nc

def rope_tables_fm(S, nrows, posb, posi, ang, C2, S2, inv_s, sgn_s, b, key):
    sl = slice(b * BLK, (b + 1) * BLK)
    S.dma("sp", posi, posb[0:nrows, sl], key)
    S.add("dve", lambda e: e.tensor_copy(out=ang, in_=posi), reads=[posi], writes=[ang])
    S.add("dve", lambda e: e.tensor_scalar(out=ang, in0=ang, scalar1=inv_s, scalar2=None, op0=ALU.mult), reads=[ang, inv_s], writes=[ang])
    S.add("dve", lambda e: e.tensor_copy(out=posi, in_=ang), reads=[ang], writes=[posi])
    S.add("dve", lambda e: e.tensor_copy(out=S2, in_=posi), reads=[posi], writes=[S2])
    S.add("dve", lambda e: e.tensor_tensor(out=ang, in0=ang, in1=S2, op=ALU.subtract), reads=[ang, S2], writes=[ang])
    S.add("dve", lambda e: e.tensor_scalar(out=C2, in0=ang, scalar1=0.25, scalar2=None, op0=ALU.add), reads=[ang], writes=[C2])
    S.add("dve", lambda e: e.scalar_tensor_tensor(out=S2, in0=ang, scalar=0.5, in1=ang, op0=ALU.is_ge, op1=ALU.subtract), reads=[ang], writes=[S2])
    S.add("dve", lambda e: e.scalar_tensor_tensor(out=ang, in0=C2, scalar=0.5, in1=C2, op0=ALU.is_ge, op1=ALU.subtract), reads=[C2], writes=[ang])
    S.add("act", lambda e: e.activation(out=S2, in_=S2, func=AF.Sin, scale=2 * PI), reads=[S2], writes=[S2])
    S.add("act", lambda e: e.activation(out=C2, in_=ang, func=AF.Sin, scale=2 * PI), reads=[ang], writes=[C2])
    S.add("dve", lambda e: e.tensor_scalar(out=S2, in0=S2, scalar1=sgn_s, scalar2=None, op0=ALU.mult), reads=[S2, sgn_s], writes=[S2])
    S.add("dve", lambda e: e.tensor_scalar(out=C2, in0=C2, scalar1=-1.0, scalar2=None, op0=ALU.mult), reads=[C2], writes=[C2])


def transpose_to_tok(S, srcT, dst_tok, ident, bank, c, eng="act"):
    pt = bank[:, 0:64].bitcast(BF16)
    sl = slice(c * 128, (c + 1) * 128)
    S.add("pe", lambda e: e.transpose(pt, srcT[:, sl], ident), reads=[srcT[:, sl], ident], writes=[pt])
    if eng == "act":
        S.add("act", lambda e: e.activation(out=dst_tok[:, c, :], in_=pt, func=AF.Copy), reads=[pt], writes=[dst_tok[:, c, :]])
    else:
        S.add("dve", lambda e: e.tensor_copy(out=dst_tok[:, c, :], in_=pt), reads=[pt], writes=[dst_tok[:, c, :]])


def ret_part(S, nc, st, ar, C, banks, oc_out):
    NB = T // BLK
    din = lambda n, shp, dt=F32: nc.dram_tensor(n, shp, dt, kind="ExternalInput").ap()
    rqT = din("rqT", [128, T]); rqTs = din("rqTs", [128, T]); rkT = din("rkT", [128, T]); rkTs = din("rkTs", [128, T])
    rvt = din("rvt", [T, 128]); posb = din("rposb", [128, T], I32)
    inv2 = din("rinv2", [128, 1]); sgn = din("rsgn", [128, 1]); lg = din("rlg", [128, 2])
    qT = ar.alloc([128, T], BF16); kT = ar.alloc([128, T], BF16)
    k_tok = ar.alloc([128, NT, 128], BF16); v_tok = ar.alloc([128, NT, 128], BF16)
    g_hls = [[ar.alloc([128, 1, 128], BF16) for _ in range(2)] for _ in range(2)]
    o_acc = ar.alloc([128, NT, 128], F32)
    pools = gla_pools(ar, banks, 2)
    C2 = ar.alloc([128, BLK], F32); S2 = ar.alloc([128, BLK], F32)
    posi = ar.alloc([128, BLK], I32); ang = ar.alloc([128, BLK], F32)
    xs = [ar.alloc([128, BLK], F32) for _ in range(4)]
    stg = ar.alloc([128, PT, 128], F32)
    inv_s = ar.alloc([128, 1], F32); sgn_s = ar.alloc([128, 1], F32); lg_s = ar.alloc([128, 2], F32)
    S.dma("sp", inv_s, inv2, "rc"); S.dma("sp", sgn_s, sgn, "rc"); S.dma("sp", lg_s, lg, "rc")
    for b in range(NB):
        sl = slice(b * BLK, (b + 1) * BLK)
        rope_tables_fm(S, 128, posb, posi, ang, C2, S2, inv_s, sgn_s, b, "rpos")
        for i, src in enumerate([rqT, rqTs, rkT, rkTs]):
            S.dma("sp", xs[i], src[:, sl], "rx%d" % i)
        for (x, xsw, dst) in [(xs[0], xs[1], qT), (xs[2], xs[3], kT)]:
            S.add("dve", lambda e, x=x: e.tensor_tensor(out=x, in0=x, in1=C2, op=ALU.mult), reads=[x, C2], writes=[x])
            S.add("dve", lambda e, xsw=xsw: e.tensor_tensor(out=xsw, in0=xsw, in1=S2, op=ALU.mult), reads=[xsw, S2], writes=[xsw])
            S.add("dve", lambda e, x=x, xsw=xsw, dst=dst, sl=sl: e.tensor_tensor(out=dst[:, sl], in0=x, in1=xsw, op=ALU.add),
                  reads=[x, xsw], writes=[dst[:, sl]])
        for j in range(BLK // 128):
            transpose_to_tok(S, kT, k_tok, C["I"], banks[5 + (j % 2)], b * (BLK // 128) + j, eng="act" if j % 2 == 0 else "dve")
    for p in range(NP):
        S.dma("sp", stg, rvt[p * PIECE:(p + 1) * PIECE, :].rearrange("(n p) d -> p n d", p=128), "rv")
        vv = v_tok[:, p * PT:(p + 1) * PT, :]
        S.add("act", lambda e, vv=vv: e.activation(out=vv, in_=stg, func=AF.Copy, scale=128 ** -0.5), reads=[stg], writes=[vv])
    streams = []
    for d in range(2):
        gcol = lg_s[:, d:d + 1]
        gh = g_hls[d][0].rearrange("p a b -> p (a b)"); gl = g_hls[d][1].rearrange("p a b -> p (a b)")
        S.add("dve", lambda e, gcol=gcol, gh=gh: e.tensor_copy(out=gh, in_=gcol.broadcast_to([128, 128])), reads=[gcol], writes=[gh])
        S.add("dve", lambda e, gcol=gcol, gh=gh, gl=gl: e.scalar_tensor_tensor(out=gl, in0=gh, scalar=-1.0, in1=gcol.broadcast_to([128, 128]), op0=ALU.mult, op1=ALU.add),
              reads=[gh, gcol], writes=[gl])
        streams.append(gla_stream(S, pools, d, qT, kT, k_tok, g_hls[d], v_tok, o_acc, d == 1, C, const_g=True))
    run_gla(streams, lambda k, i: i < NT // 2)
    for p in range(NP):
        S.dma("sp", oc_out[p * PIECE:(p + 1) * PIECE, :].rearrange("(n p) d -> p n d", p=128), o_acc[:, p * PT:(p + 1) * PT, :], "oc")
    return ["oc"]


def gdn_part(S, nc, st, ar, C, banks, od_out):
    NB = T // BLK
    din = lambda n, shp, dt=F32: nc.dram_tensor(n, shp, dt, kind="ExternalInput").ap()
    gxT = [din("gxT%d" % i, [128, T]) for i in range(3)]
    gcw = din("gcw", [128, 15])
    gha = [din("gha%d" % d, [128, NT]) for d in range(2)]
    ghb = [din("ghb%d" % d, [128, NT]) for d in range(2)]
    gal = din("gal", [128, 2]); gdt = din("gdt", [128, 2])
    b_g, b_kk, b_qk, b_t, b_x, b_p, b_v, b_o = banks

    qT = ar.alloc([128, T], BF16); kT = ar.alloc([128, T], BF16)
    k_tok = ar.alloc([128, NT, 128], BF16); v_tok = ar.alloc([128, NT, 128], BF16)
    o_acc = ar.alloc([128, NT, 128], F32)
    la = [ar.alloc([128, NT], F32) for _ in range(2)]; beta = [ar.alloc([128, NT], F32) for _ in range(2)]
    la_hl = [[ar.alloc([128, NT], BF16) for _ in range(2)] for _ in range(2)]
    cw_s = ar.alloc([128, 15], F32); al_s = ar.alloc([128, 2], F32); dt_s = ar.alloc([128, 2], F32)
    xb = ar.alloc([128, BLK + 4], F32); acc = ar.alloc([128, BLK], F32); sqb = ar.alloc([128, BLK], BF16)
    vTb = ar.alloc([128, BLK], BF16); rrow = ar.alloc([128, BLK], F32)
    ones = ar.alloc([128, 128], BF16); epsb = ar.alloc([128, 1], F32)
    Sts = [ar.alloc([128, 128], F32) for _ in range(2)]; Sbs = [ar.alloc([128, 128], BF16) for _ in range(2)]
    Sls = [ar.alloc([128, 128], BF16) for _ in range(2)]
    tbs = [[], []]
    for i in range(4):
        d_ = {}
        for n in ["tA", "tB", "Eg", "X32", "T32"]:
            d_[n] = ar.alloc([128, 128], F32)
        for n in ["lab0", "lab1", "P0", "P1", "PT0", "PT1", "Xb", "QKm", "Rv", "Rw", "nwT", "nwTl", "qd", "kd", "vn", "Tb", "Y", "Yp", "No64", "No64T", "No128", "No128T"]:
            d_[n] = ar.alloc([128, 128], BF16)
        for n in ["gc", "gl", "bw", "kds"]:
            d_[n] = ar.alloc([128, 1], F32)
        tbs[i // 2].append(d_)

    S.add("pool", lambda e: e.memset(ones, 1.0), writes=[ones])
    S.add("pool", lambda e: e.memset(epsb, EPS), writes=[epsb])
    S.dma("sp", cw_s, gcw, "gc"); S.dma("sp", al_s, gal, "gc"); S.dma("sp", dt_s, gdt, "gc")
    S.add("act", lambda e: e.activation(out=al_s, in_=al_s, func=AF.Exp), reads=[al_s], writes=[al_s])
    for d in range(2):
        S.dma("sp", la[d], gha[d], "gg%d" % d); S.dma("sp", beta[d], ghb[d], "gg%d" % d)
        S.add("act", lambda e, d=d: e.activation(out=la[d], in_=la[d], func=AF.Exp, bias=dt_s[:, d:d + 1]), reads=[la[d], dt_s], writes=[la[d]])
        S.add("act", lambda e, d=d: e.activation(out=la[d], in_=la[d], func=AF.Ln, bias=1.0), reads=[la[d]], writes=[la[d]])
        S.add("dve", lambda e, d=d: e.tensor_scalar(out=la[d], in0=la[d], scalar1=al_s[:, d:d + 1], scalar2=-1.0, op0=ALU.mult, op1=ALU.mult),
              reads=[la[d], al_s], writes=[la[d]])
        S.add("act", lambda e, d=d: e.activation(out=beta[d], in_=beta[d], func=AF.Sigmoid), reads=[beta[d]], writes=[beta[d]])
        S.add("pool", lambda e, d=d: e.tensor_copy(out=la_hl[d][0], in_=la[d]), reads=[la[d]], writes=[la_hl[d][0]])
        S.add("dve", lambda e, d=d: e.tensor_tensor(out=la_hl[d][1], in0=la[d], in1=la_hl[d][0], op=ALU.subtract),
              reads=[la[d], la_hl[d][0]], writes=[la_hl[d][1]])
    cctx = [{"i": 0, "xb": xb, "acc": acc, "sqb": sqb, "vTb": vTb, "rrow": rrow, "bg": banks[0], "bt": [banks[1], banks[2]]},
            {"i": 1, "xb": ar.alloc([128, BLK + 4], F32), "acc": ar.alloc([128, BLK], F32), "sqb": ar.alloc([128, BLK], BF16),
             "vTb": ar.alloc([128, BLK], BF16), "rrow": ar.alloc([128, BLK], F32), "bg": banks[4], "bt": [banks[5], banks[6]]}]

    def conv_block(cx, i, b):
        xb_, acc_, sqb_, vTb_, rrow_, bg_ = cx["xb"], cx["acc"], cx["sqb"], cx["vTb"], cx["rrow"], cx["bg"]
        lo = b * BLK - 2
        hi = (b + 1) * BLK + 2
        slo, shi = max(lo, 0), min(hi, T)
        if lo < 0:
            S.add("pool", lambda e: e.memset(xb_[:, 0:2], 0.0), writes=[xb_[:, 0:2]])
        if hi > T:
            S.add("pool", lambda e: e.memset(xb_[:, BLK + 2:BLK + 4], 0.0), writes=[xb_[:, BLK + 2:BLK + 4]])
        S.dma("sp", xb_[:, slo - lo:shi - lo], gxT[i][:, slo:shi], "gx%d" % cx["i"])
        yield
        S.add("dve", lambda e: e.tensor_scalar(out=acc_, in0=xb_[:, 0:BLK], scalar1=cw_s[:, i * 5:i * 5 + 1], scalar2=None, op0=ALU.mult),
              reads=[xb_[:, 0:BLK], cw_s], writes=[acc_])
        yield
        for j in range(1, 5):
            S.add("dve", lambda e, j=j: e.scalar_tensor_tensor(out=acc_, in0=xb_[:, j:j + BLK], scalar=cw_s[:, i * 5 + j:i * 5 + j + 1], in1=acc_,
                                                               op0=ALU.mult, op1=ALU.add),
                  reads=[xb_[:, j:j + BLK], cw_s, acc_], writes=[acc_])
            yield
        sl = slice(b * BLK, (b + 1) * BLK)
        nt_ = BLK // 128
        if i == 2:
            S.add("act", lambda e: e.activation(out=vTb_, in_=acc_, func=AF.Silu), reads=[acc_], writes=[vTb_])
            yield
            for j in range(nt_):
                transpose_to_tok(S, vTb_, v_tok[:, b * nt_:(b + 1) * nt_, :], C["I"], cx["bt"][j % 2], j, eng="act" if j % 2 == 0 else "dve")
                yield
        else:
            dst = qT if i == 0 else kT
            S.add("act", lambda e: e.activation(out=acc_, in_=acc_, func=AF.Silu), reads=[acc_], writes=[acc_])
            yield
            S.add("act", lambda e: e.activation(out=sqb_, in_=acc_, func=AF.Square), reads=[acc_], writes=[sqb_])
            yield
            S.add("pe", lambda e: e.matmul(bg_, lhsT=ones, rhs=sqb_, start=True, stop=True), reads=[ones, sqb_], writes=[bg_])
            yield
            S.add("act", lambda e: e.activation(out=rrow_, in_=bg_, func=AF.Sqrt, bias=epsb), reads=[bg_, epsb], writes=[rrow_])
            yield
            S.add("dve", lambda e: e.reciprocal(out=rrow_, in_=rrow_), reads=[rrow_], writes=[rrow_])
            yield
            sc = 128 ** -0.5 if i == 0 else 1.0
            S.add("dve", lambda e: e.scalar_tensor_tensor(out=dst[:, sl], in0=acc_, scalar=sc, in1=rrow_, op0=ALU.mult, op1=ALU.mult),
                  reads=[acc_, rrow_], writes=[dst[:, sl]])
            yield
            if i == 1:
                for j in range(nt_):
                    transpose_to_tok(S, kT, k_tok, C["I"], cx["bt"][j % 2], b * nt_ + j, eng="act" if j % 2 == 0 else "dve")
                    yield

    items = [(i, b) for i in range(3) for b in range(NB)]
    for k in range(0, len(items), 2):
        merge_gens([conv_block(cctx[n], *items[k + n]) for n in range(2) if k + n < len(items)])

    def prep(i, c, d, rev):
        t = tbs[d][i % 2]
        b_g = b_kk = b_qk = banks[4 * d + 0]
        b_t = b_x = banks[4 * d + 1]
        b_p = banks[4 * d + 2]
        XC = slice(0, 128); TC = slice(128, 256); WC = slice(384, 512)
        Ucs = C["L"] if rev else C["U"]
        NegStrict = C["NSU"] if rev else C["NSL"]
        sl = slice(c * 128, (c + 1) * 128)
        ecol = 0 if rev else 127
        for hl in range(2):
            S.add("act", lambda e, hl=hl: e.activation(out=t["lab%d" % hl], in_=la_hl[d][hl][:, c:c + 1].broadcast_to([128, 128]), func=AF.Copy),
                  reads=[la_hl[d][hl][:, c:c + 1]], writes=[t["lab%d" % hl]])
            yield
        gcp = b_g[:, 128:129]; grow = b_g[:, 0:128]
        for hl in range(2):
            S.add("pe", lambda e, hl=hl: e.matmul(gcp, lhsT=Ucs, rhs=la_hl[d][hl][:, c:c + 1], start=(hl == 0), stop=(hl == 1)),
                  reads=[Ucs, la_hl[d][hl][:, c:c + 1]], writes=[gcp])
            yield
        S.add("act", lambda e: e.activation(out=t["gc"], in_=gcp, func=AF.Copy), reads=[gcp], writes=[t["gc"]])
        yield
        for hl in range(2):
            S.add("pe", lambda e, hl=hl: e.matmul(grow, lhsT=t["lab%d" % hl], rhs=Ucs, start=(hl == 0), stop=(hl == 1)),
                  reads=[t["lab%d" % hl], Ucs], writes=[grow])
            yield
        S.add("dve", lambda e: e.tensor_scalar(out=t["tA"], in0=grow, scalar1=t["gc"], scalar2=0.0, op0=ALU.subtract, op1=ALU.max),
              reads=[grow, t["gc"]], writes=[t["tA"]])
        yield
        S.add("act", lambda e: e.activation(out=t["tA"], in_=t["tA"], func=AF.Exp, scale=-1.0), reads=[t["tA"]], writes=[t["tA"]])
        yield
        S.add("dve", lambda e: e.tensor_scalar(out=t["tB"], in0=grow, scalar1=t["gc"], scalar2=0.0, op0=ALU.subtract, op1=ALU.min),
              reads=[grow, t["gc"]], writes=[t["tB"]])
        yield
        S.add("act", lambda e: e.activation(out=t["tB"], in_=t["tB"], func=AF.Exp), reads=[t["tB"]], writes=[t["tB"]])
        yield
        S.add("act", lambda e: e.activation(out=t["Eg"], in_=grow, func=AF.Exp), reads=[grow], writes=[t["Eg"]])
        yield
        S.add("act", lambda e: e.activation(out=t["gl"], in_=grow[:, ecol:ecol + 1], func=AF.Copy), reads=[grow], writes=[t["gl"]])
        yield
        S.add("pe", lambda e: e.matmul(b_kk[:, 256:384], lhsT=kT[:, sl], rhs=kT[:, sl], start=True, stop=True), reads=[kT[:, sl]], writes=[b_kk])
        yield
        S.add("pe", lambda e: e.matmul(b_qk[:, 384:512], lhsT=kT[:, sl], rhs=qT[:, sl], start=True, stop=True), reads=[kT[:, sl], qT[:, sl]], writes=[b_qk])
        yield
        S.add("dve", lambda e: e.tensor_tensor(out=t["tA"], in0=b_kk[:, 256:384], in1=t["tA"], op=ALU.mult), reads=[b_kk, t["tA"]], writes=[t["tA"]])
        yield
        S.add("dve", lambda e: e.scalar_tensor_tensor(out=t["P0"], in0=t["tA"], scalar=beta[d][:, c:c + 1], in1=NegStrict, op0=ALU.mult, op1=ALU.mult),
              reads=[t["tA"], beta[d][:, c:c + 1], NegStrict], writes=[t["P0"]])
        yield
        S.add("dve", lambda e: e.tensor_tensor(out=t["tB"], in0=b_qk[:, 384:512], in1=t["tB"], op=ALU.mult), reads=[b_qk, t["tB"]], writes=[t["tB"]])
        yield
        S.add("pool", lambda e: e.tensor_tensor(out=t["QKm"], in0=t["tB"], in1=Ucs, op=ALU.mult), reads=[t["tB"], Ucs], writes=[t["QKm"]])
        yield
        ptr = b_t[:, 256:320].bitcast(BF16)
        S.add("pe", lambda e: e.transpose(ptr, t["P0"], C["I"]), reads=[t["P0"], C["I"]], writes=[ptr])
        yield
        S.add("act", lambda e: e.activation(out=t["PT0"], in_=ptr, func=AF.Copy), reads=[ptr], writes=[t["PT0"]])
        yield
        for nm, src, msk in [("No64", "P0", "OFF64"), ("No64T", "PT0", "OFF64"), ("No128", "P0", "OFF128"), ("No128T", "PT0", "OFF128")]:
            S.add("pool", lambda e, nm=nm, src=src, msk=msk: e.tensor_tensor(out=t[nm], in0=t[src], in1=C[msk], op=ALU.mult),
                  reads=[t[src], C[msk]], writes=[t[nm]])
            yield
        S.add("pool", lambda e: e.tensor_tensor(out=t["P0"], in0=t["P0"], in1=C["BD32"], op=ALU.mult), reads=[t["P0"], C["BD32"]], writes=[t["P0"]])
        yield
        S.add("pool", lambda e: e.tensor_tensor(out=t["PT0"], in0=t["PT0"], in1=C["BD32"], op=ALU.mult), reads=[t["PT0"], C["BD32"]], writes=[t["PT0"]])
        yield
        S.add("pool", lambda e: e.tensor_copy(out=t["Xb"], in_=C["I"]), reads=[C["I"]], writes=[t["Xb"]])
        yield
        S.add("pool", lambda e: e.tensor_copy(out=t["Tb"], in_=C["I"]), reads=[C["I"]], writes=[t["Tb"]])
        yield
        for lv in range(5):
            P, PT = t["P%d" % (lv % 2)], t["PT%d" % (lv % 2)]
            Pn, PTn = t["P%d" % ((lv + 1) % 2)], t["PT%d" % ((lv + 1) % 2)]
            S.add("pe", lambda e, P=P: e.matmul(b_x[:, 0:128], lhsT=P, rhs=t["Xb"], start=True, stop=True), reads=[P, t["Xb"]], writes=[b_x])
            yield
            S.add("pe", lambda e, P=P: e.matmul(b_x[:, 128:256], lhsT=t["Xb"], rhs=P, start=True, stop=True), reads=[P, t["Xb"]], writes=[b_x])
            yield
            if lv < 4:
                S.add("pe", lambda e, P=P, PT=PT: e.matmul(b_p[:, 0:128], lhsT=PT, rhs=P, start=True, stop=True), reads=[P, PT], writes=[b_p])
                yield
                S.add("pe", lambda e, P=P, PT=PT: e.matmul(b_p[:, 128:256], lhsT=P, rhs=PT, start=True, stop=True), reads=[P, PT], writes=[b_p])
                yield
            S.add("dve", lambda e: e.tensor_tensor(out=t["Xb"], in0=b_x[:, 0:128], in1=t["Xb"], op=ALU.add), reads=[b_x, t["Xb"]], writes=[t["Xb"]])
            yield
            S.add("dve", lambda e: e.tensor_tensor(out=t["Tb"], in0=b_x[:, 128:256], in1=t["Tb"], op=ALU.add), reads=[b_x, t["Tb"]], writes=[t["Tb"]])
            yield
            if lv < 4:
                S.add("act", lambda e, Pn=Pn: e.activation(out=Pn, in_=b_p[:, 0:128], func=AF.Copy), reads=[b_p], writes=[Pn])
                yield
                S.add("act", lambda e, PTn=PTn: e.activation(out=PTn, in_=b_p[:, 128:256], func=AF.Copy), reads=[b_p], writes=[PTn])
                yield
        for (No, NoT) in [("No64", "No64T"), ("No128", "No128T")]:
            S.add("pe", lambda e, NoT=NoT: e.matmul(b_p[:, 0:128], lhsT=t[NoT], rhs=t["Tb"], start=True, stop=True), reads=[t[NoT], t["Tb"]], writes=[b_p])
            yield
            S.add("pe", lambda e, No=No: e.matmul(b_p[:, 128:256], lhsT=t[No], rhs=t["Xb"], start=True, stop=True), reads=[t[No], t["Xb"]], writes=[b_p])
            yield
            S.add("act", lambda e: e.activation(out=t["Yp"], in_=b_p[:, 0:128], func=AF.Copy), reads=[b_p], writes=[t["Yp"]])
            yield
            S.add("act", lambda e: e.activation(out=t["Y"], in_=b_p[:, 128:256], func=AF.Copy), reads=[b_p], writes=[t["Y"]])
            yield
            S.add("pe", lambda e: e.matmul(b_x[:, 0:128], lhsT=t["Tb"], rhs=t["Y"], start=True, stop=True), reads=[t["Tb"], t["Y"]], writes=[b_x])
            yield
            S.add("pe", lambda e: e.matmul(b_x[:, 128:256], lhsT=t["Xb"], rhs=t["Yp"], start=True, stop=True), reads=[t["Xb"], t["Yp"]], writes=[b_x])
            yield
            S.add("dve", lambda e: e.tensor_tensor(out=t["Xb"], in0=b_x[:, 0:128], in1=t["Xb"], op=ALU.add), reads=[b_x, t["Xb"]], writes=[t["Xb"]])
            yield
            S.add("dve", lambda e: e.tensor_tensor(out=t["Tb"], in0=b_x[:, 128:256], in1=t["Tb"], op=ALU.add), reads=[b_x, t["Tb"]], writes=[t["Tb"]])
            yield
        S.add("dve", lambda e: e.tensor_scalar(out=t["Rv"], in0=v_tok[:, c, :], scalar1=beta[d][:, c:c + 1], scalar2=None, op0=ALU.mult),
              reads=[v_tok[:, c, :], beta[d][:, c:c + 1]], writes=[t["Rv"]])
        yield
        S.add("act", lambda e: e.activation(out=t["bw"], in_=t["gc"], func=AF.Exp), reads=[t["gc"]], writes=[t["bw"]])
        yield
        S.add("dve", lambda e: e.tensor_scalar(out=t["Rw"], in0=k_tok[:, c, :], scalar1=t["bw"], scalar2=beta[d][:, c:c + 1], op0=ALU.mult, op1=ALU.mult),
              reads=[k_tok[:, c, :], t["bw"], beta[d][:, c:c + 1]], writes=[t["Rw"]])
        yield
        S.add("pe", lambda e: e.matmul(b_t[:, 384:512], lhsT=t["Rw"], rhs=t["Xb"], start=True, stop=True), reads=[t["Rw"], t["Xb"]], writes=[b_t])
        yield
        S.add("act", lambda e: e.activation(out=t["nwT"], in_=b_t[:, 384:512], func=AF.Copy, scale=-1.0), reads=[b_t], writes=[t["nwT"]])
        yield
        S.add("pool", lambda e: e.tensor_tensor(out=t["qd"], in0=qT[:, sl], in1=t["Eg"], op=ALU.mult), reads=[qT[:, sl], t["Eg"]], writes=[t["qd"]])
        yield
        S.add("act", lambda e: e.activation(out=t["kds"], in_=t["gc"], func=AF.Exp, scale=-1.0, bias=t["gl"]), reads=[t["gc"], t["gl"]], writes=[t["kds"]])
        yield
        S.add("dve", lambda e: e.tensor_scalar(out=t["kd"], in0=k_tok[:, c, :], scalar1=t["kds"], scalar2=None, op0=ALU.mult),
              reads=[k_tok[:, c, :], t["kds"]], writes=[t["kd"]])
        yield

    def step(i, c, first, rev, d):
        t = tbs[d][i % 2]
        b_v = b_o = banks[4 * d + 3]
        St, Sb, Sl = Sts[d], Sbs[d], Sls[d]
        ecol = 0 if rev else 127
        S.add("pe", lambda e: e.matmul(b_v[:, 0:128], lhsT=t["Xb"], rhs=t["Rv"], start=True, stop=False), reads=[t["Xb"], t["Rv"]], writes=[b_v])
        yield
        S.add("pe", lambda e: e.matmul(b_v[:, 0:128], lhsT=t["nwT"], rhs=Sb, start=False, stop=True), reads=[t["nwT"], Sb], writes=[b_v])
        yield
        S.add("act", lambda e: e.activation(out=t["vn"], in_=b_v[:, 0:128], func=AF.Copy), reads=[b_v], writes=[t["vn"]])
        yield
        S.add("pe", lambda e: e.matmul(b_o[:, 256:384], lhsT=t["qd"], rhs=Sb, start=True, stop=False), reads=[t["qd"], Sb], writes=[b_o])
        yield
        S.add("pe", lambda e: e.matmul(b_o[:, 256:384], lhsT=t["QKm"], rhs=t["vn"], start=False, stop=True), reads=[t["QKm"], t["vn"]], writes=[b_o])
        yield
        if first:
            S.add("act", lambda e: e.activation(out=o_acc[:, c, :], in_=b_o[:, 256:384], func=AF.Copy), reads=[b_o], writes=[o_acc[:, c, :]])
            yield
        else:
            S.add("dve", lambda e: e.tensor_tensor(out=o_acc[:, c, :], in0=b_o[:, 256:384], in1=o_acc[:, c, :], op=ALU.add),
                  reads=[b_o, o_acc[:, c, :]], writes=[o_acc[:, c, :]])
            yield
        S.add("pe", lambda e: e.matmul(b_v[:, 128:256], lhsT=t["kd"], rhs=t["vn"], start=True, stop=True), reads=[t["kd"], t["vn"]], writes=[b_v])
        yield
        dcol = t["Eg"][:, ecol:ecol + 1]
        S.add("dve", lambda e: e.scalar_tensor_tensor(out=St, in0=St, scalar=dcol, in1=b_v[:, 128:256], op0=ALU.mult, op1=ALU.add),
              reads=[St, dcol, b_v], writes=[St])
        yield
        S.add("act", lambda e: e.activation(out=Sb, in_=St, func=AF.Copy), reads=[St], writes=[Sb])
        yield

    def merge(gens):
        active = list(gens)
        while active:
            for g_ in list(active):
                try:
                    next(g_)
                except StopIteration:
                    active.remove(g_)

    for d in range(2):
        S.add("pool", lambda e, d=d: e.memset(Sts[d], 0.0), writes=[Sts[d]])
        S.add("pool", lambda e, d=d: e.memset(Sbs[d], 0.0), writes=[Sbs[d]])
        S.add("pool", lambda e, d=d: e.memset(Sls[d], 0.0), writes=[Sls[d]])
    orders = [list(range(NT)), list(range(NT - 1, -1, -1))]
    merge([prep(0, orders[d][0], d, d == 1) for d in range(2)])
    for i in range(NT):
        gens = []
        for d in range(2):
            if i + 1 < NT:
                gens.append(prep(i + 1, orders[d][i + 1], d, d == 1))
            gens.append(step(i, orders[d][i], i < NT // 2, d == 1, d))
        merge(gens)
    for p in range(NP):
        S.dma("sp", od_out[p * PIECE:(p + 1) * PIECE, :].rearrange("(n p) d -> p n d", p=128), o_acc[:, p * PT:(p + 1) * PT, :], "od")
    return ["od"]


def load_consts1(S, nc, st):
    C = load_consts(S, nc, st, ["U", "L", "SU", "SL", "I", "NSU", "NSL", "BD32", "OFF64", "OFF128"])
    d = nc.dram_tensor("c_I32", [128, 128], F32, kind="ExternalInput").ap()
    t = st.enter_context(nc.sbuf_tensor("cs_I32", [128, 128], F32))
    S.dma("sp", t[:], d, "const32")
    C["I32"] = t[:]
    return C


def consts1_np():
    s = np.arange(128)[:, None]; t = np.arange(128)[None, :]
    c = {"U": s <= t, "L": s >= t, "SU": s < t, "SL": s > t, "I": s == t, "I32": s == t}
    c = {"c_" + k: v.astype(np.float32) for k, v in c.items()}
    c["c_BD32"] = ((s // 32) == (t // 32)).astype(np.float32)
    c["c_OFF64"] = (((s // 64) == (t // 64)) & ((s // 32) != (t // 32))).astype(np.float32)
    c["c_OFF128"] = ((s // 64) != (t // 64)).astype(np.float32)
    c["c_NSU"] = -(s < t).astype(np.float32)
    c["c_NSL"] = -(s > t).astype(np.float32)
    return c


def build_D(do_ret=True, do_gdn=True):
    nc = bass.Bass("TRN2", target_bir_lowering=False)
    S = Sched(nc)
    with ExitStack() as st:
        arena_t = st.enter_context(nc.sbuf_tensor("arena", [128, 44 * 1024], F32))
        C = load_consts1(S, nc, st)
        banks = [st.enter_context(nc.psum_tensor("pb%d" % i, [128, 512], F32))[:] for i in range(8)]
        keys = []
        if do_ret:
            oc = nc.dram_tensor("oc", [T, 128], F32, kind="ExternalOutput").ap()
            keys += ret_part(S, nc, st, Arena(arena_t, 44 * 1024), C, banks, oc)
        if do_gdn:
            od = nc.dram_tensor("od", [T, 128], F32, kind="ExternalOutput").ap()
            keys += gdn_part(S, nc, st, Arena(arena_t, 44 * 1024), C, banks, od)
        S.finish_wait("sp", keys)
        S.build()
    return nc


def mix1_inputs(inp, proj, i):
    h, half = i // 2, i % 2
    A = lambda a: np.ascontiguousarray(a, dtype=np.float32)
    rq = proj[:, h * 128:(h + 1) * 128]; rk = proj[:, 512 + h * 128:512 + (h + 1) * 128]
    rv = proj[:, 1024 + h * 256 + half * 128:1024 + h * 256 + (half + 1) * 128]
    sw = lambda a: np.concatenate([a[:, 64:], a[:, :64]], axis=1)
    pos = np.asarray(inp["positions"])[0][:T]
    inv = (10000.0 ** (-np.arange(64, dtype=np.float32) / 64)).astype(np.float32)
    lgam = np.log(1.0 - np.exp2(-5.0 - np.arange(4, dtype=np.float32))).astype(np.float32)
    m = {"rqT": A(rq.T), "rqTs": A(sw(rq).T), "rkT": A(rk.T), "rkTs": A(sw(rk).T), "rvt": A(rv),
         "rposb": np.ascontiguousarray(np.broadcast_to(pos[None, :], (128, T))).astype(np.int32),
         "rinv2": (np.concatenate([inv, inv])[:, None] / (2 * np.pi)).astype(np.float32),
         "rsgn": np.concatenate([np.ones(64), -np.ones(64)])[:, None].astype(np.float32),
         "rlg": A(np.broadcast_to(np.array([lgam[h], lgam[3 - h]], dtype=np.float32)[None, :], (128, 2)))}
    g = i
    qkv = proj[:, 3072:6144]
    cw = np.asarray(inp["gdn_conv_w"])[0]
    for j in range(3):
        m["gxT%d" % j] = A(qkv[:, j * 1024 + g * 128:j * 1024 + (g + 1) * 128].T)
    m["gcw"] = A(np.concatenate([cw[:, j * 1024 + g * 128:j * 1024 + (g + 1) * 128].T for j in range(3)], axis=1))
    for d in range(2):
        m["gha%d" % d] = A(proj[:, 6144 + d * 8 + g].reshape(NT, 128).T)
        m["ghb%d" % d] = A(proj[:, 6160 + d * 8 + g].reshape(NT, 128).T)
    m["gal"] = A(np.broadcast_to(np.asarray(inp["gdn_a_log"])[0][:, g][None, :], (128, 2)))
    m["gdt"] = A(np.broadcast_to(np.asarray(inp["gdn_dt_bias"])[0][:, g][None, :], (128, 2)))
    m.update(consts1_np())
    return m


def build_C(layer, NC2=7200):
    G1 = 1 if layer == 0 else 2
    norm2 = (layer == 1)
    NH = TOK // HALF
    nc = bass.Bass("TRN2", target_bir_lowering=False)
    din = lambda n, shp: nc.dram_tensor(n, shp, F32, kind="ExternalInput").ap()
    xT = din("xT", [D, TOK]); m1T = din("m1T", [1024, TOK]); g1T = din("g1T", [1024, TOK]); nw1 = din("nw1", [128, 8])
    m2T = din("m2T", [1024, TOK])
    if norm2:
        g2T = din("g2T", [1024, TOK]); nw2 = din("nw2", [128, 8])
    w_out = din("w_out", [D, D]); nfw = din("nfw", [128, 16]); w_up = din("w_up", [D, DFF]); w_down = din("w_down", [DFF, D])
    if layer == 0:
        nmw = din("nmw", [128, 16]); w_in2 = din("w_in2", [D, NC2])
        x2T = nc.dram_tensor("x2T", [D, TOK], F32, kind="ExternalOutput").ap()
        yT = nc.dram_tensor("yT", [NC2, TOK], F32, kind="ExternalOutput").ap()
    else:
        fnw = din("fnw", [128, 16])
        outT = nc.dram_tensor("outT", [D, TOK], F32, kind="ExternalOutput").ap()
    NQ = DFF // 1024
    S = Sched(nc)
    with ExitStack() as st:
        WORDS = 46 * 1024
        arena_t = st.enter_context(nc.sbuf_tensor("arena", [128, WORDS], F32))
        ar = Arena(arena_t, WORDS)
        banks = [st.enter_context(nc.psum_tensor("pb%d" % i, [128, 512], F32))[:] for i in range(8)]
        b_ss = banks[0]
        pmm = banks[1:8]
        x = ar.alloc([128, 16, TOK], F32)
        actb = ar.alloc([128, 16, TOK], BF16)
        aT = ar.alloc([128, 8, TOK], BF16)
        wts = [ar.alloc([128, 16, 512], BF16) for _ in range(2)]
        ots_off = ar.off
        ots = [ar.alloc([128, 4, HALF], F32) for _ in range(2)]
        wts.append(arena_t[:, ots_off:ots_off + 4096].bitcast(BF16).rearrange("p (a b) -> p a b", b=512))
        nslots = [3]
        lds = [ar.alloc([128, 2, HALF], F32) for _ in range(2)]
        tmp = [ar.alloc([128, HALF], F32) for _ in range(3)]
        rrow = ar.alloc([128, TOK], F32)
        sqb = ar.alloc([128, 2, HALF], BF16)
        ones = ar.alloc([128, 128], BF16)
        epsb = ar.alloc([128, 1], F32)
        nw1_s = ar.alloc([128, 8], F32); nw2_s = ar.alloc([128, 8], F32)
        nfw_s = ar.alloc([128, 16], F32); nxw_s = ar.alloc([128, 16], F32)
        S.add("pool", lambda e: e.memset(ones, 1.0), writes=[ones])
        S.add("pool", lambda e: e.memset(epsb, EPS), writes=[epsb])
        S.dma("sp", nw1_s, nw1, "cw")
        if norm2:
            S.dma("sp", nw2_s, nw2, "cw")
        S.dma("sp", nfw_s, nfw, "cw")
        S.dma("sp", nxw_s, nmw if layer == 0 else fnw, "cw")
        cnt = {"w": 0, "p": 0, "o": 0, "l": 0, "t": 0}
        halves = [slice(hf * HALF, (hf + 1) * HALF) for hf in range(NH)]

        def wload(src_ap, shape3):
            i = cnt["w"] % nslots[0]
            cnt["w"] += 1
            a_, b_ = shape3
            flat = wts[i].rearrange("p a b -> p (a b)")[:, 0:a_ * b_].rearrange("p (a b) -> p a b", b=b_)
            S.dma("pool", flat, src_ap, "w%d" % i)
            return flat

        def pbank():
            b_ = pmm[cnt["p"] % len(pmm)]
            cnt["p"] += 1
            return b_

        def rms_rows(src_chunks, nfeat, tsl):
            n = len(src_chunks)
            for i, c in enumerate(src_chunks):
                q = sqb[:, i % 2, :]
                S.add("act", lambda e, c=c, q=q: e.activation(out=q, in_=c, func=AF.Square), reads=[c], writes=[q])
                S.add("pe", lambda e, q=q, i=i: e.matmul(b_ss, lhsT=ones, rhs=q, start=(i == 0), stop=(i == n - 1)),
                      reads=[ones, q], writes=[b_ss])
            S.add("act", lambda e: e.activation(out=rrow[:, tsl], in_=b_ss, func=AF.Sqrt, scale=1.0 / nfeat, bias=epsb),
                  reads=[b_ss, epsb], writes=[rrow[:, tsl]])
            S.add("dve", lambda e: e.reciprocal(out=rrow[:, tsl], in_=rrow[:, tsl]), reads=[rrow[:, tsl]], writes=[rrow[:, tsl]])

        def norm_gate(mT, gT, nw_s, G, c0, tsl):
            for grp in range(8 // G):
                ld = lds[cnt["l"] % 2]
                cnt["l"] += 1
                key = "l%d" % (cnt["l"] % 2)
                ldg = lds[cnt["l"] % 2]
                cnt["l"] += 1
                keyg = "l%d" % (cnt["l"] % 2)
                r0 = grp * G * 128
                S.dma("sp", ld[:, 0:G, :], mT[r0:r0 + G * 128, tsl].rearrange("(c p) t -> p c t", p=128), key)
                S.dma("sp", ldg[:, 0:G, :], gT[r0:r0 + G * 128, tsl].rearrange("(c p) t -> p c t", p=128), keyg)
                rms_rows([ld[:, i, :] for i in range(G)], 128.0 * G, tsl)
                for i in range(G):
                    c = grp * G + i
                    S.add("act", lambda e, ldg=ldg, i=i: e.activation(out=ldg[:, i, :], in_=ldg[:, i, :], func=AF.Silu),
                          reads=[ldg[:, i, :]], writes=[ldg[:, i, :]])
                    S.add("dve", lambda e, ld=ld, i=i: e.tensor_tensor(out=ld[:, i, :], in0=ld[:, i, :], in1=rrow[:, tsl], op=ALU.mult),
                          reads=[ld[:, i, :], rrow[:, tsl]], writes=[ld[:, i, :]])
                    S.add("dve", lambda e, ld=ld, ldg=ldg, i=i, c=c: e.scalar_tensor_tensor(
                        out=actb[:, c0 + c, tsl], in0=ld[:, i, :], scalar=nw_s[:, c:c + 1], in1=ldg[:, i, :], op0=ALU.mult, op1=ALU.mult),
                        reads=[ld[:, i, :], nw_s[:, c:c + 1], ldg[:, i, :]], writes=[actb[:, c0 + c, tsl]])

        def rms_to_act(nw_s):
            for tsl in halves:
                rms_rows([x[:, c, tsl] for c in range(16)], float(D), tsl)
                for c in range(16):
                    S.add("act", lambda e, c=c, tsl=tsl: e.activation(out=actb[:, c, tsl], in_=x[:, c, tsl], func=AF.Copy, scale=nw_s[:, c:c + 1]),
                          reads=[x[:, c, tsl], nw_s[:, c:c + 1]], writes=[actb[:, c, tsl]])

        def mm16(pt, wt, j, m, tsl):
            for c in range(16):
                S.add("pe", lambda e, c=c: e.matmul(pt[0:m, :], lhsT=wt[:, c, j * 128:j * 128 + m], rhs=actb[:, c, tsl], start=(c == 0), stop=(c == 15)),
                      reads=[wt[:, c, j * 128:j * 128 + m], actb[:, c, tsl]], writes=[pt])

        S.dma("sp", x, xT.rearrange("(c p) t -> p c t", p=128), "x")
        for tsl in halves:
            norm_gate(m1T, g1T, nw1_s, G1, 0, tsl)
            if norm2:
                norm_gate(m2T, g2T, nw2_s, 1, 8, tsl)
            else:
                for c2 in range(0, 8, 2):
                    ld = lds[cnt["l"] % 2]
                    cnt["l"] += 1
                    S.dma("sp", ld, m2T[c2 * 128:(c2 + 2) * 128, tsl].rearrange("(c p) t -> p c t", p=128), "l%d" % (cnt["l"] % 2))
                    S.add("pool", lambda e, ld=ld, c2=c2, tsl=tsl: e.tensor_copy(out=actb[:, 8 + c2:10 + c2, tsl], in_=ld),
                          reads=[ld], writes=[actb[:, 8 + c2:10 + c2, tsl]])
        for g in range(4):
            wt = wload(w_out[:, g * 512:(g + 1) * 512].rearrange("(c p) n -> p c n", p=128), (16, 512))
            for tsl in halves:
                for j in range(4):
                    pt = pbank()
                    jj = g * 4 + j
                    mm16(pt, wt, j, 128, tsl)
                    S.add("dve", lambda e, pt=pt, jj=jj, tsl=tsl: e.tensor_tensor(out=x[:, jj, tsl], in0=pt, in1=x[:, jj, tsl], op=ALU.add),
                          reads=[pt, x[:, jj, tsl]], writes=[x[:, jj, tsl]])
        rms_to_act(nfw_s)
        for q in range(NQ):
            for g2 in range(2):
                c0 = q * 1024 + g2 * 512
                wt = wload(w_up[:, c0:c0 + 512].rearrange("(c p) n -> p c n", p=128), (16, 512))
                for tsl in halves:
                    for j in range(4):
                        pt = pbank()
                        f = g2 * 4 + j
                        mm16(pt, wt, j, 128, tsl)
                        t_ = tmp[cnt["t"] % 3]
                        cnt["t"] += 1
                        S.add("act", lambda e, pt=pt, t_=t_: e.activation(out=t_, in_=pt, func=AF.Relu), reads=[pt], writes=[t_])
                        S.add("dve", lambda e, t_=t_, tsl=tsl: e.tensor_tensor(out=t_, in0=t_, in1=rrow[:, tsl], op=ALU.mult), reads=[t_, rrow[:, tsl]], writes=[t_])
                        S.add("pool", lambda e, t_=t_, f=f, tsl=tsl: e.tensor_tensor(out=aT[:, f, tsl], in0=t_, in1=t_, op=ALU.mult), reads=[t_], writes=[aT[:, f, tsl]])
            for ch in range(2):
                wt = wload(w_down[q * 1024:(q + 1) * 1024, ch * 1024:(ch + 1) * 1024].rearrange("(c p) n -> p c n", p=128), (8, 1024))
                for jl in range(8):
                    jj = ch * 8 + jl
                    for tsl in halves:
                        pt = pbank()
                        for f in range(8):
                            S.add("pe", lambda e, pt=pt, wt=wt, f=f, jl=jl, tsl=tsl: e.matmul(pt, lhsT=wt[:, f, jl * 128:(jl + 1) * 128], rhs=aT[:, f, tsl],
                                                                                         start=(f == 0), stop=(f == 7)),
                                  reads=[wt[:, f, jl * 128:(jl + 1) * 128], aT[:, f, tsl]], writes=[pt])
                        S.add("dve", lambda e, pt=pt, jj=jj, tsl=tsl: e.tensor_tensor(out=x[:, jj, tsl], in0=pt, in1=x[:, jj, tsl], op=ALU.add),
                              reads=[pt, x[:, jj, tsl]], writes=[x[:, jj, tsl]])
        nslots[0] = 2
        cnt["w"] = 0
        rms_to_act(nxw_s)
        if layer == 0:
            S.dma("sp", x2T.rearrange("(c p) t -> p c t", p=128), x, "xo")
            ng = (NC2 + 511) // 512
            for g in range(ng):
                c0 = g * 512
                gw = min(512, NC2 - c0)
                wt = wload(w_in2[:, c0:c0 + gw].rearrange("(c p) n -> p c n", p=128), (16, gw))
                nj = (gw + 127) // 128
                for tsl in halves:
                    oi = cnt["o"] % 2
                    cnt["o"] += 1
                    ot = ots[oi]
                    for j in range(nj):
                        m = min(128, gw - j * 128)
                        pt = pbank()
                        mm16(pt, wt, j, m, tsl)
                        S.add("dve", lambda e, pt=pt, ot=ot, j=j, m=m, tsl=tsl: e.tensor_tensor(out=ot[0:m, j, :], in0=pt[0:m, :], in1=rrow[0:m, tsl], op=ALU.mult),
                              reads=[pt, rrow[:, tsl]], writes=[ot[0:m, j, :]])
                    nfull = gw // 128
                    if nfull:
                        S.dma("sp", yT[c0:c0 + nfull * 128, tsl].rearrange("(j p) t -> p j t", p=128), ot[:, 0:nfull, :], "o%d" % oi)
                    rem = gw - nfull * 128
                    if rem:
                        S.dma("sp", yT[c0 + nfull * 128:c0 + gw, tsl], ot[0:rem, nfull, :], "o%d" % oi)
        else:
            for tsl in halves:
                for g in range(4):
                    oi = cnt["o"] % 2
                    cnt["o"] += 1
                    ot = ots[oi]
                    for j in range(4):
                        c = g * 4 + j
                        S.add("dve", lambda e, ot=ot, j=j, c=c, tsl=tsl: e.scalar_tensor_tensor(out=ot[:, j, :], in0=x[:, c, tsl], scalar=nxw_s[:, c:c + 1], in1=rrow[:, tsl],
                                                                                               op0=ALU.mult, op1=ALU.mult),
                              reads=[x[:, c, tsl], nxw_s[:, c:c + 1], rrow[:, tsl]], writes=[ot[:, j, :]])
                    S.dma("sp", outT[g * 512:(g + 1) * 512, tsl].rearrange("(j p) t -> p j t", p=128), ot, "o%d" % oi)
        S.finish_wait("sp", ["o0", "o1"] + (["xo"] if layer == 0 else []))
        S.build()
    return nc


def dense_inputs(inp, layer, c, x, m1, g1, m2, g2=None, tok=1024):
    rs = slice(c * tok, (c + 1) * tok)
    T_ = lambda a: np.ascontiguousarray(a[rs].T)
    nwr = lambda w, n: np.ascontiguousarray(np.asarray(w).reshape(n, 128).T)
    m = {"xT": T_(x), "m1T": T_(m1), "g1T": T_(g1), "m2T": T_(m2)}
    if layer == 0:
        m["nw1"] = nwr(inp["hg_norm_w"][0], 8)
        m["w_out"] = np.ascontiguousarray(inp["even_w_out"][0])
        m["nmw"] = nwr(inp["norm_mix_w"][1], 16)
        m["w_in2"] = np.ascontiguousarray(inp["odd_w_in"][0])
    else:
        m["nw1"] = nwr(inp["ret_norm_w"][0], 8)
        m["g2T"] = T_(g2)
        m["nw2"] = nwr(inp["gdn_norm_w"][0], 8)
        m["w_out"] = np.ascontiguousarray(inp["odd_w_out"][0])
        m["fnw"] = nwr(inp["final_norm_w"], 16)
    m["nfw"] = nwr(inp["norm_ffn_w"][layer], 16)
    m["w_up"] = np.ascontiguousarray(inp["ffn_w_up"][layer])
    m["w_down"] = np.ascontiguousarray(inp["ffn_w_down"][layer])
    return m


def build_A(NC):
    nc = bass.Bass("TRN2", target_bir_lowering=False)
    xT = nc.dram_tensor("xT", [D, TOK], F32, kind="ExternalInput").ap()
    nw = nc.dram_tensor("nw", [128, 16], F32, kind="ExternalInput").ap()
    W = nc.dram_tensor("W", [D, NC], F32, kind="ExternalInput").ap()
    yT = nc.dram_tensor("yT", [NC, TOK], F32, kind="ExternalOutput").ap()
    S = Sched(nc)
    with ExitStack() as st:
        sb = lambda name, shape, dt: st.enter_context(nc.sbuf_tensor(name, shape, dt))
        ps = lambda name: st.enter_context(nc.psum_tensor(name, [128, 512], F32))
        x_sb = sb("x_sb", [128, 16, HALF], F32)
        sq = sb("sq", [128, 2, HALF], BF16)
        hT = sb("hT", [128, 16, HALF], BF16)
        nw_sb = sb("nw_sb", [128, 16], F32)
        ones = sb("ones", [128, 128], BF16)
        rstd = sb("rstd", [128, HALF], F32)
        wts = [sb("wt%d" % i, [128, 16, 512], BF16) for i in range(2)]
        outs = [sb("ot%d" % i, [128, 4, HALF], F32) for i in range(2)]
        pss = [ps("ps%d" % i) for i in range(4)]
        ps_ss = ps("ps_ss")

        eps_sb = sb("eps_sb", [128, 1], F32)
        S.add("pool", lambda e: e.memset(ones[:], 1.0), writes=[ones[:]])
        S.add("pool", lambda e: e.memset(eps_sb[:], EPS), writes=[eps_sb[:]])
        S.dma("sp", nw_sb[:], nw, "nw")
        ngroups = (NC + 511) // 512
        gi = 0
        pi = 0
        for hf in range(TOK // HALF):
            tsl = slice(hf * HALF, (hf + 1) * HALF)
            S.dma("sp", x_sb[:], xT[:, tsl].rearrange("(c p) t -> p c t", p=128), "x")
            for c in range(16):
                sqc = sq[:, c % 2, :]
                S.add("act", lambda e, o=sqc, i=x_sb[:, c, :]: e.activation(out=o, in_=i, func=AF.Square),
                      reads=[x_sb[:, c, :]], writes=[sqc])
                S.add("pe", lambda e, i=sqc, c=c: e.matmul(ps_ss[:], lhsT=ones[:], rhs=i, start=(c == 0), stop=(c == 15)),
                      reads=[ones[:], sqc], writes=[ps_ss[:]])
            S.add("act", lambda e: e.activation(out=rstd[:], in_=ps_ss[:], func=AF.Sqrt, scale=1.0 / D, bias=eps_sb[:]),
                  reads=[ps_ss[:], eps_sb[:]], writes=[rstd[:]])
            S.add("dve", lambda e: e.reciprocal(out=rstd[:], in_=rstd[:]),
                  reads=[rstd[:]], writes=[rstd[:]])
            for c in range(16):
                S.add("act", lambda e, c=c: e.activation(out=hT[:, c, :], in_=x_sb[:, c, :], func=AF.Copy,
                                                         scale=nw_sb[:, c:c + 1]),
                      reads=[x_sb[:, c, :], nw_sb[:, c:c + 1]], writes=[hT[:, c, :]])
            for g in range(ngroups):
                c0 = g * 512
                gw = min(512, NC - c0)
                wt = wts[gi % 2]
                ot = outs[gi % 2]
                gi += 1
                S.dma("pool", wt[:, :, 0:gw], W[:, c0:c0 + gw].rearrange("(c p) n -> p c n", p=128), "w%d" % (gi % 2))
                nj = (gw + 127) // 128
                for j in range(nj):
                    m = min(128, gw - j * 128)
                    pt = pss[pi % 4]
                    pi += 1
                    for c in range(16):
                        S.add("pe", lambda e, pt=pt, wt=wt, c=c, j=j, m=m: e.matmul(
                            pt[0:m, :], lhsT=wt[:, c, j * 128:j * 128 + m], rhs=hT[:, c, :],
                            start=(c == 0), stop=(c == 15)),
                            reads=[wt[:, c, j * 128:j * 128 + m], hT[:, c, :]], writes=[pt[0:m, :]])
                    S.add("dve", lambda e, pt=pt, ot=ot, j=j, m=m: e.tensor_tensor(
                        out=ot[0:m, j, :], in0=pt[0:m, :], in1=rstd[0:m, :], op=ALU.mult),
                        reads=[pt[0:m, :], rstd[0:m, :]], writes=[ot[0:m, j, :]])
                nfull = gw // 128
                if nfull:
                    S.dma("sp", yT[c0:c0 + nfull * 128, tsl].rearrange("(j p) t -> p j t", p=128),
                          ot[:, 0:nfull, :], "o%d" % (gi % 2))
                rem = gw - nfull * 128
                if rem:
                    S.dma("sp", yT[c0 + nfull * 128:c0 + gw, tsl], ot[0:rem, nfull, :], "o%d" % (gi % 2))
        S.finish_wait("sp", ["o0", "o1"])
        S.build()
    return nc


def mla_inputs(z, proj, h):
    c_q = proj[:, 4096 + 1024:4096 + 1024 + 512]
    kv_a = proj[:, 4096 + 1024 + 512:]
    ckv, kpe = kv_a[:, :256], kv_a[:, 256:]
    wq = np.asarray(z["mla_w_q_b"])[0][:, h * 192:(h + 1) * 192]
    wkv = np.asarray(z["mla_w_kv_b"])[0][:, h * 256:(h + 1) * 256]
    pos = np.asarray(z["positions"])[0][:T]
    inv = (10000.0 ** (-np.arange(32, dtype=np.float32) / 32)).astype(np.float32)
    m = {"cqT": np.ascontiguousarray(c_q.T), "ckvT": np.ascontiguousarray(ckv.T), "kpeT": np.ascontiguousarray(kpe.T),
         "kpeTs": np.ascontiguousarray(np.concatenate([kpe[:, 32:], kpe[:, :32]], axis=1).T),
         "posb": np.ascontiguousarray(np.broadcast_to(pos[None, :], (64, T))).astype(np.int32),
         "qnw": np.ascontiguousarray(np.asarray(z["mla_q_norm_w"])[0].reshape(4, 128).T), "kvnw": np.ascontiguousarray(np.asarray(z["mla_kv_norm_w"])[0].reshape(2, 128).T),
         "wqa": np.ascontiguousarray(wq[:, :128]), "wqr": np.ascontiguousarray(wq[:, 128:]),
         "wqrs": np.ascontiguousarray(np.concatenate([wq[:, 160:], wq[:, 128:160]], axis=1)),
         "wkn": np.ascontiguousarray(wkv[:, :128]), "wkv": np.ascontiguousarray(wkv[:, 128:]),
         "inv2": (np.concatenate([inv, inv])[:, None] / (2 * np.pi)).astype(np.float32),
         "sgn": (2 * np.pi * np.concatenate([np.ones(32), -np.ones(32)]))[:, None].astype(np.float32)}
    return m


def _consts0_np():
    s = np.arange(128)[:, None]
    t = np.arange(128)[None, :]
    return {"c_U": (s <= t).astype(np.float32), "c_L": (s >= t).astype(np.float32),
            "c_SU": (s < t).astype(np.float32), "c_SL": (s > t).astype(np.float32)}


def _run(nc, maps):
    return run_bass_kernel_spmd(nc, maps, core_ids=list(range(8))).results


def kernel(**inp):
    inp = {k: np.asarray(v) for k, v in inp.items()}
    x = inp["x"][0]
    Win = np.ascontiguousarray(inp["even_w_in"][0])
    nw = np.ascontiguousarray(inp["norm_mix_w"][0].reshape(16, 128).T)
    resA = _run(build_A(5952), [{"xT": np.ascontiguousarray(x[c * TOK:(c + 1) * TOK].T), "nw": nw, "W": Win} for c in range(8)])
    proj0 = np.concatenate([r["yT"] for r in resA], axis=1).T
    cn = _consts0_np()
    mapsB = []
    for h in range(8):
        sl = slice(h * 128, (h + 1) * 128)
        hf = [proj0[:, 1024:2048][:, sl], proj0[:, 2048:3072][:, sl]]
        lbl = inp["hg_lb_logits"][:, sl]
        m = {"hqT": np.ascontiguousarray(proj0[:, 0:1024][:, sl].T), "hit": np.ascontiguousarray(proj0[:, 3072:4096][:, sl]),
             "lbc": np.ascontiguousarray(lbl.T), "lbr": np.ascontiguousarray(np.broadcast_to(lbl[None], (128, 3, 128)))}
        for d in range(2):
            m["hfT%d" % d] = np.ascontiguousarray(hf[d].T)
            m["hft%d" % d] = np.ascontiguousarray(hf[d])
        m.update(cn)
        m.update(mla_inputs(inp, proj0, h))
        mapsB.append(m)
    resB = _run(build_B(), mapsB)
    o_a = np.concatenate([r["oa"] for r in resB], axis=1)
    o_b = np.concatenate([r["obT"].T for r in resB], axis=1)
    resC = _run(build_C(0), [dense_inputs(inp, 0, c, x, o_a, proj0[:, 4096:5120], o_b) for c in range(8)])
    x2 = np.concatenate([r["x2T"] for r in resC], axis=1).T
    proj1 = np.concatenate([r["yT"] for r in resC], axis=1).T
    resD = _run(build_D(), [mix1_inputs(inp, proj1, i) for i in range(8)])
    o_c = np.concatenate([r["oc"] for r in resD], axis=1)
    o_d = np.concatenate([r["od"] for r in resD], axis=1)
    resE = _run(build_C(1), [dense_inputs(inp, 1, c, x2, o_c, proj1[:, 2048:3072], o_d, proj1[:, 6176:7200]) for c in range(8)])
    out = np.concatenate([r["outT"] for r in resE], axis=1).T
    return np.ascontiguousarray(out[None]).astype(np.float32)
```
